# Optimizing a Trainium2 kernel written in Bass

```python
import math
import jax, jax.numpy as jnp
from jax import lax
import numpy as np

D_MODEL = 2048
BATCH = 2
SEQ = 4096
DEPTH = 4
DEC_BATCH = 4
DEC_SEQ = 8192
PAST_LEN = 128

N_MIXERS = 2
N_S5_LAYERS = (DEPTH + 1) // 2
N_ATTN_LAYERS = DEPTH // 2

S5_EXPAND = 2
S5_WIDTH = S5_EXPAND * D_MODEL
S5_GROUP = 16
S5_GROUPS = S5_WIDTH // S5_GROUP
S5_STATE = 64
S5_GROUPS_PER_BLOCK = 16
S5_BLOCKS = S5_GROUPS // S5_GROUPS_PER_BLOCK
S5_LAMBDA_RE_MAX = -1e-4

ATTN_WIDTH = D_MODEL
ATTN_HEADS = 8
ATTN_HEAD_DIM = ATTN_WIDTH // ATTN_HEADS // 2
Q_BLOCK = 128
ROPE_THETA = 10000.0
NORM_EPS = 1e-6
SUBLN_EPS = 1e-5

kernel_name = "hybrid_bidir_s5_diffattn_encoder"


def _rmsnorm(x, gain, eps):
    x32 = x.astype(jnp.float32)
    y = x32 * lax.rsqrt(jnp.mean(x32 * x32, axis=-1, keepdims=True) + eps)
    return (y * gain.astype(jnp.float32)).astype(x.dtype)


def _lambda_init(layer_idx):
    return 0.8 - 0.6 * math.exp(-0.3 * layer_idx)


def _rope(x, cos, sin):
    half = x.shape[-1] // 2
    x1, x2 = x[..., :half], x[..., half:]
    c = cos[None, :, None, :].astype(x.dtype)
    s = sin[None, :, None, :].astype(x.dtype)
    return jnp.concatenate([x1 * c - x2 * s, x2 * c + x1 * s], axis=-1)


def _linear_recurrence_combine(e1, e2):
    a1, b1 = e1
    a2, b2 = e2
    return a1 * a2, a2 * b1 + b2


def _s5_scan_block(args):
    u, a_re, a_im, log_dt, b_re, b_im, c_re, c_im = args
    L = u.shape[1]
    uc = u.astype(jnp.complex64)
    outs = []
    for d, rev in ((0, False), (1, True)):
        lam = lax.complex(jnp.minimum(a_re[d].astype(jnp.float32), S5_LAMBDA_RE_MAX),
                          a_im[d].astype(jnp.float32))
        dt = jnp.exp(log_dt[d].astype(jnp.float32))[:, None]
        lam_bar = jnp.exp(lam * dt)
        b = lax.complex(b_re[d].astype(jnp.float32), b_im[d].astype(jnp.float32))
        b_bar = ((lam_bar - 1.0) / lam)[..., None] * b
        bu = jnp.einsum('blgh,gph->blgp', uc, b_bar)
        decay = jnp.broadcast_to(lam_bar, (1, L) + lam_bar.shape)
        _, h = lax.associative_scan(_linear_recurrence_combine, (decay, bu), axis=1, reverse=rev)
        c = lax.complex(c_re[d].astype(jnp.float32), c_im[d].astype(jnp.float32))
        outs.append(jnp.einsum('blgp,ghp->blgh', h, c).real)
    return outs[0] + outs[1]


def _blocked_groups(p):
    p = p.reshape((p.shape[0], S5_BLOCKS, S5_GROUPS_PER_BLOCK) + p.shape[2:])
    return jnp.moveaxis(p, 1, 0)


def _s5_branch(h, w_in, a_re, a_im, log_dt, b_re, b_im, c_re, c_im, d_skip, w_glu, b_glu, w_out):
    Bsz, L, _ = h.shape
    proj = h @ w_in
    u, z = proj[..., :S5_WIDTH], proj[..., S5_WIDTH:]
    u32 = u.astype(jnp.float32)
    u_blocks = u32.reshape(Bsz, L, S5_BLOCKS, S5_GROUPS_PER_BLOCK, S5_GROUP).transpose(2, 0, 1, 3, 4)
    y = lax.map(_s5_scan_block, (u_blocks, _blocked_groups(a_re), _blocked_groups(a_im),
                                 _blocked_groups(log_dt), _blocked_groups(b_re), _blocked_groups(b_im),
                                 _blocked_groups(c_re), _blocked_groups(c_im)))
    y = y.transpose(1, 2, 0, 3, 4).reshape(Bsz, L, S5_WIDTH)
    y = (y + d_skip.astype(jnp.float32) * u32).astype(h.dtype)
    y = y * jax.nn.sigmoid(jax.nn.gelu(y) @ w_glu + b_glu)
    y = y * jax.nn.silu(z)
    return y @ w_out


def _diff_attn_branch(h, w_in, lq1, lk1, lq2, lk2, subln, w_out, lambda_init):
    Bsz, L, _ = h.shape
    H, dh = ATTN_HEADS, ATTN_HEAD_DIM
    proj = h @ w_in
    q, k, v, z = jnp.split(proj, 4, axis=-1)
    q = q.reshape(Bsz, L, 2 * H, dh)
    k = k.reshape(Bsz, L, 2 * H, dh)
    v = v.reshape(Bsz, L, H, 2 * dh)
    pos = jnp.arange(L, dtype=jnp.float32)
    inv_freq = 1.0 / (ROPE_THETA ** (jnp.arange(0, dh, 2, dtype=jnp.float32) / dh))
    ang = pos[:, None] * inv_freq[None, :]
    cos, sin = jnp.cos(ang), jnp.sin(ang)
    q = _rope(q, cos, sin)
    k = _rope(k, cos, sin)
    lam = (jnp.exp(jnp.sum(lq1.astype(jnp.float32) * lk1.astype(jnp.float32)))
           - jnp.exp(jnp.sum(lq2.astype(jnp.float32) * lk2.astype(jnp.float32)))
           + lambda_init)
    scale = 1.0 / math.sqrt(dh)
    k32 = k.astype(jnp.float32)
    nb = L // Q_BLOCK
    q_blocks = q.reshape(Bsz, nb, Q_BLOCK, 2 * H, dh).transpose(1, 0, 2, 3, 4)

    def one_block(qb):
        s = jnp.einsum('bqhd,bkhd->bhqk', qb.astype(jnp.float32), k32) * scale
        p = jax.nn.softmax(s, axis=-1).reshape(Bsz, H, 2, Q_BLOCK, L)
        a = p[:, :, 0] - lam * p[:, :, 1]
        return jnp.einsum('bhqk,bkhd->bqhd', a.astype(v.dtype), v)

    o = lax.map(one_block, q_blocks)
    o = o.transpose(1, 0, 2, 3, 4).reshape(Bsz, L, H, 2 * dh)
    o = _rmsnorm(o, subln, SUBLN_EPS) * (1.0 - lambda_init)
    o = o.reshape(Bsz, L, ATTN_WIDTH) * jax.nn.silu(z)
    return o @ w_out


def _trunk(x, s5_norm, s5_w_in, s5_a_re, s5_a_im, s5_log_dt, s5_b_re, s5_b_im, s5_c_re, s5_c_im,
           s5_d, s5_w_glu, s5_b_glu, s5_w_out, attn_norm, attn_w_in, attn_lambda_q1, attn_lambda_k1,
           attn_lambda_q2, attn_lambda_k2, attn_subln, attn_w_out, final_norm):
    h = x
    for i in range(DEPTH):
        j = i // N_MIXERS
        if i % N_MIXERS == 0:
            h = h + _s5_branch(_rmsnorm(h, s5_norm[j], NORM_EPS), s5_w_in[j], s5_a_re[j], s5_a_im[j],
                               s5_log_dt[j], s5_b_re[j], s5_b_im[j], s5_c_re[j], s5_c_im[j], s5_d[j],
                               s5_w_glu[j], s5_b_glu[j], s5_w_out[j])
        else:
            h = h + _diff_attn_branch(_rmsnorm(h, attn_norm[j], NORM_EPS), attn_w_in[j],
                                      attn_lambda_q1[j], attn_lambda_k1[j], attn_lambda_q2[j],
                                      attn_lambda_k2[j], attn_subln[j], attn_w_out[j], _lambda_init(i))
    return _rmsnorm(h, final_norm, NORM_EPS)


def setup_inputs(seed: int = 0) -> dict:
    key = jax.random.key(seed)
    ks = jax.random.split(key, 32)
    f32 = jnp.float32

    def nrm(k, shape, scale):
        return jax.random.normal(k, shape, f32) * scale

    G, P, HG, E = S5_GROUPS, S5_STATE, S5_GROUP, S5_WIDTH
    n_idx = jnp.arange(P, dtype=f32)
    return {
        "x_prompt": nrm(ks[0], (BATCH, SEQ, D_MODEL), 1.0),
        "x_sample": nrm(ks[1], (DEC_BATCH, DEC_SEQ, D_MODEL), 1.0),
        "s5_norm": 1.0 + nrm(ks[2], (N_S5_LAYERS, D_MODEL), 0.05),
        "s5_w_in": nrm(ks[3], (N_S5_LAYERS, D_MODEL, 2 * E), D_MODEL ** -0.5),
        "s5_a_re": -0.5 + nrm(ks[4], (N_S5_LAYERS, 2, G, P), 0.01),
        "s5_a_im": math.pi * n_idx + nrm(ks[5], (N_S5_LAYERS, 2, G, P), 0.01),
        "s5_log_dt": jax.random.uniform(ks[6], (N_S5_LAYERS, 2, G), f32,
                                        minval=math.log(1e-3), maxval=math.log(1e-1)),
        "s5_b_re": nrm(ks[7], (N_S5_LAYERS, 2, G, P, HG), (2 * HG) ** -0.5),
        "s5_b_im": nrm(ks[8], (N_S5_LAYERS, 2, G, P, HG), (2 * HG) ** -0.5),
        "s5_c_re": nrm(ks[9], (N_S5_LAYERS, 2, G, HG, P), 0.5),
        "s5_c_im": nrm(ks[10], (N_S5_LAYERS, 2, G, HG, P), 0.5),
        "s5_d": nrm(ks[11], (N_S5_LAYERS, E), 1.0),
        "s5_w_glu": nrm(ks[12], (N_S5_LAYERS, E, E), E ** -0.5),
        "s5_b_glu": nrm(ks[13], (N_S5_LAYERS, E), 0.01),
        "s5_w_out": nrm(ks[14], (N_S5_LAYERS, E, D_MODEL), E ** -0.5),
        "attn_norm": 1.0 + nrm(ks[15], (N_ATTN_LAYERS, D_MODEL), 0.05),
        "attn_w_in": nrm(ks[16], (N_ATTN_LAYERS, D_MODEL, 4 * ATTN_WIDTH), D_MODEL ** -0.5),
        "attn_lambda_q1": nrm(ks[17], (N_ATTN_LAYERS, ATTN_HEAD_DIM), 0.1),
        "attn_lambda_k1": nrm(ks[18], (N_ATTN_LAYERS, ATTN_HEAD_DIM), 0.1),
        "attn_lambda_q2": nrm(ks[19], (N_ATTN_LAYERS, ATTN_HEAD_DIM), 0.1),
        "attn_lambda_k2": nrm(ks[20], (N_ATTN_LAYERS, ATTN_HEAD_DIM), 0.1),
        "attn_subln": 1.0 + nrm(ks[21], (N_ATTN_LAYERS, 2 * ATTN_HEAD_DIM), 0.05),
        "attn_w_out": nrm(ks[22], (N_ATTN_LAYERS, ATTN_WIDTH, D_MODEL), ATTN_WIDTH ** -0.5),
        "final_norm": 1.0 + nrm(ks[23], (D_MODEL,), 0.05),
    }


def reference(x_prompt, x_sample, s5_norm, s5_w_in, s5_a_re, s5_a_im, s5_log_dt, s5_b_re, s5_b_im,
              s5_c_re, s5_c_im, s5_d, s5_w_glu, s5_b_glu, s5_w_out, attn_norm, attn_w_in,
              attn_lambda_q1, attn_lambda_k1, attn_lambda_q2, attn_lambda_k2, attn_subln, attn_w_out,
              final_norm):
    y_prompt = _trunk(x_prompt, s5_norm, s5_w_in, s5_a_re, s5_a_im, s5_log_dt, s5_b_re, s5_b_im,
                      s5_c_re, s5_c_im, s5_d, s5_w_glu, s5_b_glu, s5_w_out, attn_norm, attn_w_in,
                      attn_lambda_q1, attn_lambda_k1, attn_lambda_q2, attn_lambda_k2, attn_subln,
                      attn_w_out, final_norm)
    y_sample = _trunk(x_sample, s5_norm, s5_w_in, s5_a_re, s5_a_im, s5_log_dt, s5_b_re, s5_b_im,
                      s5_c_re, s5_c_im, s5_d, s5_w_glu, s5_b_glu, s5_w_out, attn_norm, attn_w_in,
                      attn_lambda_q1, attn_lambda_k1, attn_lambda_q2, attn_lambda_k2, attn_subln,
                      attn_w_out, final_norm)
    return (y_prompt, y_sample)
```

```python
import math
from contextlib import ExitStack

import numpy as np
import concourse.bass as bass
import concourse.mybir as mybir
from concourse.bass_utils import run_bass_kernel_spmd

F32 = mybir.dt.float32
BF16 = mybir.dt.bfloat16
AF = mybir.ActivationFunctionType
ALU = mybir.AluOpType
AX = mybir.AxisListType

NORM_EPS = 1e-6
SUBLN_EPS = 1e-5
ROPE_THETA = 10000.0
S5_LAMBDA_RE_MAX = -1e-4
SEM_ROT = 20000


class Buf:
    def __init__(self, name):
        self.name = name
        self.w = None
        self.r = {}
        self.dma_sem = None
        self.dma_cnt = 0


class Prog:
    def __init__(self, nc, stack):
        self.nc = nc
        self.stack = stack
        self.eng = {"pe": nc.tensor, "act": nc.scalar, "dve": nc.vector, "pool": nc.gpsimd, "sp": nc.sync}
        self.sems = {}
        self.cnt = {}
        self.waited = {}
        self.allsems = []
        for e in self.eng:
            self.sems[e] = [self._newsem(f"s_{e}_0")]
            self.cnt[e] = 0
        self.bufs = []
        self.n_inst = 0
        self.maxwait = {}
        self.total = {}
        self.sem_pool = []

    def _newsem(self, name):
        return self.stack.enter_context(self.nc.semaphore(name))

    def buf(self, name):
        b = Buf(name)
        b.psum = name.startswith(("tp", "ps", "cps", "cpo", "aps", "apr", "sps", "acc", "actp", "acpo"))
        self.bufs.append(b)
        return b

    def _wait(self, e, sem, val):
        key = (e, id(sem))
        if self.waited.get(key, 0) >= val:
            return
        self.waited[key] = val
        self.maxwait[id(sem)] = max(self.maxwait.get(id(sem), 0), val)
        self.eng[e].wait_ge(sem, val)
        self.n_inst += 1

    def _wait_ticket(self, e, t):
        if t is None:
            return
        kind, sem, val, src = t
        if kind == "eng" and src == e and e == "pe":
            return
        self._wait(e, sem, val)

    def _deps(self, e, reads, writes, is_dma=False):
        for b in reads:
            self._wait_ticket(e, b.w)
            if getattr(b, "psum", False):
                for t in b.r.values():
                    if not (t[0] == "eng" and t[3] == e):
                        self._wait_ticket(e, t)
        for b in writes:
            if not (is_dma and b.w is not None and b.w[0] == "dma"):
                self._wait_ticket(e, b.w)
            for t in b.r.values():
                self._wait_ticket(e, t)

    def _record(self, t, reads, writes):
        for b in reads:
            b.r[id(t[1])] = t
        for b in writes:
            b.w = t
            b.r = {}

    def op(self, e, fn, reads=(), writes=()):
        self._deps(e, reads, writes)
        if self.cnt[e] >= SEM_ROT:
            self.sems[e].append(self._newsem(f"s_{e}_{len(self.sems[e])}"))
            self.cnt[e] = 0
        sem = self.sems[e][-1]
        ins = fn(self.eng[e])
        ins.then_inc(sem, 1)
        self.cnt[e] += 1
        self.total[id(sem)] = self.cnt[e]
        self.n_inst += 1
        t = ("eng", sem, self.cnt[e], e)
        self._record(t, reads, writes)
        return t

    def dma(self, q, out, in_, sb, reads=(), writes=(), **kw):
        self._deps(q, reads, writes, is_dma=True)
        if sb.dma_sem is None:
            if self.sem_pool:
                sb.dma_sem, sb.dma_cnt = self.sem_pool.pop()
            else:
                sb.dma_sem = self._newsem(f"d_{len(self.allsems)}")
                sb.dma_cnt = 0
                self.allsems.append(sb.dma_sem)
        ins = self.eng[q].dma_start(out=out, in_=in_, **kw)
        ins.then_inc(sb.dma_sem, 16)
        sb.dma_cnt += 16
        self.total[id(sb.dma_sem)] = sb.dma_cnt
        self.n_inst += 1
        t = ("dma", sb.dma_sem, sb.dma_cnt, q)
        self._record(t, reads, writes)
        return t

    def barrier(self):
        for e in self.eng:
            for e2 in self.eng:
                if e2 != e and self.cnt[e2] > 0:
                    self._wait(e, self.sems[e2][-1], self.cnt[e2])
            for b in self.bufs:
                if b.dma_sem is not None and b.dma_cnt > 0:
                    self._wait(e, b.dma_sem, b.dma_cnt)
        for b in self.bufs:
            b.w = None
            b.r = {}
            if b.dma_sem is not None:
                self.sem_pool.append((b.dma_sem, b.dma_cnt))
                b.dma_sem = None
                b.dma_cnt = 0
        self.bufs = [b for b in self.bufs if b.name == "const"]


def _ceil_div(a, b):
    return (a + b - 1) // b


class Cfg:
    def __init__(self, L, D, H, depth=4):
        self.L = L
        self.D = D
        self.E = 2 * D
        self.G = self.E // 16
        self.AW = D
        self.H = H
        assert H * 256 == D
        self.depth = depth
        self.NS5 = (depth + 1) // 2
        self.NAT = max(depth // 2, 1)
        self.KT = D // 128
        self.ET = self.E // 128
        self.BT = 512
        self.NB = L // 512
        self.NT = L // 128
        self.NLEV = int(round(math.log2(L)))
        assert 2 ** self.NLEV == L


def lambda_init(i):
    return 0.8 - 0.6 * math.exp(-0.3 * i)


def build_program(cfg):
    c = cfg
    L, D, E, G, H, KT, ET, NB, NT, NLEV = c.L, c.D, c.E, c.G, c.H, c.KT, c.ET, c.NB, c.NT, c.NLEV
    NS5, NAT = c.NS5, c.NAT
    nc = bass.Bass("TRN2", target_bir_lowering=False)

    _uid = [0]

    def SBT(name, shape, dt):
        _uid[0] += 1
        return nc.sbuf_tensor(f"{name}_u{_uid[0]}", shape, dt)

    def PST(name, shape, dt):
        _uid[0] += 1
        return nc.psum_tensor(f"{name}_u{_uid[0]}", shape, dt)

    def din(name, shape, dt=F32):
        return nc.dram_tensor(name, list(shape), dt, kind="ExternalInput").ap()

    def dscr(name, shape, dt):
        kind = "ExternalOutput" if (getattr(cfg, "debug", False) and name in ("uT", "zsT", "yT", "hA", "hB", "qT", "kT", "vA", "zs", "oS")) else "Internal"
        return nc.dram_tensor(name, list(shape), dt, kind=kind).ap()

    x_in = din("x", [L, D])
    tokmask = din("tokmask", [128, NT])
    keybias = din("keybias", [128, NT])
    ropec = din("ropec", [128, L])
    ropes = din("ropes", [128, L])
    cident = din("cident", [128, 128])
    cjj = din("cjj", [128, 128])
    crot = din("crot", [128, 128])
    w = {}
    w["s5_norm"] = din("s5_norm", [NS5, D])
    w["s5_w_in"] = din("s5_w_in", [NS5, D, 2 * E])
    w["s5_a_re"] = din("s5_a_re", [NS5, 2, G, 64])
    w["s5_a_im"] = din("s5_a_im", [NS5, 2, G, 64])
    w["s5_log_dt"] = din("s5_log_dt", [NS5, 2, G])
    w["s5_b_re"] = din("s5_b_re", [NS5, 2, G, 64, 16])
    w["s5_b_im"] = din("s5_b_im", [NS5, 2, G, 64, 16])
    w["s5_c_re"] = din("s5_c_re", [NS5, 2, G, 16, 64])
    w["s5_c_im"] = din("s5_c_im", [NS5, 2, G, 16, 64])
    w["s5_d"] = din("s5_d", [NS5, E])
    w["s5_w_glu"] = din("s5_w_glu", [NS5, E, E])
    w["s5_b_glu"] = din("s5_b_glu", [NS5, E])
    w["s5_w_out"] = din("s5_w_out", [NS5, E, D])
    w["attn_norm"] = din("attn_norm", [NAT, D])
    w["attn_w_in"] = din("attn_w_in", [NAT, D, 4 * D])
    for nm in ("attn_lambda_q1", "attn_lambda_k1", "attn_lambda_q2", "attn_lambda_k2"):
        w[nm] = din(nm, [NAT, 128])
    w["attn_subln"] = din("attn_subln", [NAT, 256])
    w["attn_w_out"] = din("attn_w_out", [NAT, D, D])
    w["final_norm"] = din("final_norm", [D])
    y_out = nc.dram_tensor("y", [L, D], F32, kind="ExternalOutput").ap()

    hA = dscr("hA", [L, D], F32)
    hB = dscr("hB", [L, D], F32)
    wb = {
        "s5_w_in": dscr("wb_s5_w_in", [NS5, D, 2 * E], BF16),
        "s5_w_glu": dscr("wb_s5_w_glu", [NS5, E, E], BF16),
        "s5_w_out": dscr("wb_s5_w_out", [NS5, E, D], BF16),
        "attn_w_in": dscr("wb_attn_w_in", [NAT, D, 4 * D], BF16),
        "attn_w_out": dscr("wb_attn_w_out", [NAT, D, D], BF16),
    }
    uT = dscr("uT", [E, L], BF16)
    zsT = dscr("zsT", [E, L], BF16)
    yT = dscr("yT", [E, L], BF16)
    qT = dscr("qT", [2 * H, 128, L], BF16)
    kT = dscr("kT", [2 * H, 128, L], BF16)
    vA = dscr("vA", [H, L, 256], BF16)
    zs = dscr("zs", [L, D], BF16)
    oS = dscr("oS", [L, D], BF16)

    stack = ExitStack()
    with stack:
        P = Prog(nc, stack)

        def sb(name, shape, dt):
            return stack.enter_context(SBT(name, list(shape), dt))

        ident_f = sb("ident_f", [128, 128], F32)
        ident = sb("ident", [128, 128], BF16)
        jj_f = sb("jj_f", [128, 128], F32)
        rot_f = sb("rot_f", [128, 128], F32)
        rotb = sb("rotb", [128, 128], BF16)
        tmask = sb("tmask", [128, NT], F32)
        kbias = sb("kbias", [128, NT], F32)
        b_const = P.buf("const")
        P.dma("sp", ident_f[:], cident[:, :], b_const, writes=[b_const])
        P.dma("sp", jj_f[:], cjj[:, :], b_const, writes=[b_const])
        P.dma("sp", rot_f[:], crot[:, :], b_const, writes=[b_const])
        P.dma("sp", tmask[:], tokmask[:, :], b_const, writes=[b_const])
        P.dma("sp", kbias[:], keybias[:, :], b_const, writes=[b_const])
        P.op("dve", lambda e: e.tensor_copy(out=ident[:], in_=ident_f[:]), reads=[b_const], writes=[b_const])
        P.op("dve", lambda e: e.tensor_copy(out=rotb[:], in_=rot_f[:]), reads=[b_const], writes=[b_const])

        def phase_cast():
            with ExitStack() as st:
                CW = 2048
                stg_f = [st.enter_context(SBT(f"cw_f{i}", [128, CW], F32)) for i in range(2)]
                stg_b = [st.enter_context(SBT(f"cw_b{i}", [128, CW], BF16)) for i in range(2)]
                bf = [P.buf(f"cw_f{i}") for i in range(2)]
                bb = [P.buf(f"cw_b{i}") for i in range(2)]
                it = 0
                for nm, dst in wb.items():
                    src = w[nm]
                    nl, R, C = src.shape
                    for l in range(nl):
                        for r0 in range(0, R, 128):
                            for c0 in range(0, C, CW):
                                cw = min(CW, C - c0)
                                s = it % 2
                                P.dma("sp", stg_f[s][:, :cw], src[l, r0:r0 + 128, c0:c0 + cw], bf[s], writes=[bf[s]])
                                eng = "dve" if it % 2 == 0 else "pool"
                                P.op(eng, lambda e, s=s, cw=cw: e.tensor_copy(out=stg_b[s][:, :cw], in_=stg_f[s][:, :cw]),
                                     reads=[bf[s]], writes=[bb[s]])
                                P.dma("act", dst[l, r0:r0 + 128, c0:c0 + cw], stg_b[s][:, :cw], bb[s], reads=[bb[s]])
                                it += 1
                P.barrier()

        def norm_block(st_bufs, h_src, b, gain_sb):
            (hb, hb_b, hn, hn_b, ss, ss_b, junk, junk_b, hnT, hnT_b, tp_ps, tp_b) = st_bufs
            P.dma("sp", hb[:], h_src[b * 512:(b + 1) * 512, :].rearrange("(i p) d -> p i d", p=128), hb_b, writes=[hb_b])
            for i in range(4):
                P.op("act", lambda e, i=i: e.activation(out=junk[:], in_=hb[:, i, :], func=AF.Square,
                                                         accum_out=ss[:, i:i + 1]),
                     reads=[hb_b], writes=[junk_b, ss_b])
            P.op("dve", lambda e: e.tensor_scalar(out=ss[:, 0:4], in0=ss[:, 0:4], scalar1=1.0 / D, scalar2=NORM_EPS,
                                                  op0=ALU.mult, op1=ALU.add), reads=[ss_b], writes=[ss_b])
            P.op("act", lambda e: e.activation(out=ss[:, 0:4], in_=ss[:, 0:4], func=AF.Sqrt), reads=[ss_b], writes=[ss_b])
            P.op("dve", lambda e: e.reciprocal(out=ss[:, 0:4], in_=ss[:, 0:4]), reads=[ss_b], writes=[ss_b])
            P.op("dve", lambda e: e.tensor_tensor(out=ss[:, 0:4], in0=ss[:, 0:4], in1=tmask[:, b * 4:(b + 1) * 4],
                                                  op=ALU.mult), reads=[ss_b, b_const], writes=[ss_b])
            for i in range(4):
                P.op("act", lambda e, i=i: e.activation(out=hn[:, i, :], in_=hb[:, i, :], func=AF.Copy,
                                                         scale=ss[:, i:i + 1]),
                     reads=[hb_b, ss_b], writes=[hn_b])
            for kt in range(KT):
                s = kt % 2
                for i in range(4):
                    P.op("pe", lambda e, i=i, kt=kt, s=s: e.transpose(out=tp_ps[s][:, i * 128:(i + 1) * 128],
                                                                   in_=hn[:, i, kt * 128:(kt + 1) * 128],
                                                                   identity=ident[:]),
                         reads=[hn_b, b_const], writes=[tp_b[s]])
                eng = "dve" if kt % 2 == 0 else "pool"
                if eng == "pool":
                    eng = "dve"
                P.op(eng, lambda e, kt=kt, s=s: e.tensor_scalar(out=hnT[:, kt, :], in0=tp_ps[s][:, 0:512], scalar1=gain_sb[:, kt:kt + 1],
                                                           scalar2=None, op0=ALU.mult),
                     reads=[tp_b[s], b_const], writes=[hnT_b])

        def alloc_norm_bufs(st, tag):
            hb = st.enter_context(SBT(f"hb_{tag}", [128, 4, D], F32))
            hn = st.enter_context(SBT(f"hn_{tag}", [128, 4, D], BF16))
            ss = st.enter_context(SBT(f"ss_{tag}", [128, 4], F32))
            junk = st.enter_context(SBT(f"junk_{tag}", [128, D], BF16))
            hnT = st.enter_context(SBT(f"hnT_{tag}", [128, KT, 512], BF16))
            tp = [st.enter_context(PST(f"tp_{tag}{i}", [128, 1024], BF16)) for i in range(2)]
            return (hb, P.buf("hb"), hn, P.buf("hn"), ss, P.buf("ss"), junk, P.buf("junk"), hnT, P.buf("hnT"),
                    tp, [P.buf("tp0"), P.buf("tp1")])

        def load_gain(st, src_row, tag):
            g = st.enter_context(SBT(f"gain_{tag}", [128, KT], F32))
            P.dma("sp", g[:], src_row.rearrange("(kt p) -> p kt", p=128), b_const, writes=[b_const],
                  allow_slow_non_contiguous=True)
            return g

        def phase_s5a(j, h_src):
            with ExitStack() as st:
                nb = alloc_norm_bufs(st, "s5a")
                hnT, hnT_b = nb[8], nb[9]
                gain = load_gain(st, w["s5_norm"][j], "s5a")
                MW = 512
                wp = [st.enter_context(SBT(f"wp_s5a{i}", [128, KT, MW], BF16)) for i in range(2)]
                wp_b = [P.buf("wp0"), P.buf("wp1")]
                ps = [st.enter_context(PST(f"ps_s5a{i}", [128, 512], F32)) for i in range(2)]
                ps_b = [P.buf("ps0"), P.buf("ps1")]
                og = [st.enter_context(SBT(f"og_s5a{i}", [128, 512], BF16)) for i in range(4)]
                og_b = [P.buf(f"og{i}") for i in range(4)]
                wsrc = wb["s5_w_in"][j]
                it = 0
                oi = 0
                for b in range(NB):
                    norm_block(nb, h_src, b, gain)
                    for pn in range(2 * E // MW):
                        s = it % 2
                        it += 1
                        P.dma("pool", wp[s][:], wsrc[:, pn * MW:(pn + 1) * MW].rearrange("(kt p) m -> p kt m", p=128),
                              wp_b[s], writes=[wp_b[s]])
                        for mi in range(MW // 128):
                            m = pn * (MW // 128) + mi
                            q = m % 2
                            for kt in range(KT):
                                P.op("pe", lambda e, q=q, s=s, mi=mi, kt=kt: e.matmul(
                                    ps[q][:], lhsT=wp[s][:, kt, mi * 128:(mi + 1) * 128], rhs=hnT[:, kt, :],
                                    start=(kt == 0), stop=(kt == KT - 1)),
                                     reads=[wp_b[s], hnT_b], writes=[ps_b[q]])
                            o = oi % 4
                            oi += 1
                            if m < ET:
                                P.op("dve", lambda e, o=o, q=q: e.tensor_copy(out=og[o][:], in_=ps[q][:]),
                                     reads=[ps_b[q]], writes=[og_b[o]])
                                P.dma("sp", uT[m * 128:(m + 1) * 128, b * 512:(b + 1) * 512], og[o][:], og_b[o], reads=[og_b[o]])
                            else:
                                P.op("act", lambda e, o=o, q=q: e.activation(out=og[o][:], in_=ps[q][:], func=AF.Silu),
                                     reads=[ps_b[q]], writes=[og_b[o]])
                                P.dma("sp", zsT[(m - ET) * 128:(m - ET + 1) * 128, b * 512:(b + 1) * 512], og[o][:], og_b[o],
                                      reads=[og_b[o]])
                P.barrier()

        def phase_s5b(j):
            with ExitStack() as st:
                G2 = 2 * G
                are = st.enter_context(SBT("t_are", [128, G2], F32))
                aim = st.enter_context(SBT("t_aim", [128, G2], F32))
                dt = st.enter_context(SBT("t_dt", [128, G2], F32))
                t0 = st.enter_context(SBT("t_t0", [128, G2], F32))
                t1 = st.enter_context(SBT("t_t1", [128, G2], F32))
                t2 = st.enter_context(SBT("t_t2", [128, G2], F32))
                t3 = st.enter_context(SBT("t_t3", [128, G2], F32))
                fr = st.enter_context(SBT("t_fr", [128, G2], F32))
                fi = st.enter_context(SBT("t_fi", [128, G2], F32))
                PA = st.enter_context(SBT("t_PA", [128, NLEV, G2], F32))
                PC = st.enter_context(SBT("t_PC", [128, NLEV, G2], F32))
                sgn = st.enter_context(SBT("t_sgn", [128, 1], F32))
                dsk = st.enter_context(SBT("t_dsk", [16, G], F32))
                tb = P.buf("tables")
                for half in range(2):
                    P.dma("sp", are[half * 64:(half + 1) * 64, :], w["s5_a_re"][j].rearrange("d g p -> p (d g)"), tb,
                          writes=[tb], allow_slow_non_contiguous=True)
                    P.dma("sp", aim[half * 64:(half + 1) * 64, :], w["s5_a_im"][j].rearrange("d g p -> p (d g)"), tb,
                          writes=[tb], allow_slow_non_contiguous=True)
                P.dma("sp", dt[:], w["s5_log_dt"][j].rearrange("d g -> (d g)").partition_broadcast(128), tb, writes=[tb])
                P.dma("sp", dsk[:], w["s5_d"][j].rearrange("(g i) -> i g", i=16), tb, writes=[tb],
                      allow_slow_non_contiguous=True)
                V = lambda fn: P.op("dve", fn, reads=[tb], writes=[tb])
                A_ = lambda fn: P.op("act", fn, reads=[tb], writes=[tb])
                V(lambda e: e.memset(sgn[0:64, :], 1.0))
                V(lambda e: e.memset(sgn[64:128, :], -1.0))
                V(lambda e: e.tensor_scalar(out=are[:], in0=are[:], scalar1=S5_LAMBDA_RE_MAX, scalar2=None, op0=ALU.min))
                A_(lambda e: e.activation(out=dt[:], in_=dt[:], func=AF.Exp))
                V(lambda e: e.tensor_tensor(out=t0[:], in0=are[:], in1=dt[:], op=ALU.mult))
                A_(lambda e: e.activation(out=t0[:], in_=t0[:], func=AF.Exp))
                V(lambda e: e.tensor_tensor(out=t1[:], in0=aim[:], in1=dt[:], op=ALU.mult))
                TWO_PI = 2.0 * math.pi
                def sin_of(dst, shift):
                    V(lambda e: e.tensor_scalar(out=dst[:], in0=t1[:], scalar1=shift, scalar2=None, op0=ALU.add))
                    for _ in range(5):
                        V(lambda e: e.tensor_scalar(out=fr[:], in0=dst[:], scalar1=TWO_PI, scalar2=TWO_PI, op0=ALU.is_ge, op1=ALU.mult))
                        V(lambda e: e.tensor_tensor(out=dst[:], in0=dst[:], in1=fr[:], op=ALU.subtract))
                    V(lambda e: e.tensor_scalar(out=dst[:], in0=dst[:], scalar1=-math.pi, scalar2=None, op0=ALU.add))
                    V(lambda e: e.tensor_scalar(out=dst[:], in0=dst[:], scalar1=math.pi, scalar2=-math.pi, op0=ALU.min, op1=ALU.max))
                    A_(lambda e: e.activation(out=dst[:], in_=dst[:], func=AF.Sin))
                sin_of(t2, math.pi)
                sin_of(t3, 1.5 * math.pi)
                V(lambda e: e.tensor_tensor(out=PA[:, 0, :], in0=t0[:], in1=t3[:], op=ALU.mult))
                V(lambda e: e.tensor_tensor(out=t2[:], in0=t0[:], in1=t2[:], op=ALU.mult))
                V(lambda e: e.tensor_tensor(out=t0[:], in0=are[:], in1=are[:], op=ALU.mult))
                V(lambda e: e.tensor_tensor(out=t1[:], in0=aim[:], in1=aim[:], op=ALU.mult))
                V(lambda e: e.tensor_tensor(out=t0[:], in0=t0[:], in1=t1[:], op=ALU.add))
                V(lambda e: e.reciprocal(out=t0[:], in_=t0[:]))
                V(lambda e: e.tensor_scalar(out=t3[:], in0=PA[:, 0, :], scalar1=-1.0, scalar2=None, op0=ALU.add))
                V(lambda e: e.tensor_tensor(out=fr[:], in0=t3[:], in1=are[:], op=ALU.mult))
                V(lambda e: e.tensor_tensor(out=t1[:], in0=t2[:], in1=aim[:], op=ALU.mult))
                V(lambda e: e.tensor_tensor(out=fr[:], in0=fr[:], in1=t1[:], op=ALU.add))
                V(lambda e: e.tensor_tensor(out=fr[:], in0=fr[:], in1=t0[:], op=ALU.mult))
                V(lambda e: e.tensor_tensor(out=fi[:], in0=t2[:], in1=are[:], op=ALU.mult))
                V(lambda e: e.tensor_tensor(out=t1[:], in0=t3[:], in1=aim[:], op=ALU.mult))
                V(lambda e: e.tensor_tensor(out=fi[:], in0=fi[:], in1=t1[:], op=ALU.subtract))
                V(lambda e: e.tensor_tensor(out=fi[:], in0=fi[:], in1=t0[:], op=ALU.mult))
                V(lambda e: e.tensor_scalar(out=fi[:], in0=fi[:], scalar1=sgn[:, 0:1], scalar2=-1.0, op0=ALU.mult, op1=ALU.mult))
                V(lambda e: e.tensor_scalar(out=PC[:, 0, :], in0=t2[:], scalar1=sgn[:, 0:1], scalar2=None, op0=ALU.mult))
                for k in range(1, NLEV):
                    V(lambda e, k=k: e.tensor_tensor(out=t0[:], in0=PA[:, k - 1, :], in1=PA[:, k - 1, :], op=ALU.mult))
                    V(lambda e, k=k: e.tensor_tensor(out=t1[:], in0=PC[:, k - 1, :], in1=PC[:, k - 1, :], op=ALU.mult))
                    V(lambda e, k=k: e.tensor_tensor(out=PC[:, k, :], in0=PA[:, k - 1, :], in1=PC[:, k - 1, :], op=ALU.mult))
                    V(lambda e, k=k: e.tensor_scalar(out=PC[:, k, :], in0=PC[:, k, :], scalar1=2.0, scalar2=None, op0=ALU.mult))
                    V(lambda e, k=k: e.tensor_tensor(out=PA[:, k, :], in0=t0[:], in1=t1[:], op=ALU.subtract))

                NCH = 2
                ug = [st.enter_context(SBT(f"ug{i}", [16, L], BF16)) for i in range(2)]
                ug_b = [P.buf(f"ug{i}") for i in range(2)]
                X = [st.enter_context(SBT(f"X{i}", [128, L], BF16)) for i in range(2)]
                X_b = [P.buf(f"X{i}") for i in range(2)]
                bt1 = [st.enter_context(SBT(f"bt1_{i}", [128, 16], F32)) for i in range(2)]
                bt2 = [st.enter_context(SBT(f"bt2_{i}", [128, 16], F32)) for i in range(2)]
                bt_b = [P.buf(f"bt{i}") for i in range(2)]
                bbar = [st.enter_context(SBT(f"bbar{i}", [128, 16], BF16)) for i in range(2)]
                bbar_b = [P.buf(f"bbar{i}") for i in range(2)]
                Bl = [st.enter_context(SBT(f"Bl{i}", [16, 128], BF16)) for i in range(2)]
                Bl_b = [P.buf(f"Bl{i}") for i in range(2)]
                ct = [st.enter_context(SBT(f"ct{i}", [16, 128], F32)) for i in range(2)]
                ct_b = [P.buf(f"ct{i}") for i in range(2)]
                ctb = [st.enter_context(SBT(f"ctb{i}", [16, 128], BF16)) for i in range(2)]
                ctb_b = [P.buf(f"ctb{i}") for i in range(2)]
                Cl = [st.enter_context(SBT(f"Cl{i}", [128, 16], BF16)) for i in range(2)]
                Cl_b = [P.buf(f"Cl{i}") for i in range(2)]
                AM = [st.enter_context(SBT(f"AM{i}", [128, NLEV, 128], BF16)) for i in range(2)]
                AM_b = [P.buf(f"AM{i}") for i in range(2)]
                amt = [st.enter_context(SBT(f"amt{i}", [128, 128], F32)) for i in range(2)]
                amt_b = [P.buf(f"amt{i}") for i in range(2)]
                tps = st.enter_context(PST("tps_s5b", [128, 1024], BF16))
                tps_b = P.buf("tps")
                ps = [st.enter_context(PST(f"ps_s5b{i}", [128, 512], F32)) for i in range(4)]
                ps_b = [P.buf(f"psb{i}") for i in range(4)]
                pso = [st.enter_context(PST(f"pso_s5b{i}", [128, 512], F32)) for i in range(2)]
                pso_b = [P.buf(f"pso{i}") for i in range(2)]

                def cols(idx_list):
                    if len(idx_list) == 1:
                        return slice(idx_list[0], idx_list[0] + 1)
                    stp = idx_list[1] - idx_list[0]
                    return slice(idx_list[0], idx_list[-1] + 1, stp)

                for g in range(G):
                    gs = g % 2
                    P.dma("sp", ug[gs][:], uT[g * 16:(g + 1) * 16, :], ug_b[gs], writes=[ug_b[gs]])
                    for d in range(2):
                        col = d * G + g
                        P.dma("pool", bt1[d][0:64, :], w["s5_b_re"][j, d, g], bt_b[d], writes=[bt_b[d]])
                        P.dma("pool", bt1[d][64:128, :], w["s5_b_im"][j, d, g], bt_b[d], writes=[bt_b[d]])
                        P.dma("pool", bt2[d][0:64, :], w["s5_b_im"][j, d, g], bt_b[d], writes=[bt_b[d]])
                        P.dma("pool", bt2[d][64:128, :], w["s5_b_re"][j, d, g], bt_b[d], writes=[bt_b[d]])
                        P.op("dve", lambda e, d=d, col=col: e.tensor_scalar(out=bt1[d][:], in0=bt1[d][:], scalar1=fr[:, col:col + 1],
                                                                          scalar2=None, op0=ALU.mult),
                             reads=[bt_b[d], tb], writes=[bt_b[d]])
                        P.op("dve", lambda e, d=d, col=col: e.scalar_tensor_tensor(out=bbar[d][:], in0=bt2[d][:],
                                                                                 scalar=fi[:, col:col + 1], in1=bt1[d][:],
                                                                                 op0=ALU.mult, op1=ALU.add),
                             reads=[bt_b[d], tb], writes=[bbar_b[d]])
                        P.op("pe", lambda e, d=d: e.transpose(out=tps[0:16, 0:128], in_=bbar[d][:], identity=ident[:]),
                             reads=[bbar_b[d], b_const], writes=[tps_b])
                        P.op("dve", lambda e, d=d: e.tensor_copy(out=Bl[d][:], in_=tps[0:16, 0:128]), reads=[tps_b], writes=[Bl_b[d]])
                        P.dma("pool", ct[d][:, 0:64], w["s5_c_re"][j, d, g], ct_b[d], writes=[ct_b[d]])
                        P.dma("pool", ct[d][:, 64:128], w["s5_c_im"][j, d, g], ct_b[d], writes=[ct_b[d]])
                        P.op("dve", lambda e, d=d: e.tensor_copy(out=ctb[d][:, 0:64], in_=ct[d][:, 0:64]),
                             reads=[ct_b[d]], writes=[ctb_b[d]])
                        P.op("dve", lambda e, d=d: e.tensor_scalar(out=ctb[d][:, 64:128], in0=ct[d][:, 64:128], scalar1=-1.0,
                                                                 scalar2=None, op0=ALU.mult),
                             reads=[ct_b[d]], writes=[ctb_b[d]])
                        P.op("pe", lambda e, d=d: e.transpose(out=tps[:, 0:16], in_=ctb[d][:], identity=ident[0:16, 0:16]),
                             reads=[ctb_b[d], b_const], writes=[tps_b])
                        P.op("dve", lambda e, d=d: e.tensor_copy(out=Cl[d][:], in_=tps[:, 0:16]), reads=[tps_b], writes=[Cl_b[d]])
                        for k in range(NLEV):
                            P.op("pool", lambda e, d=d, k=k, col=col: e.tensor_scalar(out=amt[d][:], in0=jj_f[:],
                                                                                    scalar1=PC[:, k, col:col + 1], scalar2=None,
                                                                                    op0=ALU.mult),
                                 reads=[tb, b_const], writes=[amt_b[d]])
                            P.op("dve", lambda e, d=d, k=k, col=col: e.scalar_tensor_tensor(out=AM[d][:, k, :], in0=ident_f[:],
                                                                                          scalar=PA[:, k, col:col + 1],
                                                                                          in1=amt[d][:], op0=ALU.mult, op1=ALU.add),
                                 reads=[tb, b_const, amt_b[d]], writes=[AM_b[d]])
                        for blk in range(NB):
                            q = (2 * blk + d) % 4
                            P.op("pe", lambda e, d=d, q=q, blk=blk: e.matmul(ps[q][:], lhsT=Bl[d][:], rhs=ug[gs][:, blk * 512:(blk + 1) * 512],
                                                                         start=True, stop=True),
                                 reads=[Bl_b[d], ug_b[gs]], writes=[ps_b[q]])
                            ev = "act" if d == 0 else "dve"
                            if ev == "act":
                                P.op("act", lambda e, d=d, q=q, blk=blk: e.activation(out=X[d][:, blk * 512:(blk + 1) * 512], in_=ps[q][:],
                                                                                   func=AF.Copy),
                                     reads=[ps_b[q]], writes=[X_b[d]])
                            else:
                                P.op("dve", lambda e, d=d, q=q, blk=blk: e.tensor_copy(out=X[d][:, blk * 512:(blk + 1) * 512], in_=ps[q][:]),
                                     reads=[ps_b[q]], writes=[X_b[d]])
                    steps = []
                    for k in range(NLEV):
                        s = 2 ** (k + 1)
                        tgt = list(range(s - 1, L, s))
                        steps.append((k, tgt))
                    for k in range(NLEV - 2, -1, -1):
                        s = 2 ** (k + 1)
                        tgt = list(range(s - 1 + 2 ** k, L, s))
                        steps.append((k, tgt))
                    qi = 0
                    for (k, tgt) in steps:
                        hop = 2 ** k
                        for c0 in range(0, len(tgt), 512):
                            tg = tgt[c0:c0 + 512]
                            for d in range(2):
                                if d == 0:
                                    tcols = cols(tg)
                                    scols = cols([t - hop for t in tg])
                                else:
                                    tcols = cols(sorted(L - 1 - t for t in tg))
                                    scols = cols(sorted(L - 1 - (t - hop) for t in tg))
                                n = len(tg)
                                q = qi % 4
                                qi += 1
                                P.op("pe", lambda e, d=d, q=q, n=n, tcols=tcols: e.matmul(ps[q][:, 0:n], lhsT=ident[:], rhs=X[d][:, tcols],
                                                                                      start=True, stop=False),
                                     reads=[X_b[d], b_const], writes=[ps_b[q]])
                                P.op("pe", lambda e, d=d, q=q, n=n, k=k, scols=scols: e.matmul(ps[q][:, 0:n], lhsT=AM[d][:, k, :],
                                                                                           rhs=X[d][:, scols], start=False, stop=True),
                                     reads=[X_b[d], AM_b[d]], writes=[ps_b[q]])
                                if d == 0:
                                    P.op("act", lambda e, d=d, q=q, n=n, tcols=tcols: e.activation(out=X[d][:, tcols], in_=ps[q][:, 0:n],
                                                                                               func=AF.Copy),
                                         reads=[ps_b[q]], writes=[X_b[d]])
                                else:
                                    P.op("dve", lambda e, d=d, q=q, n=n, tcols=tcols: e.tensor_copy(out=X[d][:, tcols], in_=ps[q][:, 0:n]),
                                         reads=[ps_b[q]], writes=[X_b[d]])
                    for blk in range(NB):
                        q = blk % 2
                        sl = slice(blk * 512, (blk + 1) * 512)
                        P.op("pe", lambda e, q=q, sl=sl: e.matmul(pso[q][0:16, :], lhsT=Cl[0][:], rhs=X[0][:, sl], start=True, stop=False),
                             reads=[Cl_b[0], X_b[0]], writes=[pso_b[q]])
                        P.op("pe", lambda e, q=q, sl=sl: e.matmul(pso[q][0:16, :], lhsT=Cl[1][:], rhs=X[1][:, sl], start=False, stop=True),
                             reads=[Cl_b[1], X_b[1]], writes=[pso_b[q]])
                        P.op("dve", lambda e, q=q, sl=sl, g=g: e.scalar_tensor_tensor(out=ug[gs][:, sl], in0=ug[gs][:, sl],
                                                                                   scalar=dsk[:, g:g + 1], in1=pso[q][0:16, :],
                                                                                   op0=ALU.mult, op1=ALU.add),
                             reads=[pso_b[q], ug_b[gs], tb], writes=[ug_b[gs]])
                    P.dma("act", yT[g * 16:(g + 1) * 16, :], ug[gs][:], ug_b[gs], reads=[ug_b[gs]])
                P.barrier()

        def phase_s5c(j, h_src, h_dst):
            with ExitStack() as st:
                NW0 = min(256, D)
                yb = st.enter_context(SBT("c_yb", [128, ET, 512], BF16))
                gb = st.enter_context(SBT("c_gb", [128, ET, 512], BF16))
                yb_b, gb_b = P.buf("yb"), P.buf("gb")
                tq = st.enter_context(SBT("c_tq", [128, 2048], F32))
                tq_b = P.buf("tq")
                zt = [st.enter_context(SBT(f"c_zt{i}", [128, 512], BF16)) for i in range(2)]
                zt_b = [P.buf("zt0"), P.buf("zt1")]
                sg = [st.enter_context(SBT(f"c_sg{i}", [128, 512], BF16)) for i in range(2)]
                sg_b = [P.buf("sg0"), P.buf("sg1")]
                hb = [st.enter_context(SBT(f"c_hb{i}", [128, 4, NW0], F32)) for i in range(2)]
                hb_b = [P.buf("chb0"), P.buf("chb1")]
                MW = 256
                wg = [st.enter_context(SBT(f"c_wg{i}", [128, ET, MW], BF16)) for i in range(2)]
                wg_b = [P.buf("wg0"), P.buf("wg1")]
                NW = min(256, D)
                wo = [st.enter_context(SBT(f"c_wo{i}", [128, ET, NW], BF16)) for i in range(2)]
                wo_b = [P.buf("wo0"), P.buf("wo1")]
                bglu = st.enter_context(SBT("c_bglu", [128, ET], F32))
                ps = [st.enter_context(PST(f"c_ps{i}", [128, 512], F32)) for i in range(2)]
                ps_b = [P.buf("cps0"), P.buf("cps1")]
                po = [st.enter_context(PST(f"c_po{i}", [128, 4, NW], F32)) for i in range(2)]
                po_b = [P.buf("cpo0"), P.buf("cpo1")]
                P.dma("sp", bglu[:], w["s5_b_glu"][j].rearrange("(m p) -> p m", p=128), b_const, writes=[b_const],
                      allow_slow_non_contiguous=True)
                wgsrc = wb["s5_w_glu"][j]
                wosrc = wb["s5_w_out"][j]
                itg = 0
                ito = 0
                ih = 0
                for b in range(NB):
                    bs = slice(b * 512, (b + 1) * 512)
                    P.dma("sp", yb[:], yT[:, bs].rearrange("(m p) t -> p m t", p=128), yb_b, writes=[yb_b])
                    CH = 4
                    for c0 in range(0, ET, CH):
                        cn = min(CH, ET - c0)
                        ysl = yb[:, c0:c0 + cn, :]
                        tsl = tq[:, 0:cn * 512].rearrange("p (m t) -> p m t", t=512)
                        gsl = gb[:, c0:c0 + cn, :]
                        P.op("act", lambda e, ysl=ysl, tsl=tsl: e.activation(out=tsl, in_=ysl, func=AF.Square),
                             reads=[yb_b], writes=[tq_b])
                        P.op("dve", lambda e, tsl=tsl: e.tensor_scalar(out=tsl, in0=tsl, scalar1=0.044715, scalar2=1.0,
                                                                      op0=ALU.mult, op1=ALU.add), reads=[tq_b], writes=[tq_b])
                        P.op("dve", lambda e, tsl=tsl, ysl=ysl: e.tensor_tensor(out=tsl, in0=tsl, in1=ysl, op=ALU.mult),
                             reads=[tq_b, yb_b], writes=[tq_b])
                        P.op("act", lambda e, tsl=tsl: e.activation(out=tsl, in_=tsl, func=AF.Sigmoid, scale=1.5957691216057308),
                             reads=[tq_b], writes=[tq_b])
                        P.op("dve", lambda e, tsl=tsl, ysl=ysl, gsl=gsl: e.tensor_tensor(out=gsl, in0=tsl, in1=ysl, op=ALU.mult),
                             reads=[tq_b, yb_b], writes=[gb_b])
                    for pn in range(E // MW):
                        s = itg % 2
                        itg += 1
                        P.dma("pool", wg[s][:], wgsrc[:, pn * MW:(pn + 1) * MW].rearrange("(kt p) m -> p kt m", p=128),
                              wg_b[s], writes=[wg_b[s]])
                        for mi in range(MW // 128):
                            m = pn * (MW // 128) + mi
                            q = m % 2
                            P.dma("sp", zt[q][:], zsT[m * 128:(m + 1) * 128, bs], zt_b[q], writes=[zt_b[q]])
                            for kt in range(ET):
                                P.op("pe", lambda e, q=q, s=s, mi=mi, kt=kt: e.matmul(
                                    ps[q][:], lhsT=wg[s][:, kt, mi * 128:(mi + 1) * 128], rhs=gb[:, kt, :],
                                    start=(kt == 0), stop=(kt == ET - 1)),
                                     reads=[wg_b[s], gb_b], writes=[ps_b[q]])
                            P.op("act", lambda e, q=q, m=m: e.activation(out=sg[q][:], in_=ps[q][:], func=AF.Sigmoid,
                                                                         bias=bglu[:, m:m + 1]),
                                 reads=[ps_b[q], b_const], writes=[sg_b[q]])
                            P.op("dve", lambda e, q=q: e.tensor_tensor(out=sg[q][:], in0=sg[q][:], in1=zt[q][:], op=ALU.mult),
                                 reads=[sg_b[q], zt_b[q]], writes=[sg_b[q]])
                            P.op("dve", lambda e, q=q, m=m: e.tensor_tensor(out=yb[:, m, :], in0=yb[:, m, :], in1=sg[q][:], op=ALU.mult),
                                 reads=[sg_b[q], yb_b], writes=[yb_b])
                    for pn in range(D // NW):
                        s = ito % 2
                        ito += 1
                        P.dma("pool", wo[s][:], wosrc[:, pn * NW:(pn + 1) * NW].rearrange("(kt p) n -> p kt n", p=128),
                              wo_b[s], writes=[wo_b[s]])
                        q = pn % 2
                        for i in range(4):
                            for kt in range(ET):
                                P.op("pe", lambda e, q=q, s=s, i=i, kt=kt: e.matmul(
                                    po[q][:, i, :], lhsT=yb[:, kt, i * 128:(i + 1) * 128], rhs=wo[s][:, kt, :],
                                    start=(kt == 0), stop=(kt == ET - 1)),
                                     reads=[wo_b[s], yb_b], writes=[po_b[q]])
                        hs = ih % 2
                        ih += 1
                        hv = hb[hs][:]
                        P.dma("sp", hv, h_src[bs, pn * NW:(pn + 1) * NW].rearrange("(i p) n -> p i n", p=128), hb_b[hs],
                              writes=[hb_b[hs]])
                        P.op("dve", lambda e, hv=hv, q=q: e.tensor_tensor(out=hv, in0=hv, in1=po[q][:], op=ALU.add),
                             reads=[hb_b[hs], po_b[q]], writes=[hb_b[hs]])
                        P.dma("act", h_dst[bs, pn * NW:(pn + 1) * NW].rearrange("(i p) n -> p i n", p=128), hv, hb_b[hs],
                              reads=[hb_b[hs]])
                P.barrier()

        def phase_at_a(j, h_src):
            with ExitStack() as st:
                nb = alloc_norm_bufs(st, "ata")
                hnT, hnT_b = nb[8], nb[9]
                gain = load_gain(st, w["attn_norm"][j], "ata")
                MW = min(512, D)
                wp = [st.enter_context(SBT(f"a_wp{i}", [128, KT, MW], BF16)) for i in range(2)]
                wp_b = [P.buf("awp0"), P.buf("awp1")]
                ps = [st.enter_context(PST(f"a_ps{i}", [128, 512], F32)) for i in range(2)]
                ps_b = [P.buf("aps0"), P.buf("aps1")]
                pr = [st.enter_context(PST(f"a_pr{i}", [128, 512], F32)) for i in range(2)]
                pr_b = [P.buf("apr0"), P.buf("apr1")]
                qb = [st.enter_context(SBT(f"a_qb{i}", [128, 512], BF16)) for i in range(2)]
                qb_b = [P.buf("aqb0"), P.buf("aqb1")]
                qf = [st.enter_context(SBT(f"a_qf{i}", [128, 512], F32)) for i in range(2)]
                qf_b = [P.buf("aqf0"), P.buf("aqf1")]
                og = [st.enter_context(SBT(f"a_og{i}", [128, 512], BF16)) for i in range(4)]
                og_b = [P.buf(f"aog{i}") for i in range(4)]
                rc = st.enter_context(SBT("a_rc", [128, 512], F32))
                rs_ = st.enter_context(SBT("a_rs", [128, 512], F32))
                rc_b = P.buf("rc")
                wsrc = wb["attn_w_in"][j]
                it = 0
                oi = 0
                for b in range(NB):
                    bs = slice(b * 512, (b + 1) * 512)
                    norm_block(nb, h_src, b, gain)
                    P.dma("sp", rc[:], ropec[:, bs], rc_b, writes=[rc_b])
                    P.dma("sp", rs_[:], ropes[:, bs], rc_b, writes=[rc_b])
                    for pn in range(4 * D // MW):
                        s = it % 2
                        it += 1
                        P.dma("pool", wp[s][:], wsrc[:, pn * MW:(pn + 1) * MW].rearrange("(kt p) m -> p kt m", p=128),
                              wp_b[s], writes=[wp_b[s]])
                        if pn < 2 * D // MW:
                            for mi in range(MW // 128):
                                m = pn * (MW // 128) + mi
                                q = m % 2
                                for kt in range(KT):
                                    P.op("pe", lambda e, q=q, s=s, mi=mi, kt=kt: e.matmul(
                                        ps[q][:], lhsT=wp[s][:, kt, mi * 128:(mi + 1) * 128], rhs=hnT[:, kt, :],
                                        start=(kt == 0), stop=(kt == KT - 1)),
                                         reads=[wp_b[s], hnT_b], writes=[ps_b[q]])
                                P.op("act", lambda e, q=q: e.activation(out=qb[q][:], in_=ps[q][:], func=AF.Copy),
                                     reads=[ps_b[q]], writes=[qb_b[q]])
                                P.op("pe", lambda e, q=q: e.matmul(pr[q][:], lhsT=rotb[:], rhs=qb[q][:], start=True, stop=True),
                                     reads=[qb_b[q], b_const], writes=[pr_b[q]])
                                P.op("dve", lambda e, q=q: e.tensor_tensor(out=qf[q][:], in0=ps[q][:], in1=rc[:], op=ALU.mult),
                                     reads=[ps_b[q], rc_b], writes=[qf_b[q]])
                                o = oi % 4
                                oi += 1
                                P.op("dve", lambda e, q=q, o=o: e.tensor_tensor(out=og[o][:], in0=pr[q][:], in1=rs_[:], op=ALU.mult),
                                     reads=[pr_b[q], rc_b], writes=[og_b[o]])
                                P.op("pool", lambda e, q=q, o=o: e.tensor_tensor(out=og[o][:], in0=og[o][:], in1=qf[q][:], op=ALU.add),
                                     reads=[qf_b[q], og_b[o]], writes=[og_b[o]])
                                dst = qT[m] if m < 2 * H else kT[m - 2 * H]
                                P.dma("sp", dst[:, bs], og[o][:], og_b[o], reads=[og_b[o]])
                        else:
                            cc0 = pn * MW - 2 * D
                            for i in range(4):
                                q = i % 2
                                for kt in range(KT):
                                    P.op("pe", lambda e, q=q, s=s, i=i, kt=kt: e.matmul(
                                        ps[q][:, 0:MW], lhsT=hnT[:, kt, i * 128:(i + 1) * 128], rhs=wp[s][:, kt, :],
                                        start=(kt == 0), stop=(kt == KT - 1)),
                                         reads=[wp_b[s], hnT_b], writes=[ps_b[q]])
                                o = oi % 4
                                oi += 1
                                rs = slice(b * 512 + i * 128, b * 512 + (i + 1) * 128)
                                if cc0 < D:
                                    P.op("dve", lambda e, o=o, q=q: e.tensor_copy(out=og[o][:, 0:MW], in_=ps[q][:, 0:MW]),
                                         reads=[ps_b[q]], writes=[og_b[o]])
                                    for hh in range(MW // 256):
                                        hd = (cc0 + hh * 256) // 256
                                        P.dma("sp", vA[hd, rs, :], og[o][:, hh * 256:(hh + 1) * 256], og_b[o], reads=[og_b[o]])
                                else:
                                    P.op("act", lambda e, o=o, q=q: e.activation(out=og[o][:, 0:MW], in_=ps[q][:, 0:MW], func=AF.Silu),
                                         reads=[ps_b[q]], writes=[og_b[o]])
                                    P.dma("sp", zs[rs, cc0 - D:cc0 - D + MW], og[o][:, 0:MW], og_b[o], reads=[og_b[o]])
                P.barrier()

        def phase_at_b(j, layer_idx):
            li = lambda_init(layer_idx)
            with ExitStack() as st:
                QB = 256
                NQ = L // QB
                Kt = st.enter_context(SBT("b_Kt", [128, 2, L], BF16))
                Vt = st.enter_context(SBT("b_Vt", [128, NT, 258], BF16))
                Kt_b, Vt_b = P.buf("Kt"), P.buf("Vt")
                Qt = [st.enter_context(SBT(f"b_Qt{i}", [128, 2, QB], BF16)) for i in range(2)]
                Qt_b = [P.buf("Qt0"), P.buf("Qt1")]
                Pt = [st.enter_context(SBT(f"b_Pt{i}", [128, 2 * QB], BF16)) for i in range(3)]
                Pt_b = [P.buf(f"Pt{i}") for i in range(3)]
                sps = [st.enter_context(PST(f"b_sps{i}", [128, 2 * QB], F32)) for i in range(2)]
                sps_b = [P.buf("sps0"), P.buf("sps1")]
                acc = [st.enter_context(PST(f"b_acc{i}", [128, 512], F32)) for i in range(4)]
                acc_b = [P.buf(f"acc{i}") for i in range(4)]
                lam = st.enter_context(SBT("b_lam", [128, 8], F32))
                lqk = st.enter_context(SBT("b_lqk", [128, 4, 128], F32))
                subg = st.enter_context(SBT("b_subg", [128, 256], F32))
                lam_b = P.buf("lam")
                o1 = st.enter_context(SBT("b_o1", [128, 256], F32))
                o2 = st.enter_context(SBT("b_o2", [128, 256], F32))
                o_b = P.buf("o12")
                rcp = st.enter_context(SBT("b_rcp", [128, 4], F32))
                zt = [st.enter_context(SBT(f"b_zt{i}", [128, 256], BF16)) for i in range(2)]
                zt_b = [P.buf("bzt0"), P.buf("bzt1")]
                oo = [st.enter_context(SBT(f"b_oo{i}", [128, 256], BF16)) for i in range(2)]
                oo_b = [P.buf("boo0"), P.buf("boo1")]
                junk = st.enter_context(SBT("b_junk", [128, 256], F32))
                for ii, nm in enumerate(("attn_lambda_q1", "attn_lambda_k1", "attn_lambda_q2", "attn_lambda_k2")):
                    P.dma("sp", lqk[:, ii, :], w[nm][j].partition_broadcast(128), lam_b, writes=[lam_b])
                P.dma("sp", subg[:], w["attn_subln"][j].partition_broadcast(128), lam_b, writes=[lam_b])
                for ii in range(2):
                    P.op("dve", lambda e, ii=ii: e.tensor_tensor(out=lqk[:, 2 * ii, :], in0=lqk[:, 2 * ii, :], in1=lqk[:, 2 * ii + 1, :],
                                                              op=ALU.mult), reads=[lam_b], writes=[lam_b])
                    P.op("dve", lambda e, ii=ii: e.reduce_sum(out=lam[:, ii:ii + 1], in_=lqk[:, 2 * ii, :], axis=AX.X),
                         reads=[lam_b], writes=[lam_b])
                P.op("act", lambda e: e.activation(out=lam[:, 0:2], in_=lam[:, 0:2], func=AF.Exp), reads=[lam_b], writes=[lam_b])
                P.op("dve", lambda e: e.tensor_tensor(out=lam[:, 2:3], in0=lam[:, 0:1], in1=lam[:, 1:2], op=ALU.subtract),
                     reads=[lam_b], writes=[lam_b])
                P.op("dve", lambda e: e.tensor_scalar(out=lam[:, 3:4], in0=lam[:, 2:3], scalar1=li, scalar2=-1.0,
                                                      op0=ALU.add, op1=ALU.mult), reads=[lam_b], writes=[lam_b])
                P.op("dve", lambda e: e.tensor_scalar(out=subg[:], in0=subg[:], scalar1=1.0 - li, scalar2=None, op0=ALU.mult),
                     reads=[lam_b], writes=[lam_b])
                scale = 1.0 / math.sqrt(128.0)
                pi = 0
                for hd in range(H):
                    for s2 in range(2):
                        P.dma("sp", Kt[:, s2, :], kT[2 * hd + s2], Kt_b, writes=[Kt_b])
                    P.dma("pool", Vt[:, :, 0:256], vA[hd].rearrange("(t p) c -> p t c", p=128), Vt_b, writes=[Vt_b])
                    P.op("dve", lambda e: e.memset(Vt[:, :, 256:257], 1.0), writes=[Vt_b])
                    for qb in range(NQ):
                        qs = slice(qb * QB, (qb + 1) * QB)
                        qq = qb % 2
                        for s2 in range(2):
                            P.dma("sp", Qt[qq][:, s2, :], qT[2 * hd + s2][:, qs], Qt_b[qq], writes=[Qt_b[qq]])
                        for kt in range(NT):
                            sq = kt % 2
                            for s2 in range(2):
                                P.op("pe", lambda e, sq=sq, s2=s2, kt=kt, qq=qq: e.matmul(
                                    sps[sq][:, s2 * QB:(s2 + 1) * QB], lhsT=Kt[:, s2, kt * 128:(kt + 1) * 128], rhs=Qt[qq][:, s2, :],
                                    start=True, stop=True),
                                     reads=[Kt_b, Qt_b[qq]], writes=[sps_b[sq]])
                            pp = pi % 3
                            pi += 1
                            P.op("act", lambda e, pp=pp, sq=sq, kt=kt: e.activation(out=Pt[pp][:], in_=sps[sq][:], func=AF.Exp,
                                                                                 scale=scale, bias=kbias[:, kt:kt + 1]),
                                 reads=[sps_b[sq], b_const], writes=[Pt_b[pp]])
                            for s2 in range(2):
                                for qi in range(QB // 128):
                                    a = s2 * 2 + qi
                                    P.op("pe", lambda e, a=a, pp=pp, s2=s2, qi=qi, kt=kt: e.matmul(
                                        acc[a][:, 0:257], lhsT=Pt[pp][:, s2 * QB + qi * 128:s2 * QB + (qi + 1) * 128],
                                        rhs=Vt[:, kt, 0:257], start=(kt == 0), stop=(kt == NT - 1)),
                                         reads=[Pt_b[pp], Vt_b], writes=[acc_b[a]])
                        for qi in range(QB // 128):
                            a1, a2 = qi, 2 + qi
                            rs = slice(qb * QB + qi * 128, qb * QB + (qi + 1) * 128)
                            zq = (qb * 2 + qi) % 2
                            P.dma("sp", zt[zq][:], zs[rs, hd * 256:(hd + 1) * 256], zt_b[zq], writes=[zt_b[zq]])
                            P.op("dve", lambda e, a1=a1: e.reciprocal(out=rcp[:, 0:1], in_=acc[a1][:, 256:257]),
                                 reads=[acc_b[a1]], writes=[o_b])
                            P.op("dve", lambda e, a2=a2: e.reciprocal(out=rcp[:, 1:2], in_=acc[a2][:, 256:257]),
                                 reads=[acc_b[a2]], writes=[o_b])
                            P.op("dve", lambda e: e.tensor_tensor(out=rcp[:, 1:2], in0=rcp[:, 1:2], in1=lam[:, 3:4], op=ALU.mult),
                                 reads=[o_b, lam_b], writes=[o_b])
                            P.op("act", lambda e, a1=a1: e.activation(out=o1[:], in_=acc[a1][:, 0:256], func=AF.Copy, scale=rcp[:, 0:1]),
                                 reads=[acc_b[a1], o_b], writes=[o_b])
                            P.op("dve", lambda e, a2=a2: e.scalar_tensor_tensor(out=o2[:], in0=acc[a2][:, 0:256], scalar=rcp[:, 1:2],
                                                                              in1=o1[:], op0=ALU.mult, op1=ALU.add),
                                 reads=[acc_b[a2], o_b], writes=[o_b])
                            P.op("act", lambda e: e.activation(out=junk[:], in_=o2[:], func=AF.Square, accum_out=rcp[:, 2:3]),
                                 reads=[o_b], writes=[o_b])
                            P.op("dve", lambda e: e.tensor_scalar(out=rcp[:, 2:3], in0=rcp[:, 2:3], scalar1=1.0 / 256.0, scalar2=SUBLN_EPS,
                                                                  op0=ALU.mult, op1=ALU.add), reads=[o_b], writes=[o_b])
                            P.op("act", lambda e: e.activation(out=rcp[:, 2:3], in_=rcp[:, 2:3], func=AF.Sqrt), reads=[o_b], writes=[o_b])
                            P.op("dve", lambda e: e.reciprocal(out=rcp[:, 2:3], in_=rcp[:, 2:3]), reads=[o_b], writes=[o_b])
                            P.op("dve", lambda e: e.scalar_tensor_tensor(out=o1[:], in0=o2[:], scalar=rcp[:, 2:3], in1=subg[:],
                                                                         op0=ALU.mult, op1=ALU.mult),
                                 reads=[o_b, lam_b], writes=[o_b])
                            P.op("dve", lambda e, zq=zq: e.tensor_tensor(out=oo[zq][:], in0=o1[:], in1=zt[zq][:], op=ALU.mult),
                                 reads=[o_b, zt_b[zq]], writes=[oo_b[zq]])
                            P.dma("act", oS[rs, hd * 256:(hd + 1) * 256], oo[zq][:], oo_b[zq], reads=[oo_b[zq]])
                P.barrier()

        def phase_at_c(j, h_src, h_dst):
            with ExitStack() as st:
                ob = st.enter_context(SBT("ac_ob", [128, 4, D], BF16))
                ob_b = P.buf("ob")
                oT = st.enter_context(SBT("ac_oT", [128, KT, 512], BF16))
                oT_b = P.buf("oT")
                tp = [st.enter_context(PST(f"ac_tp{i}", [128, 1024], BF16)) for i in range(2)]
                tp_b = [P.buf("actp0"), P.buf("actp1")]
                NW = min(256, D)
                wo = [st.enter_context(SBT(f"ac_wo{i}", [128, KT, NW], BF16)) for i in range(2)]
                wo_b = [P.buf("acwo0"), P.buf("acwo1")]
                po = [st.enter_context(PST(f"ac_po{i}", [128, 4, NW], F32)) for i in range(2)]
                po_b = [P.buf("acpo0"), P.buf("acpo1")]
                hb = [st.enter_context(SBT(f"ac_hb{i}", [128, 4, NW], F32)) for i in range(2)]
                hb_b = [P.buf("achb0"), P.buf("achb1")]
                wosrc = wb["attn_w_out"][j]
                ito = 0
                for b in range(NB):
                    bs = slice(b * 512, (b + 1) * 512)
                    P.dma("sp", ob[:], oS[bs, :].rearrange("(i p) d -> p i d", p=128), ob_b, writes=[ob_b])
                    for kt in range(KT):
                        s = kt % 2
                        for i in range(4):
                            P.op("pe", lambda e, i=i, kt=kt, s=s: e.transpose(out=tp[s][:, i * 128:(i + 1) * 128],
                                                                           in_=ob[:, i, kt * 128:(kt + 1) * 128], identity=ident[:]),
                                 reads=[ob_b, b_const], writes=[tp_b[s]])
                        P.op("dve", lambda e, kt=kt, s=s: e.tensor_copy(out=oT[:, kt, :], in_=tp[s][:, 0:512]),
                             reads=[tp_b[s]], writes=[oT_b])
                    for pn in range(D // NW):
                        s = ito % 2
                        ito += 1
                        P.dma("pool", wo[s][:], wosrc[:, pn * NW:(pn + 1) * NW].rearrange("(kt p) n -> p kt n", p=128),
                              wo_b[s], writes=[wo_b[s]])
                        for i in range(4):
                            for kt in range(KT):
                                P.op("pe", lambda e, s=s, i=i, kt=kt: e.matmul(
                                    po[s][:, i, :], lhsT=oT[:, kt, i * 128:(i + 1) * 128], rhs=wo[s][:, kt, :],
                                    start=(kt == 0), stop=(kt == KT - 1)),
                                     reads=[wo_b[s], oT_b], writes=[po_b[s]])
                        P.dma("sp", hb[s][:], h_src[bs, pn * NW:(pn + 1) * NW].rearrange("(i p) n -> p i n", p=128), hb_b[s],
                              writes=[hb_b[s]])
                        P.op("dve", lambda e, s=s: e.tensor_tensor(out=hb[s][:], in0=hb[s][:], in1=po[s][:], op=ALU.add),
                             reads=[hb_b[s], po_b[s]], writes=[hb_b[s]])
                        P.dma("act", h_dst[bs, pn * NW:(pn + 1) * NW].rearrange("(i p) n -> p i n", p=128), hb[s][:], hb_b[s],
                              reads=[hb_b[s]])
                P.barrier()

        def phase_final(h_src):
            with ExitStack() as st:
                hb = [st.enter_context(SBT(f"f_hb{i}", [128, D], F32)) for i in range(2)]
                hb_b = [P.buf("fhb0"), P.buf("fhb1")]
                ho = [st.enter_context(SBT(f"f_ho{i}", [128, D], F32)) for i in range(2)]
                ho_b = [P.buf("fho0"), P.buf("fho1")]
                junk = st.enter_context(SBT("f_junk", [128, D], BF16))
                junk_b = P.buf("fjunk")
                ss = st.enter_context(SBT("f_ss", [128, 2], F32))
                ss_b = P.buf("fss")
                gfull = st.enter_context(SBT("f_g", [128, D], F32))
                P.dma("sp", gfull[:], w["final_norm"].partition_broadcast(128), b_const, writes=[b_const])
                for t in range(NT):
                    s = t % 2
                    rs = slice(t * 128, (t + 1) * 128)
                    P.dma("sp", hb[s][:], h_src[rs, :], hb_b[s], writes=[hb_b[s]])
                    P.op("act", lambda e, s=s: e.activation(out=junk[:], in_=hb[s][:], func=AF.Square, accum_out=ss[:, s:s + 1]),
                         reads=[hb_b[s]], writes=[junk_b, ss_b])
                    P.op("dve", lambda e, s=s: e.tensor_scalar(out=ss[:, s:s + 1], in0=ss[:, s:s + 1], scalar1=1.0 / D, scalar2=NORM_EPS,
                                                            op0=ALU.mult, op1=ALU.add), reads=[ss_b], writes=[ss_b])
                    P.op("act", lambda e, s=s: e.activation(out=ss[:, s:s + 1], in_=ss[:, s:s + 1], func=AF.Sqrt), reads=[ss_b], writes=[ss_b])
                    P.op("dve", lambda e, s=s: e.reciprocal(out=ss[:, s:s + 1], in_=ss[:, s:s + 1]), reads=[ss_b], writes=[ss_b])
                    P.op("dve", lambda e, s=s: e.scalar_tensor_tensor(out=ho[s][:], in0=hb[s][:], scalar=ss[:, s:s + 1], in1=gfull[:],
                                                                   op0=ALU.mult, op1=ALU.mult),
                         reads=[hb_b[s], ss_b, b_const], writes=[ho_b[s]])
                    P.dma("act", y_out[rs, :], ho[s][:], ho_b[s], reads=[ho_b[s]])
                P.barrier()

        phase_cast()
        hbufs = [hA, hB]
        cur = x_in
        for i in range(c.depth):
            j = i // 2
            nxt = hbufs[i % 2]
            if i % 2 == 0:
                phase_s5a(j, cur)
                phase_s5b(j)
                phase_s5c(j, cur, nxt)
            else:
                stop = getattr(cfg, "stop", "")
                phase_at_a(j, cur)
                if stop != "at_a":
                    phase_at_b(j, i)
                    if stop != "at_b":
                        phase_at_c(j, cur, nxt)
            cur = nxt
        phase_final(cur)
        build_program.n_inst = P.n_inst
        for k, v in P.maxwait.items():
            assert v <= P.total.get(k, 0), ("unreachable wait", v, P.total.get(k, 0))
    return nc


def host_constants(L):
    ident = np.eye(128, dtype=np.float32)
    jj = np.zeros((128, 128), np.float32)
    for k in range(64):
        jj[k, k + 64] = 1.0
        jj[k + 64, k] = 1.0
    rot = jj.copy()
    pos = np.arange(L, dtype=np.float32)
    inv_freq = (1.0 / (np.float32(ROPE_THETA) ** (np.arange(0, 128, 2, dtype=np.float32) / np.float32(128)))).astype(np.float32)
    ang = pos[None, :] * inv_freq[:, None]
    cos = np.cos(ang).astype(np.float32)
    sin = np.sin(ang).astype(np.float32)
    ropec = np.concatenate([cos, cos], axis=0)
    ropes = np.concatenate([-sin, sin], axis=0)
    return dict(cident=ident, cjj=jj, crot=rot, ropec=np.ascontiguousarray(ropec), ropes=np.ascontiguousarray(ropes))


def run_cfg(cfg, seqs, weights, n_cores):
    L, D = cfg.L, cfg.D
    nc = build_program(cfg)
    consts = host_constants(L)
    in_maps = []
    for ci in range(n_cores):
        xs = seqs[ci % len(seqs)]
        Li = xs.shape[0]
        x = np.zeros((L, D), np.float32)
        x[:Li] = xs
        m = np.zeros((L,), np.float32)
        m[:Li] = 1.0
        kb = np.where(m > 0, 0.0, -30000.0).astype(np.float32)
        d = {"x": x,
             "tokmask": np.ascontiguousarray(m.reshape(L // 128, 128).T),
             "keybias": np.ascontiguousarray(kb.reshape(L // 128, 128).T)}
        d.update(consts)
        for k, v in weights.items():
            d[k] = np.ascontiguousarray(v, dtype=np.float32)
        in_maps.append(d)
    res = run_bass_kernel_spmd(nc, in_maps, core_ids=list(range(n_cores)))
    run_cfg.last = res
    outs = []
    for ci in range(len(seqs)):
        outs.append(np.asarray(res.results[ci]["y"])[:seqs[ci].shape[0]])
    return outs


def kernel(x_prompt, x_sample, **weights):
    x_prompt = np.asarray(x_prompt, dtype=np.float32)
    x_sample = np.asarray(x_sample, dtype=np.float32)
    D = x_prompt.shape[-1]
    L = max(x_prompt.shape[1], x_sample.shape[1])
    cfg = Cfg(L=L, D=D, H=D // 256)
    seqs = [x_sample[i] for i in range(x_sample.shape[0])] + [x_prompt[i] for i in range(x_prompt.shape[0])]
    assert len(seqs) <= 8
    outs = run_cfg(cfg, seqs, weights, 8)
    ns = x_sample.shape[0]
    y_sample = np.stack(outs[:ns], axis=0)
    y_prompt = np.stack(outs[ns:], axis=0)
    return (y_prompt.astype(np.float32), y_sample.astype(np.float32))
```

```python
import math
from contextlib import ExitStack

import numpy as np
import concourse.bass as bass
import concourse.mybir as mybir
from concourse.bass_utils import run_bass_kernel_spmd

F32 = mybir.dt.float32
BF16 = mybir.dt.bfloat16
AF = mybir.ActivationFunctionType
ALU = mybir.AluOpType
AX = mybir.AxisListType

NORM_EPS = 1e-6
SUBLN_EPS = 1e-5
ROPE_THETA = 10000.0
S5_LAMBDA_RE_MAX = -1e-4
SEM_ROT = 20000


class Buf:
    def __init__(self, name):
        self.name = name
        self.w = None
        self.r = {}
        self.dma_sem = None
        self.dma_cnt = 0


class Prog:
    def __init__(self, nc, stack):
        self.nc = nc
        self.stack = stack
        self.eng = {"pe": nc.tensor, "act": nc.scalar, "dve": nc.vector, "pool": nc.gpsimd, "sp": nc.sync}
        self.sems = {}
        self.cnt = {}
        self.waited = {}
        self.allsems = []
        for e in self.eng:
            self.sems[e] = [self._newsem(f"s_{e}_0")]
            self.cnt[e] = 0
        self.bufs = []
        self.n_inst = 0
        self.maxwait = {}
        self.total = {}
        self.sem_pool = []

    def _newsem(self, name):
        return self.stack.enter_context(self.nc.semaphore(name))

    def buf(self, name):
        b = Buf(name)
        b.psum = name.startswith(("tp", "ps", "cps", "cpo", "aps", "apr", "sps", "acc", "actp", "acpo"))
        self.bufs.append(b)
        return b

    def _wait(self, e, sem, val):
        key = (e, id(sem))
        if self.waited.get(key, 0) >= val:
            return
        self.waited[key] = val
        self.maxwait[id(sem)] = max(self.maxwait.get(id(sem), 0), val)
        self.eng[e].wait_ge(sem, val)
        self.n_inst += 1

    def _wait_ticket(self, e, t):
        if t is None:
            return
        kind, sem, val, src = t
        if kind == "eng" and src == e and e == "pe":
            return
        self._wait(e, sem, val)

    def _deps(self, e, reads, writes, is_dma=False):
        for b in reads:
            self._wait_ticket(e, b.w)
            if getattr(b, "psum", False):
                for t in b.r.values():
                    if not (t[0] == "eng" and t[3] == e):
                        self._wait_ticket(e, t)
        for b in writes:
            if not (is_dma and b.w is not None and b.w[0] == "dma"):
                self._wait_ticket(e, b.w)
            for t in b.r.values():
                self._wait_ticket(e, t)

    def _record(self, t, reads, writes):
        for b in reads:
            b.r[id(t[1])] = t
        for b in writes:
            b.w = t
            b.r = {}

    def sync_on(self, e, b):
        self._wait_ticket(e, b.w)
        for t in b.r.values():
            self._wait_ticket(e, t)

    def op(self, e, fn, reads=(), writes=(), marks=()):
        self._deps(e, reads, writes)
        if self.cnt[e] >= SEM_ROT:
            self.sems[e].append(self._newsem(f"s_{e}_{len(self.sems[e])}"))
            self.cnt[e] = 0
        sem = self.sems[e][-1]
        ins = fn(self.eng[e])
        ins.then_inc(sem, 1)
        self.cnt[e] += 1
        self.total[id(sem)] = self.cnt[e]
        self.n_inst += 1
        t = ("eng", sem, self.cnt[e], e)
        self._record(t, reads, writes)
        for b in marks:
            b.w = t
        return t

    def dma(self, q, out, in_, sb, reads=(), writes=(), **kw):
        self._deps(q, reads, writes, is_dma=True)
        if sb.dma_sem is None:
            if self.sem_pool:
                sb.dma_sem, sb.dma_cnt = self.sem_pool.pop()
            else:
                sb.dma_sem = self._newsem(f"d_{len(self.allsems)}")
                sb.dma_cnt = 0
                self.allsems.append(sb.dma_sem)
        ins = self.eng[q].dma_start(out=out, in_=in_, **kw)
        ins.then_inc(sb.dma_sem, 16)
        sb.dma_cnt += 16
        self.total[id(sb.dma_sem)] = sb.dma_cnt
        self.n_inst += 1
        t = ("dma", sb.dma_sem, sb.dma_cnt, q)
        self._record(t, reads, writes)
        return t

    def barrier(self):
        for e in self.eng:
            for e2 in self.eng:
                if e2 != e and self.cnt[e2] > 0:
                    self._wait(e, self.sems[e2][-1], self.cnt[e2])
            for b in self.bufs:
                if b.dma_sem is not None and b.dma_cnt > 0:
                    self._wait(e, b.dma_sem, b.dma_cnt)
        for b in self.bufs:
            b.w = None
            b.r = {}
            if b.dma_sem is not None:
                self.sem_pool.append((b.dma_sem, b.dma_cnt))
                b.dma_sem = None
                b.dma_cnt = 0
        self.bufs = [b for b in self.bufs if b.name == "const"]


def _ceil_div(a, b):
    return (a + b - 1) // b


class Cfg:
    def __init__(self, L, D, H, depth=4):
        self.L = L
        self.D = D
        self.E = 2 * D
        self.G = self.E // 16
        self.AW = D
        self.H = H
        assert H * 256 == D
        self.depth = depth
        self.NS5 = (depth + 1) // 2
        self.NAT = max(depth // 2, 1)
        self.KT = D // 128
        self.ET = self.E // 128
        self.BT = 512
        self.NB = L // 512
        self.NT = L // 128
        self.NLEV = int(round(math.log2(L)))
        assert 2 ** self.NLEV == L


def lambda_init(i):
    return 0.8 - 0.6 * math.exp(-0.3 * i)


def build_program(cfg):
    c = cfg
    L, D, E, G, H, KT, ET, NB, NT, NLEV = c.L, c.D, c.E, c.G, c.H, c.KT, c.ET, c.NB, c.NT, c.NLEV
    NS5, NAT = c.NS5, c.NAT
    nc = bass.Bass("TRN2", target_bir_lowering=False)

    _uid = [0]

    def SBT(name, shape, dt):
        _uid[0] += 1
        return nc.sbuf_tensor(f"{name}_u{_uid[0]}", shape, dt)

    def PST(name, shape, dt):
        _uid[0] += 1
        return nc.psum_tensor(f"{name}_u{_uid[0]}", shape, dt)

    def din(name, shape, dt=F32):
        return nc.dram_tensor(name, list(shape), dt, kind="ExternalInput").ap()

    def dscr(name, shape, dt):
        kind = "ExternalOutput" if (getattr(cfg, "debug", False) and name in ("uT", "zsT", "yT", "hA", "hB", "qT", "kT", "vA", "zs", "oS")) else "Internal"
        return nc.dram_tensor(name, list(shape), dt, kind=kind).ap()

    x_in = din("x", [L, D])
    tokmask = din("tokmask", [128, NT])
    keybias = din("keybias", [128, NT])
    ropec = din("ropec", [128, L])
    ropes = din("ropes", [128, L])
    cident = din("cident", [128, 128])
    cjj = din("cjj", [128, 128])
    crot = din("crot", [128, 128])
    w = {}
    w["s5_norm"] = din("s5_norm", [NS5, D])
    w["s5_w_in"] = din("s5_w_in", [NS5, D, 2 * E])
    w["s5_a_re"] = din("s5_a_re", [NS5, 2, G, 64])
    w["s5_a_im"] = din("s5_a_im", [NS5, 2, G, 64])
    w["s5_log_dt"] = din("s5_log_dt", [NS5, 2, G])
    w["s5_b_re"] = din("s5_b_re", [NS5, 2, G, 64, 16])
    w["s5_b_im"] = din("s5_b_im", [NS5, 2, G, 64, 16])
    w["s5_c_re"] = din("s5_c_re", [NS5, 2, G, 16, 64])
    w["s5_c_im"] = din("s5_c_im", [NS5, 2, G, 16, 64])
    w["s5_d"] = din("s5_d", [NS5, E])
    w["s5_w_glu"] = din("s5_w_glu", [NS5, E, E])
    w["s5_b_glu"] = din("s5_b_glu", [NS5, E])
    w["s5_w_out"] = din("s5_w_out", [NS5, E, D])
    w["attn_norm"] = din("attn_norm", [NAT, D])
    w["attn_w_in"] = din("attn_w_in", [NAT, D, 4 * D])
    for nm in ("attn_lambda_q1", "attn_lambda_k1", "attn_lambda_q2", "attn_lambda_k2"):
        w[nm] = din(nm, [NAT, 128])
    w["attn_subln"] = din("attn_subln", [NAT, 256])
    w["attn_w_out"] = din("attn_w_out", [NAT, D, D])
    w["final_norm"] = din("final_norm", [D])
    y_out = nc.dram_tensor("y", [L, D], F32, kind="ExternalOutput").ap()

    hA = dscr("hA", [L, D], F32)
    hB = dscr("hB", [L, D], F32)
    wb = {
        "s5_w_in": dscr("wb_s5_w_in", [NS5, D, 2 * E], BF16),
        "s5_w_glu": dscr("wb_s5_w_glu", [NS5, E, E], BF16),
        "s5_w_out": dscr("wb_s5_w_out", [NS5, E, D], BF16),
        "attn_w_in": dscr("wb_attn_w_in", [NAT, D, 4 * D], BF16),
        "attn_w_out": dscr("wb_attn_w_out", [NAT, D, D], BF16),
    }
    uT = dscr("uT", [E, L], BF16)
    zsT = dscr("zsT", [E, L], BF16)
    yT = dscr("yT", [E, L], BF16)
    qT = dscr("qT", [2 * H, 128, L], BF16)
    kT = dscr("kT", [2 * H, 128, L], BF16)
    vA = dscr("vA", [H, L, 256], BF16)
    zs = dscr("zs", [L, D], BF16)
    oS = dscr("oS", [L, D], BF16)

    stack = ExitStack()
    with stack:
        P = Prog(nc, stack)

        def sb(name, shape, dt):
            return stack.enter_context(SBT(name, list(shape), dt))

        ident_f = sb("ident_f", [128, 128], F32)
        ident = sb("ident", [128, 128], BF16)
        jj_f = sb("jj_f", [128, 128], F32)
        rot_f = sb("rot_f", [128, 128], F32)
        rotb = sb("rotb", [128, 128], BF16)
        tmask = sb("tmask", [128, NT], F32)
        kbias = sb("kbias", [128, NT], F32)
        b_const = P.buf("const")
        P.dma("sp", ident_f[:], cident[:, :], b_const, writes=[b_const])
        P.dma("sp", jj_f[:], cjj[:, :], b_const, writes=[b_const])
        P.dma("sp", rot_f[:], crot[:, :], b_const, writes=[b_const])
        P.dma("sp", tmask[:], tokmask[:, :], b_const, writes=[b_const])
        P.dma("sp", kbias[:], keybias[:, :], b_const, writes=[b_const])
        P.op("dve", lambda e: e.tensor_copy(out=ident[:], in_=ident_f[:]), reads=[b_const], writes=[b_const])
        P.op("dve", lambda e: e.tensor_copy(out=rotb[:], in_=rot_f[:]), reads=[b_const], writes=[b_const])

        def phase_cast():
            with ExitStack() as st:
                CW = 2048
                stg_f = [st.enter_context(SBT(f"cw_f{i}", [128, CW], F32)) for i in range(2)]
                stg_b = [st.enter_context(SBT(f"cw_b{i}", [128, CW], BF16)) for i in range(2)]
                bf = [P.buf(f"cw_f{i}") for i in range(2)]
                bb = [P.buf(f"cw_b{i}") for i in range(2)]
                it = 0
                for nm, dst in wb.items():
                    src = w[nm]
                    nl, R, C = src.shape
                    for l in range(nl):
                        for r0 in range(0, R, 128):
                            for c0 in range(0, C, CW):
                                cw = min(CW, C - c0)
                                s = it % 2
                                P.dma("sp", stg_f[s][:, :cw], src[l, r0:r0 + 128, c0:c0 + cw], bf[s], writes=[bf[s]])
                                eng = "dve" if it % 2 == 0 else "pool"
                                P.op(eng, lambda e, s=s, cw=cw: e.tensor_copy(out=stg_b[s][:, :cw], in_=stg_f[s][:, :cw]),
                                     reads=[bf[s]], writes=[bb[s]])
                                P.dma("act", dst[l, r0:r0 + 128, c0:c0 + cw], stg_b[s][:, :cw], bb[s], reads=[bb[s]])
                                it += 1
                P.barrier()

        def norm_block(st_bufs, h_src, b, gain_sb):
            (hb, hb_b, hn, hn_b, ss, ss_b, junk, junk_b, hnT, hnT_b, tp_ps, tp_b) = st_bufs
            P.dma("sp", hb[:], h_src[b * 512:(b + 1) * 512, :].rearrange("(i p) d -> p i d", p=128), hb_b, writes=[hb_b])
            for i in range(4):
                P.op("act", lambda e, i=i: e.activation(out=junk[:], in_=hb[:, i, :], func=AF.Square,
                                                         accum_out=ss[:, i:i + 1]),
                     reads=[hb_b], writes=[junk_b, ss_b])
            P.op("dve", lambda e: e.tensor_scalar(out=ss[:, 0:4], in0=ss[:, 0:4], scalar1=1.0 / D, scalar2=NORM_EPS,
                                                  op0=ALU.mult, op1=ALU.add), reads=[ss_b], writes=[ss_b])
            P.op("act", lambda e: e.activation(out=ss[:, 0:4], in_=ss[:, 0:4], func=AF.Sqrt), reads=[ss_b], writes=[ss_b])
            P.op("dve", lambda e: e.reciprocal(out=ss[:, 0:4], in_=ss[:, 0:4]), reads=[ss_b], writes=[ss_b])
            P.op("dve", lambda e: e.tensor_tensor(out=ss[:, 0:4], in0=ss[:, 0:4], in1=tmask[:, b * 4:(b + 1) * 4],
                                                  op=ALU.mult), reads=[ss_b, b_const], writes=[ss_b])
            for i in range(4):
                P.op("act", lambda e, i=i: e.activation(out=hn[:, i, :], in_=hb[:, i, :], func=AF.Copy,
                                                         scale=ss[:, i:i + 1]),
                     reads=[hb_b, ss_b], writes=[hn_b])
            for kt in range(KT):
                s = kt % 2
                for i in range(4):
                    P.op("pe", lambda e, i=i, kt=kt, s=s: e.transpose(out=tp_ps[s][:, i * 128:(i + 1) * 128],
                                                                   in_=hn[:, i, kt * 128:(kt + 1) * 128],
                                                                   identity=ident[:]),
                         reads=[hn_b, b_const], writes=[tp_b[s]])
                eng = "dve" if kt % 2 == 0 else "pool"
                if eng == "pool":
                    eng = "dve"
                P.op(eng, lambda e, kt=kt, s=s: e.tensor_scalar(out=hnT[:, kt, :], in0=tp_ps[s][:, 0:512], scalar1=gain_sb[:, kt:kt + 1],
                                                           scalar2=None, op0=ALU.mult),
                     reads=[tp_b[s], b_const], writes=[hnT_b])

        def alloc_norm_bufs(st, tag):
            hb = st.enter_context(SBT(f"hb_{tag}", [128, 4, D], F32))
            hn = st.enter_context(SBT(f"hn_{tag}", [128, 4, D], BF16))
            ss = st.enter_context(SBT(f"ss_{tag}", [128, 4], F32))
            junk = st.enter_context(SBT(f"junk_{tag}", [128, D], BF16))
            hnT = st.enter_context(SBT(f"hnT_{tag}", [128, KT, 512], BF16))
            tp = [st.enter_context(PST(f"tp_{tag}{i}", [128, 1024], BF16)) for i in range(2)]
            return (hb, P.buf("hb"), hn, P.buf("hn"), ss, P.buf("ss"), junk, P.buf("junk"), hnT, P.buf("hnT"),
                    tp, [P.buf("tp0"), P.buf("tp1")])

        def load_gain(st, src_row, tag):
            g = st.enter_context(SBT(f"gain_{tag}", [128, KT], F32))
            P.dma("sp", g[:], src_row.rearrange("(kt p) -> p kt", p=128), b_const, writes=[b_const],
                  allow_slow_non_contiguous=True)
            return g

        def phase_s5a(j, h_src):
            with ExitStack() as st:
                nb = alloc_norm_bufs(st, "s5a")
                hnT, hnT_b = nb[8], nb[9]
                gain = load_gain(st, w["s5_norm"][j], "s5a")
                MW = 512
                wp = [st.enter_context(SBT(f"wp_s5a{i}", [128, KT, MW], BF16)) for i in range(2)]
                wp_b = [P.buf("wp0"), P.buf("wp1")]
                ps = [st.enter_context(PST(f"ps_s5a{i}", [128, 512], F32)) for i in range(2)]
                ps_b = [P.buf("ps0"), P.buf("ps1")]
                og = [st.enter_context(SBT(f"og_s5a{i}", [128, 512], BF16)) for i in range(4)]
                og_b = [P.buf(f"og{i}") for i in range(4)]
                wsrc = wb["s5_w_in"][j]
                it = 0
                oi = 0
                for b in range(NB):
                    norm_block(nb, h_src, b, gain)
                    for pn in range(2 * E // MW):
                        s = it % 2
                        it += 1
                        P.dma("pool", wp[s][:], wsrc[:, pn * MW:(pn + 1) * MW].rearrange("(kt p) m -> p kt m", p=128),
                              wp_b[s], writes=[wp_b[s]])
                        for mi in range(MW // 128):
                            m = pn * (MW // 128) + mi
                            q = m % 2
                            for kt in range(KT):
                                P.op("pe", lambda e, q=q, s=s, mi=mi, kt=kt: e.matmul(
                                    ps[q][:], lhsT=wp[s][:, kt, mi * 128:(mi + 1) * 128], rhs=hnT[:, kt, :],
                                    start=(kt == 0), stop=(kt == KT - 1)),
                                     reads=[wp_b[s], hnT_b], writes=[ps_b[q]])
                            o = oi % 4
                            oi += 1
                            if m < ET:
                                P.op("dve", lambda e, o=o, q=q: e.tensor_copy(out=og[o][:], in_=ps[q][:]),
                                     reads=[ps_b[q]], writes=[og_b[o]])
                                P.dma("sp", uT[m * 128:(m + 1) * 128, b * 512:(b + 1) * 512], og[o][:], og_b[o], reads=[og_b[o]])
                            else:
                                P.op("act", lambda e, o=o, q=q: e.activation(out=og[o][:], in_=ps[q][:], func=AF.Silu),
                                     reads=[ps_b[q]], writes=[og_b[o]])
                                P.dma("sp", zsT[(m - ET) * 128:(m - ET + 1) * 128, b * 512:(b + 1) * 512], og[o][:], og_b[o],
                                      reads=[og_b[o]])
                P.barrier()

        def phase_s5b(j):
            with ExitStack() as st:
                G2 = 2 * G
                are = st.enter_context(SBT("t_are", [128, G2], F32))
                aim = st.enter_context(SBT("t_aim", [128, G2], F32))
                dt = st.enter_context(SBT("t_dt", [128, G2], F32))
                t0 = st.enter_context(SBT("t_t0", [128, G2], F32))
                t1 = st.enter_context(SBT("t_t1", [128, G2], F32))
                t2 = st.enter_context(SBT("t_t2", [128, G2], F32))
                t3 = st.enter_context(SBT("t_t3", [128, G2], F32))
                fr = st.enter_context(SBT("t_fr", [128, G2], F32))
                fi = st.enter_context(SBT("t_fi", [128, G2], F32))
                PA = st.enter_context(SBT("t_PA", [128, NLEV, G2], F32))
                PC = st.enter_context(SBT("t_PC", [128, NLEV, G2], F32))
                sgn = st.enter_context(SBT("t_sgn", [128, 1], F32))
                dsk = st.enter_context(SBT("t_dsk", [16, G], F32))
                tb = P.buf("tables")
                for half in range(2):
                    P.dma("sp", are[half * 64:(half + 1) * 64, :], w["s5_a_re"][j].rearrange("d g p -> p (d g)"), tb,
                          writes=[tb], allow_slow_non_contiguous=True)
                    P.dma("sp", aim[half * 64:(half + 1) * 64, :], w["s5_a_im"][j].rearrange("d g p -> p (d g)"), tb,
                          writes=[tb], allow_slow_non_contiguous=True)
                P.dma("sp", dt[:], w["s5_log_dt"][j].rearrange("d g -> (d g)").partition_broadcast(128), tb, writes=[tb])
                P.dma("sp", dsk[:], w["s5_d"][j].rearrange("(g i) -> i g", i=16), tb, writes=[tb],
                      allow_slow_non_contiguous=True)
                V = lambda fn: P.op("dve", fn, reads=[tb], writes=[tb])
                A_ = lambda fn: P.op("act", fn, reads=[tb], writes=[tb])
                V(lambda e: e.memset(sgn[0:64, :], 1.0))
                V(lambda e: e.memset(sgn[64:128, :], -1.0))
                V(lambda e: e.tensor_scalar(out=are[:], in0=are[:], scalar1=S5_LAMBDA_RE_MAX, scalar2=None, op0=ALU.min))
                A_(lambda e: e.activation(out=dt[:], in_=dt[:], func=AF.Exp))
                V(lambda e: e.tensor_tensor(out=t0[:], in0=are[:], in1=dt[:], op=ALU.mult))
                A_(lambda e: e.activation(out=t0[:], in_=t0[:], func=AF.Exp))
                V(lambda e: e.tensor_tensor(out=t1[:], in0=aim[:], in1=dt[:], op=ALU.mult))
                TWO_PI = 2.0 * math.pi
                def sin_of(dst, shift):
                    V(lambda e: e.tensor_scalar(out=dst[:], in0=t1[:], scalar1=shift, scalar2=None, op0=ALU.add))
                    for _ in range(5):
                        V(lambda e: e.tensor_scalar(out=fr[:], in0=dst[:], scalar1=TWO_PI, scalar2=TWO_PI, op0=ALU.is_ge, op1=ALU.mult))
                        V(lambda e: e.tensor_tensor(out=dst[:], in0=dst[:], in1=fr[:], op=ALU.subtract))
                    V(lambda e: e.tensor_scalar(out=dst[:], in0=dst[:], scalar1=-math.pi, scalar2=None, op0=ALU.add))
                    V(lambda e: e.tensor_scalar(out=dst[:], in0=dst[:], scalar1=math.pi, scalar2=-math.pi, op0=ALU.min, op1=ALU.max))
                    A_(lambda e: e.activation(out=dst[:], in_=dst[:], func=AF.Sin))
                sin_of(t2, math.pi)
                sin_of(t3, 1.5 * math.pi)
                V(lambda e: e.tensor_tensor(out=PA[:, 0, :], in0=t0[:], in1=t3[:], op=ALU.mult))
                V(lambda e: e.tensor_tensor(out=t2[:], in0=t0[:], in1=t2[:], op=ALU.mult))
                V(lambda e: e.tensor_tensor(out=t0[:], in0=are[:], in1=are[:], op=ALU.mult))
                V(lambda e: e.tensor_tensor(out=t1[:], in0=aim[:], in1=aim[:], op=ALU.mult))
                V(lambda e: e.tensor_tensor(out=t0[:], in0=t0[:], in1=t1[:], op=ALU.add))
                V(lambda e: e.reciprocal(out=t0[:], in_=t0[:]))
                V(lambda e: e.tensor_scalar(out=t3[:], in0=PA[:, 0, :], scalar1=-1.0, scalar2=None, op0=ALU.add))
                V(lambda e: e.tensor_tensor(out=fr[:], in0=t3[:], in1=are[:], op=ALU.mult))
                V(lambda e: e.tensor_tensor(out=t1[:], in0=t2[:], in1=aim[:], op=ALU.mult))
                V(lambda e: e.tensor_tensor(out=fr[:], in0=fr[:], in1=t1[:], op=ALU.add))
                V(lambda e: e.tensor_tensor(out=fr[:], in0=fr[:], in1=t0[:], op=ALU.mult))
                V(lambda e: e.tensor_tensor(out=fi[:], in0=t2[:], in1=are[:], op=ALU.mult))
                V(lambda e: e.tensor_tensor(out=t1[:], in0=t3[:], in1=aim[:], op=ALU.mult))
                V(lambda e: e.tensor_tensor(out=fi[:], in0=fi[:], in1=t1[:], op=ALU.subtract))
                V(lambda e: e.tensor_tensor(out=fi[:], in0=fi[:], in1=t0[:], op=ALU.mult))
                V(lambda e: e.tensor_scalar(out=fi[:], in0=fi[:], scalar1=sgn[:, 0:1], scalar2=-1.0, op0=ALU.mult, op1=ALU.mult))
                V(lambda e: e.tensor_scalar(out=PC[:, 0, :], in0=t2[:], scalar1=sgn[:, 0:1], scalar2=None, op0=ALU.mult))
                for k in range(1, NLEV):
                    V(lambda e, k=k: e.tensor_tensor(out=t0[:], in0=PA[:, k - 1, :], in1=PA[:, k - 1, :], op=ALU.mult))
                    V(lambda e, k=k: e.tensor_tensor(out=t1[:], in0=PC[:, k - 1, :], in1=PC[:, k - 1, :], op=ALU.mult))
                    V(lambda e, k=k: e.tensor_tensor(out=PC[:, k, :], in0=PA[:, k - 1, :], in1=PC[:, k - 1, :], op=ALU.mult))
                    V(lambda e, k=k: e.tensor_scalar(out=PC[:, k, :], in0=PC[:, k, :], scalar1=2.0, scalar2=None, op0=ALU.mult))
                    V(lambda e, k=k: e.tensor_tensor(out=PA[:, k, :], in0=t0[:], in1=t1[:], op=ALU.subtract))

                NCH = 2
                ug = [st.enter_context(SBT(f"ug{i}", [16, L], BF16)) for i in range(2)]
                ug_b = [P.buf(f"ug{i}") for i in range(2)]
                X = [st.enter_context(SBT(f"X{i}", [128, L], BF16)) for i in range(2)]
                X_b = [P.buf(f"X{i}") for i in range(2)]
                bt1 = [st.enter_context(SBT(f"bt1_{i}", [128, 16], F32)) for i in range(2)]
                bt2 = [st.enter_context(SBT(f"bt2_{i}", [128, 16], F32)) for i in range(2)]
                bt_b = [P.buf(f"bt{i}") for i in range(2)]
                bbar = [st.enter_context(SBT(f"bbar{i}", [128, 16], BF16)) for i in range(2)]
                bbar_b = [P.buf(f"bbar{i}") for i in range(2)]
                Bl = [st.enter_context(SBT(f"Bl{i}", [16, 128], BF16)) for i in range(2)]
                Bl_b = [P.buf(f"Bl{i}") for i in range(2)]
                ct = [st.enter_context(SBT(f"ct{i}", [16, 128], F32)) for i in range(2)]
                ct_b = [P.buf(f"ct{i}") for i in range(2)]
                ctb = [st.enter_context(SBT(f"ctb{i}", [16, 128], BF16)) for i in range(2)]
                ctb_b = [P.buf(f"ctb{i}") for i in range(2)]
                Cl = [st.enter_context(SBT(f"Cl{i}", [128, 16], BF16)) for i in range(2)]
                Cl_b = [P.buf(f"Cl{i}") for i in range(2)]
                AM = [st.enter_context(SBT(f"AM{i}", [128, NLEV, 128], BF16)) for i in range(2)]
                AM_b = [P.buf(f"AM{i}") for i in range(2)]
                amt = [st.enter_context(SBT(f"amt{i}", [128, 128], F32)) for i in range(2)]
                amt_b = [P.buf(f"amt{i}") for i in range(2)]
                tps = st.enter_context(PST("tps_s5b", [128, 1024], BF16))
                tps_b = P.buf("tps")
                ps = [st.enter_context(PST(f"ps_s5b{i}", [128, 512], F32)) for i in range(4)]
                ps_b = [P.buf(f"psb{i}") for i in range(4)]
                pso = [st.enter_context(PST(f"pso_s5b{i}", [128, 512], F32)) for i in range(2)]
                pso_b = [P.buf(f"pso{i}") for i in range(2)]

                def cols(idx_list):
                    if len(idx_list) == 1:
                        return slice(idx_list[0], idx_list[0] + 1)
                    stp = idx_list[1] - idx_list[0]
                    return slice(idx_list[0], idx_list[-1] + 1, stp)

                dg = [st.enter_context(SBT(f"dg{i}", [16, 16], BF16)) for i in range(2)]
                dg_b = [P.buf(f"dg{i}") for i in range(2)]
                amt3 = [st.enter_context(SBT(f"amt3_{i}", [128, NLEV, 128], F32)) for i in range(2)]
                am3f = [st.enter_context(SBT(f"am3f_{i}", [128, NLEV, 128], F32)) for i in range(2)]
                am3_b = [P.buf(f"am3_{i}") for i in range(2)]
                ugk = [[P.buf(f"ugk{i}_{blk}") for blk in range(NB)] for i in range(2)]
                jjb = jj_f[:].unsqueeze(1).to_broadcast([128, NLEV, 128])
                idb = ident_f[:].unsqueeze(1).to_broadcast([128, NLEV, 128])
                steps = []
                for k in range(NLEV):
                    s_ = 2 ** (k + 1)
                    steps.append((k, list(range(s_ - 1, L, s_))))
                for k in range(NLEV - 2, -1, -1):
                    s_ = 2 ** (k + 1)
                    steps.append((k, list(range(s_ - 1 + 2 ** k, L, s_))))
                qi = 0
                for g in range(G):
                    gs = g % 2
                    P.dma("sp", ug[gs][:], uT[g * 16:(g + 1) * 16, :], ug_b[gs], writes=[ug_b[gs]] + ugk[gs])
                    P.op("pool", lambda e, gs=gs, g=g: e.tensor_scalar(out=dg[gs][:], in0=ident_f[0:16, 0:16], scalar1=dsk[:, g:g + 1],
                                                                     scalar2=None, op0=ALU.mult),
                         reads=[tb, b_const], writes=[dg_b[gs]])
                    Xtok = [None, None]
                    for d in range(2):
                        col = d * G + g
                        P.dma("pool", bt1[d][0:64, :], w["s5_b_re"][j, d, g], bt_b[d], writes=[bt_b[d]])
                        P.dma("pool", bt1[d][64:128, :], w["s5_b_im"][j, d, g], bt_b[d], writes=[bt_b[d]])
                        P.dma("pool", bt2[d][0:64, :], w["s5_b_im"][j, d, g], bt_b[d], writes=[bt_b[d]])
                        P.dma("pool", bt2[d][64:128, :], w["s5_b_re"][j, d, g], bt_b[d], writes=[bt_b[d]])
                        P.op("pool", lambda e, d=d, col=col: e.tensor_scalar(out=bt1[d][:], in0=bt1[d][:], scalar1=fr[:, col:col + 1],
                                                                           scalar2=None, op0=ALU.mult),
                             reads=[bt_b[d], tb], writes=[bt_b[d]])
                        P.op("dve", lambda e, d=d, col=col: e.scalar_tensor_tensor(out=bbar[d][:], in0=bt2[d][:],
                                                                                  scalar=fi[:, col:col + 1], in1=bt1[d][:],
                                                                                  op0=ALU.mult, op1=ALU.add),
                             reads=[bt_b[d], tb], writes=[bbar_b[d]])
                        P.op("pe", lambda e, d=d: e.transpose(out=tps[0:16, 0:128], in_=bbar[d][:], identity=ident[:]),
                             reads=[bbar_b[d], b_const], writes=[tps_b])
                        P.op("act", lambda e, d=d: e.activation(out=Bl[d][:], in_=tps[0:16, 0:128], func=AF.Copy), reads=[tps_b], writes=[Bl_b[d]])
                        P.dma("pool", ct[d][:, 0:64], w["s5_c_re"][j, d, g], ct_b[d], writes=[ct_b[d]])
                        P.dma("pool", ct[d][:, 64:128], w["s5_c_im"][j, d, g], ct_b[d], writes=[ct_b[d]])
                        P.op("pool", lambda e, d=d: e.tensor_copy(out=ctb[d][:, 0:64], in_=ct[d][:, 0:64]),
                             reads=[ct_b[d]], writes=[ctb_b[d]])
                        P.op("pool", lambda e, d=d: e.tensor_scalar(out=ctb[d][:, 64:128], in0=ct[d][:, 64:128], scalar1=-1.0,
                                                                  scalar2=None, op0=ALU.mult),
                             reads=[ct_b[d]], writes=[ctb_b[d]])
                        P.op("pe", lambda e, d=d: e.transpose(out=tps[:, 0:16], in_=ctb[d][:], identity=ident[0:16, 0:16]),
                             reads=[ctb_b[d], b_const], writes=[tps_b])
                        P.op("act", lambda e, d=d: e.activation(out=Cl[d][:], in_=tps[:, 0:16], func=AF.Copy), reads=[tps_b], writes=[Cl_b[d]])
                        pab = PA[:, :, col:col + 1].to_broadcast([128, NLEV, 128])
                        pcb = PC[:, :, col:col + 1].to_broadcast([128, NLEV, 128])
                        P.op("pool", lambda e, d=d, pcb=pcb: e.tensor_tensor(out=amt3[d][:], in0=jjb, in1=pcb, op=ALU.mult),
                             reads=[tb, b_const], writes=[am3_b[d]])
                        P.op("pool", lambda e, d=d, pab=pab: e.tensor_tensor(out=am3f[d][:], in0=idb, in1=pab, op=ALU.mult),
                             reads=[tb, b_const], writes=[am3_b[d]])
                        P.op("pool", lambda e, d=d: e.tensor_tensor(out=AM[d][:], in0=am3f[d][:], in1=amt3[d][:], op=ALU.add),
                             reads=[am3_b[d]], writes=[AM_b[d]])
                        x0 = P.buf(f"Xl{d}_in")
                        P.sync_on("act", X_b[d])
                        for blk in range(NB):
                            q = qi % 4
                            qi += 1
                            P.op("pe", lambda e, d=d, q=q, blk=blk: e.matmul(ps[q][:], lhsT=Bl[d][:], rhs=ug[gs][:, blk * 512:(blk + 1) * 512],
                                                                         start=True, stop=True),
                                 reads=[Bl_b[d], ugk[gs][blk]], writes=[ps_b[q]])
                            P.op("act", lambda e, d=d, q=q, blk=blk: e.activation(out=X[d][:, blk * 512:(blk + 1) * 512], in_=ps[q][:],
                                                                               func=AF.Copy),
                                 reads=[ps_b[q]], marks=[x0])
                        Xtok[d] = x0
                    for (k, tgt) in steps:
                        hop = 2 ** k
                        ntok = [P.buf(f"Xl0_{k}"), P.buf(f"Xl1_{k}")]
                        for c0 in range(0, len(tgt), 512):
                            tg = tgt[c0:c0 + 512]
                            for d in range(2):
                                if d == 0:
                                    tcols = cols(tg)
                                    scols = cols([t - hop for t in tg])
                                else:
                                    tcols = cols(sorted(L - 1 - t for t in tg))
                                    scols = cols(sorted(L - 1 - (t - hop) for t in tg))
                                n = len(tg)
                                q = qi % 4
                                qi += 1
                                P.op("pe", lambda e, d=d, q=q, n=n, k=k, scols=scols: e.matmul(ps[q][:, 0:n], lhsT=AM[d][:, k, :],
                                                                                           rhs=X[d][:, scols], start=True, stop=True),
                                     reads=[Xtok[d], AM_b[d]], writes=[ps_b[q]])
                                P.op("dve", lambda e, d=d, q=q, n=n, tcols=tcols: e.tensor_tensor(out=X[d][:, tcols], in0=ps[q][:, 0:n],
                                                                                              in1=X[d][:, tcols], op=ALU.add),
                                     reads=[ps_b[q], Xtok[d]], marks=[ntok[d]])
                        Xtok = ntok
                    for blk in range(NB):
                        q = blk % 2
                        sl = slice(blk * 512, (blk + 1) * 512)
                        P.op("pe", lambda e, q=q, sl=sl: e.matmul(pso[q][0:16, :], lhsT=Cl[0][:], rhs=X[0][:, sl], start=True, stop=False),
                             reads=[Cl_b[0], Xtok[0]], writes=[pso_b[q]])
                        P.op("pe", lambda e, q=q, sl=sl: e.matmul(pso[q][0:16, :], lhsT=Cl[1][:], rhs=X[1][:, sl], start=False, stop=False),
                             reads=[Cl_b[1], Xtok[1]], writes=[pso_b[q]])
                        P.op("pe", lambda e, q=q, sl=sl, gs=gs: e.matmul(pso[q][0:16, :], lhsT=dg[gs][:], rhs=ug[gs][:, sl], start=False, stop=True),
                             reads=[dg_b[gs], ugk[gs][blk]], writes=[pso_b[q]])
                        P.op("act", lambda e, q=q, sl=sl, gs=gs: e.activation(out=ug[gs][:, sl], in_=pso[q][0:16, :], func=AF.Copy),
                             reads=[pso_b[q]], writes=[ugk[gs][blk]])
                    for d in range(2):
                        X_b[d].w = None
                        X_b[d].r = dict(Xtok[d].r)
                        if Xtok[d].w is not None:
                            X_b[d].r[("w", id(Xtok[d].w[1]))] = Xtok[d].w
                    P.dma("act", yT[g * 16:(g + 1) * 16, :], ug[gs][:], ug_b[gs], reads=[ug_b[gs]] + ugk[gs])
                P.barrier()

        def phase_s5c(j, h_src, h_dst):
            with ExitStack() as st:
                NW0 = min(256, D)
                yb = st.enter_context(SBT("c_yb", [128, ET, 512], BF16))
                gb = st.enter_context(SBT("c_gb", [128, ET, 512], BF16))
                yb_b, gb_b = P.buf("yb"), P.buf("gb")
                tq = st.enter_context(SBT("c_tq", [128, 2048], F32))
                tq_b = P.buf("tq")
                zt = [st.enter_context(SBT(f"c_zt{i}", [128, 512], BF16)) for i in range(2)]
                zt_b = [P.buf("zt0"), P.buf("zt1")]
                sg = [st.enter_context(SBT(f"c_sg{i}", [128, 512], BF16)) for i in range(2)]
                sg_b = [P.buf("sg0"), P.buf("sg1")]
                hb = [st.enter_context(SBT(f"c_hb{i}", [128, 4, NW0], F32)) for i in range(2)]
                hb_b = [P.buf("chb0"), P.buf("chb1")]
                MW = 256
                wg = [st.enter_context(SBT(f"c_wg{i}", [128, ET, MW], BF16)) for i in range(2)]
                wg_b = [P.buf("wg0"), P.buf("wg1")]
                NW = min(256, D)
                wo = [st.enter_context(SBT(f"c_wo{i}", [128, ET, NW], BF16)) for i in range(2)]
                wo_b = [P.buf("wo0"), P.buf("wo1")]
                bglu = st.enter_context(SBT("c_bglu", [128, ET], F32))
                ps = [st.enter_context(PST(f"c_ps{i}", [128, 512], F32)) for i in range(2)]
                ps_b = [P.buf("cps0"), P.buf("cps1")]
                po = [st.enter_context(PST(f"c_po{i}", [128, 4, NW], F32)) for i in range(2)]
                po_b = [P.buf("cpo0"), P.buf("cpo1")]
                P.dma("sp", bglu[:], w["s5_b_glu"][j].rearrange("(m p) -> p m", p=128), b_const, writes=[b_const],
                      allow_slow_non_contiguous=True)
                wgsrc = wb["s5_w_glu"][j]
                wosrc = wb["s5_w_out"][j]
                itg = 0
                ito = 0
                ih = 0
                for b in range(NB):
                    bs = slice(b * 512, (b + 1) * 512)
                    P.dma("sp", yb[:], yT[:, bs].rearrange("(m p) t -> p m t", p=128), yb_b, writes=[yb_b])
                    CH = 4
                    for c0 in range(0, ET, CH):
                        cn = min(CH, ET - c0)
                        ysl = yb[:, c0:c0 + cn, :]
                        tsl = tq[:, 0:cn * 512].rearrange("p (m t) -> p m t", t=512)
                        gsl = gb[:, c0:c0 + cn, :]
                        P.op("act", lambda e, ysl=ysl, tsl=tsl: e.activation(out=tsl, in_=ysl, func=AF.Square),
                             reads=[yb_b], writes=[tq_b])
                        P.op("dve", lambda e, tsl=tsl: e.tensor_scalar(out=tsl, in0=tsl, scalar1=0.044715, scalar2=1.0,
                                                                      op0=ALU.mult, op1=ALU.add), reads=[tq_b], writes=[tq_b])
                        P.op("dve", lambda e, tsl=tsl, ysl=ysl: e.tensor_tensor(out=tsl, in0=tsl, in1=ysl, op=ALU.mult),
                             reads=[tq_b, yb_b], writes=[tq_b])
                        P.op("act", lambda e, tsl=tsl: e.activation(out=tsl, in_=tsl, func=AF.Sigmoid, scale=1.5957691216057308),
                             reads=[tq_b], writes=[tq_b])
                        P.op("dve", lambda e, tsl=tsl, ysl=ysl, gsl=gsl: e.tensor_tensor(out=gsl, in0=tsl, in1=ysl, op=ALU.mult),
                             reads=[tq_b, yb_b], writes=[gb_b])
                    for pn in range(E // MW):
                        s = itg % 2
                        itg += 1
                        P.dma("pool", wg[s][:], wgsrc[:, pn * MW:(pn + 1) * MW].rearrange("(kt p) m -> p kt m", p=128),
                              wg_b[s], writes=[wg_b[s]])
                        for mi in range(MW // 128):
                            m = pn * (MW // 128) + mi
                            q = m % 2
                            P.dma("sp", zt[q][:], zsT[m * 128:(m + 1) * 128, bs], zt_b[q], writes=[zt_b[q]])
                            for kt in range(ET):
                                P.op("pe", lambda e, q=q, s=s, mi=mi, kt=kt: e.matmul(
                                    ps[q][:], lhsT=wg[s][:, kt, mi * 128:(mi + 1) * 128], rhs=gb[:, kt, :],
                                    start=(kt == 0), stop=(kt == ET - 1)),
                                     reads=[wg_b[s], gb_b], writes=[ps_b[q]])
                            P.op("act", lambda e, q=q, m=m: e.activation(out=sg[q][:], in_=ps[q][:], func=AF.Sigmoid,
                                                                         bias=bglu[:, m:m + 1]),
                                 reads=[ps_b[q], b_const], writes=[sg_b[q]])
                            P.op("dve", lambda e, q=q: e.tensor_tensor(out=sg[q][:], in0=sg[q][:], in1=zt[q][:], op=ALU.mult),
                                 reads=[sg_b[q], zt_b[q]], writes=[sg_b[q]])
                            P.op("dve", lambda e, q=q, m=m: e.tensor_tensor(out=yb[:, m, :], in0=yb[:, m, :], in1=sg[q][:], op=ALU.mult),
                                 reads=[sg_b[q], yb_b], writes=[yb_b])
                    for pn in range(D // NW):
                        s = ito % 2
                        ito += 1
                        P.dma("pool", wo[s][:], wosrc[:, pn * NW:(pn + 1) * NW].rearrange("(kt p) n -> p kt n", p=128),
                              wo_b[s], writes=[wo_b[s]])
                        q = pn % 2
                        for i in range(4):
                            for kt in range(ET):
                                P.op("pe", lambda e, q=q, s=s, i=i, kt=kt: e.matmul(
                                    po[q][:, i, :], lhsT=yb[:, kt, i * 128:(i + 1) * 128], rhs=wo[s][:, kt, :],
                                    start=(kt == 0), stop=(kt == ET - 1)),
                                     reads=[wo_b[s], yb_b], writes=[po_b[q]])
                        hs = ih % 2
                        ih += 1
                        hv = hb[hs][:]
                        P.dma("sp", hv, h_src[bs, pn * NW:(pn + 1) * NW].rearrange("(i p) n -> p i n", p=128), hb_b[hs],
                              writes=[hb_b[hs]])
                        P.op("dve", lambda e, hv=hv, q=q: e.tensor_tensor(out=hv, in0=hv, in1=po[q][:], op=ALU.add),
                             reads=[hb_b[hs], po_b[q]], writes=[hb_b[hs]])
                        P.dma("act", h_dst[bs, pn * NW:(pn + 1) * NW].rearrange("(i p) n -> p i n", p=128), hv, hb_b[hs],
                              reads=[hb_b[hs]])
                P.barrier()

        def phase_at_a(j, h_src):
            with ExitStack() as st:
                nb = alloc_norm_bufs(st, "ata")
                hnT, hnT_b = nb[8], nb[9]
                gain = load_gain(st, w["attn_norm"][j], "ata")
                MW = min(512, D)
                wp = [st.enter_context(SBT(f"a_wp{i}", [128, KT, MW], BF16)) for i in range(2)]
                wp_b = [P.buf("awp0"), P.buf("awp1")]
                ps = [st.enter_context(PST(f"a_ps{i}", [128, 512], F32)) for i in range(2)]
                ps_b = [P.buf("aps0"), P.buf("aps1")]
                pr = [st.enter_context(PST(f"a_pr{i}", [128, 512], F32)) for i in range(2)]
                pr_b = [P.buf("apr0"), P.buf("apr1")]
                qb = [st.enter_context(SBT(f"a_qb{i}", [128, 512], BF16)) for i in range(2)]
                qb_b = [P.buf("aqb0"), P.buf("aqb1")]
                qf = [st.enter_context(SBT(f"a_qf{i}", [128, 512], F32)) for i in range(2)]
                qf_b = [P.buf("aqf0"), P.buf("aqf1")]
                og = [st.enter_context(SBT(f"a_og{i}", [128, 512], BF16)) for i in range(4)]
                og_b = [P.buf(f"aog{i}") for i in range(4)]
                rc = st.enter_context(SBT("a_rc", [128, 512], F32))
                rs_ = st.enter_context(SBT("a_rs", [128, 512], F32))
                rc_b = P.buf("rc")
                wsrc = wb["attn_w_in"][j]
                it = 0
                oi = 0
                for b in range(NB):
                    bs = slice(b * 512, (b + 1) * 512)
                    norm_block(nb, h_src, b, gain)
                    P.dma("sp", rc[:], ropec[:, bs], rc_b, writes=[rc_b])
                    P.dma("sp", rs_[:], ropes[:, bs], rc_b, writes=[rc_b])
                    for pn in range(4 * D // MW):
                        s = it % 2
                        it += 1
                        P.dma("pool", wp[s][:], wsrc[:, pn * MW:(pn + 1) * MW].rearrange("(kt p) m -> p kt m", p=128),
                              wp_b[s], writes=[wp_b[s]])
                        if pn < 2 * D // MW:
                            for mi in range(MW // 128):
                                m = pn * (MW // 128) + mi
                                q = m % 2
                                for kt in range(KT):
                                    P.op("pe", lambda e, q=q, s=s, mi=mi, kt=kt: e.matmul(
                                        ps[q][:], lhsT=wp[s][:, kt, mi * 128:(mi + 1) * 128], rhs=hnT[:, kt, :],
                                        start=(kt == 0), stop=(kt == KT - 1)),
                                         reads=[wp_b[s], hnT_b], writes=[ps_b[q]])
                                P.op("act", lambda e, q=q: e.activation(out=qb[q][:], in_=ps[q][:], func=AF.Copy),
                                     reads=[ps_b[q]], writes=[qb_b[q]])
                                P.op("pe", lambda e, q=q: e.matmul(pr[q][:], lhsT=rotb[:], rhs=qb[q][:], start=True, stop=True),
                                     reads=[qb_b[q], b_const], writes=[pr_b[q]])
                                P.op("dve", lambda e, q=q: e.tensor_tensor(out=qf[q][:], in0=ps[q][:], in1=rc[:], op=ALU.mult),
                                     reads=[ps_b[q], rc_b], writes=[qf_b[q]])
                                o = oi % 4
                                oi += 1
                                P.op("dve", lambda e, q=q, o=o: e.tensor_tensor(out=og[o][:], in0=pr[q][:], in1=rs_[:], op=ALU.mult),
                                     reads=[pr_b[q], rc_b], writes=[og_b[o]])
                                P.op("pool", lambda e, q=q, o=o: e.tensor_tensor(out=og[o][:], in0=og[o][:], in1=qf[q][:], op=ALU.add),
                                     reads=[qf_b[q], og_b[o]], writes=[og_b[o]])
                                dst = qT[m] if m < 2 * H else kT[m - 2 * H]
                                P.dma("sp", dst[:, bs], og[o][:], og_b[o], reads=[og_b[o]])
                        else:
                            cc0 = pn * MW - 2 * D
                            for i in range(4):
                                q = i % 2
                                for kt in range(KT):
                                    P.op("pe", lambda e, q=q, s=s, i=i, kt=kt: e.matmul(
                                        ps[q][:, 0:MW], lhsT=hnT[:, kt, i * 128:(i + 1) * 128], rhs=wp[s][:, kt, :],
                                        start=(kt == 0), stop=(kt == KT - 1)),
                                         reads=[wp_b[s], hnT_b], writes=[ps_b[q]])
                                o = oi % 4
                                oi += 1
                                rs = slice(b * 512 + i * 128, b * 512 + (i + 1) * 128)
                                if cc0 < D:
                                    P.op("dve", lambda e, o=o, q=q: e.tensor_copy(out=og[o][:, 0:MW], in_=ps[q][:, 0:MW]),
                                         reads=[ps_b[q]], writes=[og_b[o]])
                                    for hh in range(MW // 256):
                                        hd = (cc0 + hh * 256) // 256
                                        P.dma("sp", vA[hd, rs, :], og[o][:, hh * 256:(hh + 1) * 256], og_b[o], reads=[og_b[o]])
                                else:
                                    P.op("act", lambda e, o=o, q=q: e.activation(out=og[o][:, 0:MW], in_=ps[q][:, 0:MW], func=AF.Silu),
                                         reads=[ps_b[q]], writes=[og_b[o]])
                                    P.dma("sp", zs[rs, cc0 - D:cc0 - D + MW], og[o][:, 0:MW], og_b[o], reads=[og_b[o]])
                P.barrier()

        def phase_at_b(j, layer_idx):
            li = lambda_init(layer_idx)
            with ExitStack() as st:
                QB = 256
                NQ = L // QB
                Kt = st.enter_context(SBT("b_Kt", [128, 2, L], BF16))
                Vt = st.enter_context(SBT("b_Vt", [128, NT, 258], BF16))
                Kt_b, Vt_b = P.buf("Kt"), P.buf("Vt")
                Qt = [st.enter_context(SBT(f"b_Qt{i}", [128, 2, QB], BF16)) for i in range(2)]
                Qt_b = [P.buf("Qt0"), P.buf("Qt1")]
                Pt = [st.enter_context(SBT(f"b_Pt{i}", [128, 2, 2 * QB], BF16)) for i in range(3)]
                Pt_b = [P.buf(f"Pt{i}") for i in range(3)]
                sps = [st.enter_context(PST(f"b_sps{i}", [128, 2, 2 * QB], F32)) for i in range(2)]
                sps_b = [P.buf("sps0"), P.buf("sps1")]
                acc = [st.enter_context(PST(f"b_acc{i}", [128, 512], F32)) for i in range(4)]
                acc_b = [P.buf(f"acc{i}") for i in range(4)]
                lam = st.enter_context(SBT("b_lam", [128, 8], F32))
                lqk = st.enter_context(SBT("b_lqk", [128, 4, 128], F32))
                subg = st.enter_context(SBT("b_subg", [128, 256], F32))
                lam_b = P.buf("lam")
                o1 = st.enter_context(SBT("b_o1", [128, 256], F32))
                o2 = st.enter_context(SBT("b_o2", [128, 256], F32))
                o_b = P.buf("o12")
                rcp = st.enter_context(SBT("b_rcp", [128, 4], F32))
                zt = [st.enter_context(SBT(f"b_zt{i}", [128, 256], BF16)) for i in range(2)]
                zt_b = [P.buf("bzt0"), P.buf("bzt1")]
                oo = [st.enter_context(SBT(f"b_oo{i}", [128, 256], BF16)) for i in range(2)]
                oo_b = [P.buf("boo0"), P.buf("boo1")]
                junk = st.enter_context(SBT("b_junk", [128, 256], F32))
                for ii, nm in enumerate(("attn_lambda_q1", "attn_lambda_k1", "attn_lambda_q2", "attn_lambda_k2")):
                    P.dma("sp", lqk[:, ii, :], w[nm][j].partition_broadcast(128), lam_b, writes=[lam_b])
                P.dma("sp", subg[:], w["attn_subln"][j].partition_broadcast(128), lam_b, writes=[lam_b])
                for ii in range(2):
                    P.op("dve", lambda e, ii=ii: e.tensor_tensor(out=lqk[:, 2 * ii, :], in0=lqk[:, 2 * ii, :], in1=lqk[:, 2 * ii + 1, :],
                                                              op=ALU.mult), reads=[lam_b], writes=[lam_b])
                    P.op("dve", lambda e, ii=ii: e.reduce_sum(out=lam[:, ii:ii + 1], in_=lqk[:, 2 * ii, :], axis=AX.X),
                         reads=[lam_b], writes=[lam_b])
                P.op("act", lambda e: e.activation(out=lam[:, 0:2], in_=lam[:, 0:2], func=AF.Exp), reads=[lam_b], writes=[lam_b])
                P.op("dve", lambda e: e.tensor_tensor(out=lam[:, 2:3], in0=lam[:, 0:1], in1=lam[:, 1:2], op=ALU.subtract),
                     reads=[lam_b], writes=[lam_b])
                P.op("dve", lambda e: e.tensor_scalar(out=lam[:, 3:4], in0=lam[:, 2:3], scalar1=li, scalar2=-1.0,
                                                      op0=ALU.add, op1=ALU.mult), reads=[lam_b], writes=[lam_b])
                P.op("dve", lambda e: e.tensor_scalar(out=subg[:], in0=subg[:], scalar1=1.0 - li, scalar2=None, op0=ALU.mult),
                     reads=[lam_b], writes=[lam_b])
                scale = 1.0 / math.sqrt(128.0)
                pi = 0
                for hd in range(H):
                    for s2 in range(2):
                        P.dma("sp", Kt[:, s2, :], kT[2 * hd + s2], Kt_b, writes=[Kt_b])
                    P.dma("pool", Vt[:, :, 0:256], vA[hd].rearrange("(t p) c -> p t c", p=128), Vt_b, writes=[Vt_b])
                    P.op("dve", lambda e: e.tensor_copy(out=Vt[:, :, 256:257], in_=tmask[:, :].unsqueeze(2)), reads=[b_const], writes=[Vt_b])
                    for qb in range(NQ):
                        qs = slice(qb * QB, (qb + 1) * QB)
                        qq = qb % 2
                        for s2 in range(2):
                            P.dma("sp", Qt[qq][:, s2, :], qT[2 * hd + s2][:, qs], Qt_b[qq], writes=[Qt_b[qq]])
                        for kp in range(NT // 2):
                            sq = kp % 2
                            for kk in range(2):
                                kt = 2 * kp + kk
                                for s2 in range(2):
                                    P.op("pe", lambda e, sq=sq, s2=s2, kt=kt, kk=kk, qq=qq: e.matmul(
                                        sps[sq][:, kk, s2 * QB:(s2 + 1) * QB], lhsT=Kt[:, s2, kt * 128:(kt + 1) * 128], rhs=Qt[qq][:, s2, :],
                                        start=True, stop=True),
                                         reads=[Kt_b, Qt_b[qq]], writes=[sps_b[sq]])
                            pp = pi % 3
                            pi += 1
                            P.op("act", lambda e, pp=pp, sq=sq: e.activation(out=Pt[pp][:], in_=sps[sq][:], func=AF.Exp, scale=scale),
                                 reads=[sps_b[sq]], writes=[Pt_b[pp]])
                            for kk in range(2):
                                kt = 2 * kp + kk
                                for s2 in range(2):
                                    for qi in range(QB // 128):
                                        a = s2 * 2 + qi
                                        P.op("pe", lambda e, a=a, pp=pp, s2=s2, qi=qi, kt=kt, kk=kk: e.matmul(
                                            acc[a][:, 0:257], lhsT=Pt[pp][:, kk, s2 * QB + qi * 128:s2 * QB + (qi + 1) * 128],
                                            rhs=Vt[:, kt, 0:257], start=(kt == 0), stop=(kt == NT - 1)),
                                             reads=[Pt_b[pp], Vt_b], writes=[acc_b[a]])
                        for qi in range(QB // 128):
                            a1, a2 = qi, 2 + qi
                            rs = slice(qb * QB + qi * 128, qb * QB + (qi + 1) * 128)
                            zq = (qb * 2 + qi) % 2
                            P.dma("sp", zt[zq][:], zs[rs, hd * 256:(hd + 1) * 256], zt_b[zq], writes=[zt_b[zq]])
                            P.op("dve", lambda e, a1=a1: e.reciprocal(out=rcp[:, 0:1], in_=acc[a1][:, 256:257]),
                                 reads=[acc_b[a1]], writes=[o_b])
                            P.op("dve", lambda e, a2=a2: e.reciprocal(out=rcp[:, 1:2], in_=acc[a2][:, 256:257]),
                                 reads=[acc_b[a2]], writes=[o_b])
                            P.op("dve", lambda e: e.tensor_tensor(out=rcp[:, 1:2], in0=rcp[:, 1:2], in1=lam[:, 3:4], op=ALU.mult),
                                 reads=[o_b, lam_b], writes=[o_b])
                            P.op("act", lambda e, a1=a1: e.activation(out=o1[:], in_=acc[a1][:, 0:256], func=AF.Copy, scale=rcp[:, 0:1]),
                                 reads=[acc_b[a1], o_b], writes=[o_b])
                            P.op("dve", lambda e, a2=a2: e.scalar_tensor_tensor(out=o2[:], in0=acc[a2][:, 0:256], scalar=rcp[:, 1:2],
                                                                              in1=o1[:], op0=ALU.mult, op1=ALU.add),
                                 reads=[acc_b[a2], o_b], writes=[o_b])
                            P.op("act", lambda e: e.activation(out=junk[:], in_=o2[:], func=AF.Square, accum_out=rcp[:, 2:3]),
                                 reads=[o_b], writes=[o_b])
                            P.op("dve", lambda e: e.tensor_scalar(out=rcp[:, 2:3], in0=rcp[:, 2:3], scalar1=1.0 / 256.0, scalar2=SUBLN_EPS,
                                                                  op0=ALU.mult, op1=ALU.add), reads=[o_b], writes=[o_b])
                            P.op("act", lambda e: e.activation(out=rcp[:, 2:3], in_=rcp[:, 2:3], func=AF.Sqrt), reads=[o_b], writes=[o_b])
                            P.op("dve", lambda e: e.reciprocal(out=rcp[:, 2:3], in_=rcp[:, 2:3]), reads=[o_b], writes=[o_b])
                            P.op("dve", lambda e: e.scalar_tensor_tensor(out=o1[:], in0=o2[:], scalar=rcp[:, 2:3], in1=subg[:],
                                                                         op0=ALU.mult, op1=ALU.mult),
                                 reads=[o_b, lam_b], writes=[o_b])
                            P.op("dve", lambda e, zq=zq: e.tensor_tensor(out=oo[zq][:], in0=o1[:], in1=zt[zq][:], op=ALU.mult),
                                 reads=[o_b, zt_b[zq]], writes=[oo_b[zq]])
                            P.dma("act", oS[rs, hd * 256:(hd + 1) * 256], oo[zq][:], oo_b[zq], reads=[oo_b[zq]])
                P.barrier()

        def phase_at_c(j, h_src, h_dst):
            with ExitStack() as st:
                ob = st.enter_context(SBT("ac_ob", [128, 4, D], BF16))
                ob_b = P.buf("ob")
                oT = st.enter_context(SBT("ac_oT", [128, KT, 512], BF16))
                oT_b = P.buf("oT")
                tp = [st.enter_context(PST(f"ac_tp{i}", [128, 1024], BF16)) for i in range(2)]
                tp_b = [P.buf("actp0"), P.buf("actp1")]
                NW = min(256, D)
                wo = [st.enter_context(SBT(f"ac_wo{i}", [128, KT, NW], BF16)) for i in range(2)]
                wo_b = [P.buf("acwo0"), P.buf("acwo1")]
                po = [st.enter_context(PST(f"ac_po{i}", [128, 4, NW], F32)) for i in range(2)]
                po_b = [P.buf("acpo0"), P.buf("acpo1")]
                hb = [st.enter_context(SBT(f"ac_hb{i}", [128, 4, NW], F32)) for i in range(2)]
                hb_b = [P.buf("achb0"), P.buf("achb1")]
                wosrc = wb["attn_w_out"][j]
                ito = 0
                for b in range(NB):
                    bs = slice(b * 512, (b + 1) * 512)
                    P.dma("sp", ob[:], oS[bs, :].rearrange("(i p) d -> p i d", p=128), ob_b, writes=[ob_b])
                    for kt in range(KT):
                        s = kt % 2
                        for i in range(4):
                            P.op("pe", lambda e, i=i, kt=kt, s=s: e.transpose(out=tp[s][:, i * 128:(i + 1) * 128],
                                                                           in_=ob[:, i, kt * 128:(kt + 1) * 128], identity=ident[:]),
                                 reads=[ob_b, b_const], writes=[tp_b[s]])
                        P.op("dve", lambda e, kt=kt, s=s: e.tensor_copy(out=oT[:, kt, :], in_=tp[s][:, 0:512]),
                             reads=[tp_b[s]], writes=[oT_b])
                    for pn in range(D // NW):
                        s = ito % 2
                        ito += 1
                        P.dma("pool", wo[s][:], wosrc[:, pn * NW:(pn + 1) * NW].rearrange("(kt p) n -> p kt n", p=128),
                              wo_b[s], writes=[wo_b[s]])
                        for i in range(4):
                            for kt in range(KT):
                                P.op("pe", lambda e, s=s, i=i, kt=kt: e.matmul(
                                    po[s][:, i, :], lhsT=oT[:, kt, i * 128:(i + 1) * 128], rhs=wo[s][:, kt, :],
                                    start=(kt == 0), stop=(kt == KT - 1)),
                                     reads=[wo_b[s], oT_b], writes=[po_b[s]])
                        P.dma("sp", hb[s][:], h_src[bs, pn * NW:(pn + 1) * NW].rearrange("(i p) n -> p i n", p=128), hb_b[s],
                              writes=[hb_b[s]])
                        P.op("dve", lambda e, s=s: e.tensor_tensor(out=hb[s][:], in0=hb[s][:], in1=po[s][:], op=ALU.add),
                             reads=[hb_b[s], po_b[s]], writes=[hb_b[s]])
                        P.dma("act", h_dst[bs, pn * NW:(pn + 1) * NW].rearrange("(i p) n -> p i n", p=128), hb[s][:], hb_b[s],
                              reads=[hb_b[s]])
                P.barrier()

        def phase_final(h_src):
            with ExitStack() as st:
                hb = [st.enter_context(SBT(f"f_hb{i}", [128, D], F32)) for i in range(2)]
                hb_b = [P.buf("fhb0"), P.buf("fhb1")]
                ho = [st.enter_context(SBT(f"f_ho{i}", [128, D], F32)) for i in range(2)]
                ho_b = [P.buf("fho0"), P.buf("fho1")]
                junk = st.enter_context(SBT("f_junk", [128, D], BF16))
                junk_b = P.buf("fjunk")
                ss = st.enter_context(SBT("f_ss", [128, 2], F32))
                ss_b = P.buf("fss")
                gfull = st.enter_context(SBT("f_g", [128, D], F32))
                P.dma("sp", gfull[:], w["final_norm"].partition_broadcast(128), b_const, writes=[b_const])
                for t in range(NT):
                    s = t % 2
                    rs = slice(t * 128, (t + 1) * 128)
                    P.dma("sp", hb[s][:], h_src[rs, :], hb_b[s], writes=[hb_b[s]])
                    P.op("act", lambda e, s=s: e.activation(out=junk[:], in_=hb[s][:], func=AF.Square, accum_out=ss[:, s:s + 1]),
                         reads=[hb_b[s]], writes=[junk_b, ss_b])
                    P.op("dve", lambda e, s=s: e.tensor_scalar(out=ss[:, s:s + 1], in0=ss[:, s:s + 1], scalar1=1.0 / D, scalar2=NORM_EPS,
                                                            op0=ALU.mult, op1=ALU.add), reads=[ss_b], writes=[ss_b])
                    P.op("act", lambda e, s=s: e.activation(out=ss[:, s:s + 1], in_=ss[:, s:s + 1], func=AF.Sqrt), reads=[ss_b], writes=[ss_b])
                    P.op("dve", lambda e, s=s: e.reciprocal(out=ss[:, s:s + 1], in_=ss[:, s:s + 1]), reads=[ss_b], writes=[ss_b])
                    P.op("dve", lambda e, s=s: e.scalar_tensor_tensor(out=ho[s][:], in0=hb[s][:], scalar=ss[:, s:s + 1], in1=gfull[:],
                                                                   op0=ALU.mult, op1=ALU.mult),
                         reads=[hb_b[s], ss_b, b_const], writes=[ho_b[s]])
                    P.dma("act", y_out[rs, :], ho[s][:], ho_b[s], reads=[ho_b[s]])
                P.barrier()

        phase_cast()
        hbufs = [hA, hB]
        cur = x_in
        for i in range(c.depth):
            j = i // 2
            nxt = hbufs[i % 2]
            if i % 2 == 0:
                phase_s5a(j, cur)
                phase_s5b(j)
                phase_s5c(j, cur, nxt)
            else:
                stop = getattr(cfg, "stop", "")
                phase_at_a(j, cur)
                if stop != "at_a":
                    phase_at_b(j, i)
                    if stop != "at_b":
                        phase_at_c(j, cur, nxt)
            cur = nxt
        phase_final(cur)
        build_program.n_inst = P.n_inst
        for k, v in P.maxwait.items():
            assert v <= P.total.get(k, 0), ("unreachable wait", v, P.total.get(k, 0))
    return nc


def host_constants(L):
    ident = np.eye(128, dtype=np.float32)
    jj = np.zeros((128, 128), np.float32)
    for k in range(64):
        jj[k, k + 64] = 1.0
        jj[k + 64, k] = 1.0
    rot = jj.copy()
    pos = np.arange(L, dtype=np.float32)
    inv_freq = (1.0 / (np.float32(ROPE_THETA) ** (np.arange(0, 128, 2, dtype=np.float32) / np.float32(128)))).astype(np.float32)
    ang = pos[None, :] * inv_freq[:, None]
    cos = np.cos(ang).astype(np.float32)
    sin = np.sin(ang).astype(np.float32)
    ropec = np.concatenate([cos, cos], axis=0)
    ropes = np.concatenate([-sin, sin], axis=0)
    return dict(cident=ident, cjj=jj, crot=rot, ropec=np.ascontiguousarray(ropec), ropes=np.ascontiguousarray(ropes))


def run_cfg(cfg, seqs, weights, n_cores):
    L, D = cfg.L, cfg.D
    nc = build_program(cfg)
    consts = host_constants(L)
    in_maps = []
    for ci in range(n_cores):
        xs = seqs[ci % len(seqs)]
        Li = xs.shape[0]
        x = np.zeros((L, D), np.float32)
        x[:Li] = xs
        m = np.zeros((L,), np.float32)
        m[:Li] = 1.0
        kb = np.where(m > 0, 0.0, -30000.0).astype(np.float32)
        d = {"x": x,
             "tokmask": np.ascontiguousarray(m.reshape(L // 128, 128).T),
             "keybias": np.ascontiguousarray(kb.reshape(L // 128, 128).T)}
        d.update(consts)
        for k, v in weights.items():
            d[k] = np.ascontiguousarray(v, dtype=np.float32)
        in_maps.append(d)
    res = run_bass_kernel_spmd(nc, in_maps, core_ids=list(range(n_cores)))
    run_cfg.last = res
    outs = []
    for ci in range(len(seqs)):
        outs.append(np.asarray(res.results[ci]["y"])[:seqs[ci].shape[0]])
    return outs


def kernel(x_prompt, x_sample, **weights):
    x_prompt = np.asarray(x_prompt, dtype=np.float32)
    x_sample = np.asarray(x_sample, dtype=np.float32)
    D = x_prompt.shape[-1]
    L = max(x_prompt.shape[1], x_sample.shape[1])
    cfg = Cfg(L=L, D=D, H=D // 256)
    seqs = [x_sample[i] for i in range(x_sample.shape[0])] + [x_prompt[i] for i in range(x_prompt.shape[0])]
    assert len(seqs) <= 8
    outs = run_cfg(cfg, seqs, weights, 8)
    ns = x_sample.shape[0]
    y_sample = np.stack(outs[:ns], axis=0)
    y_prompt = np.stack(outs[ns:], axis=0)
    return (y_prompt.astype(np.float32), y_sample.astype(np.float32))
```

```python
import math
from contextlib import ExitStack

import numpy as np
import concourse.bass as bass
import concourse.mybir as mybir
from concourse.bass_utils import run_bass_kernel_spmd

F32 = mybir.dt.float32
BF16 = mybir.dt.bfloat16
AF = mybir.ActivationFunctionType
ALU = mybir.AluOpType
AX = mybir.AxisListType

NORM_EPS = 1e-6
SUBLN_EPS = 1e-5
ROPE_THETA = 10000.0
S5_LAMBDA_RE_MAX = -1e-4
SEM_ROT = 20000


class Buf:
    def __init__(self, name):
        self.name = name
        self.w = None
        self.r = {}
        self.dma_sem = None
        self.dma_cnt = 0


class Prog:
    def __init__(self, nc, stack):
        self.nc = nc
        self.stack = stack
        self.eng = {"pe": nc.tensor, "act": nc.scalar, "dve": nc.vector, "pool": nc.gpsimd, "sp": nc.sync}
        self.sems = {}
        self.cnt = {}
        self.waited = {}
        self.allsems = []
        for e in self.eng:
            self.sems[e] = [self._newsem(f"s_{e}_0")]
            self.cnt[e] = 0
        self.bufs = []
        self.n_inst = 0
        self.maxwait = {}
        self.total = {}
        self.sem_pool = []

    def _newsem(self, name):
        return self.stack.enter_context(self.nc.semaphore(name))

    def buf(self, name):
        b = Buf(name)
        b.psum = name.startswith(("tp", "ps", "cps", "cpo", "aps", "apr", "sps", "acc", "actp", "acpo"))
        self.bufs.append(b)
        return b

    def _wait(self, e, sem, val):
        key = (e, id(sem))
        if self.waited.get(key, 0) >= val:
            return
        self.waited[key] = val
        self.maxwait[id(sem)] = max(self.maxwait.get(id(sem), 0), val)
        self.eng[e].wait_ge(sem, val)
        self.n_inst += 1

    def _wait_ticket(self, e, t):
        if t is None:
            return
        kind, sem, val, src = t
        if kind == "eng" and src == e and e == "pe":
            return
        self._wait(e, sem, val)

    def _deps(self, e, reads, writes, is_dma=False):
        for b in reads:
            self._wait_ticket(e, b.w)
            if getattr(b, "psum", False):
                for t in b.r.values():
                    if not (t[0] == "eng" and t[3] == e):
                        self._wait_ticket(e, t)
        for b in writes:
            if not (is_dma and b.w is not None and b.w[0] == "dma"):
                self._wait_ticket(e, b.w)
            for t in b.r.values():
                self._wait_ticket(e, t)

    def _record(self, t, reads, writes):
        for b in reads:
            b.r[id(t[1])] = t
        for b in writes:
            b.w = t
            b.r = {}

    def sync_on(self, e, b):
        self._wait_ticket(e, b.w)
        for t in b.r.values():
            self._wait_ticket(e, t)

    def op(self, e, fn, reads=(), writes=(), marks=()):
        self._deps(e, reads, writes)
        if self.cnt[e] >= SEM_ROT:
            self.sems[e].append(self._newsem(f"s_{e}_{len(self.sems[e])}"))
            self.cnt[e] = 0
        sem = self.sems[e][-1]
        ins = fn(self.eng[e])
        ins.then_inc(sem, 1)
        self.cnt[e] += 1
        self.total[id(sem)] = self.cnt[e]
        self.n_inst += 1
        t = ("eng", sem, self.cnt[e], e)
        self._record(t, reads, writes)
        for b in marks:
            b.w = t
        return t

    def dma(self, q, out, in_, sb, reads=(), writes=(), **kw):
        self._deps(q, reads, writes, is_dma=True)
        if sb.dma_sem is None:
            if self.sem_pool:
                sb.dma_sem, sb.dma_cnt = self.sem_pool.pop()
            else:
                sb.dma_sem = self._newsem(f"d_{len(self.allsems)}")
                sb.dma_cnt = 0
                self.allsems.append(sb.dma_sem)
        ins = self.eng[q].dma_start(out=out, in_=in_, **kw)
        ins.then_inc(sb.dma_sem, 16)
        sb.dma_cnt += 16
        self.total[id(sb.dma_sem)] = sb.dma_cnt
        self.n_inst += 1
        t = ("dma", sb.dma_sem, sb.dma_cnt, q)
        self._record(t, reads, writes)
        return t

    def barrier(self):
        for e in self.eng:
            for e2 in self.eng:
                if e2 != e and self.cnt[e2] > 0:
                    self._wait(e, self.sems[e2][-1], self.cnt[e2])
            for b in self.bufs:
                if b.dma_sem is not None and b.dma_cnt > 0:
                    self._wait(e, b.dma_sem, b.dma_cnt)
        for b in self.bufs:
            b.w = None
            b.r = {}
            if b.dma_sem is not None:
                self.sem_pool.append((b.dma_sem, b.dma_cnt))
                b.dma_sem = None
                b.dma_cnt = 0
        self.bufs = [b for b in self.bufs if b.name == "const"]


def _ceil_div(a, b):
    return (a + b - 1) // b


class Cfg:
    def __init__(self, L, D, H, depth=4):
        self.L = L
        self.D = D
        self.E = 2 * D
        self.G = self.E // 16
        self.AW = D
        self.H = H
        assert H * 256 == D
        self.depth = depth
        self.NS5 = (depth + 1) // 2
        self.NAT = max(depth // 2, 1)
        self.KT = D // 128
        self.ET = self.E // 128
        self.BT = 512
        self.NB = L // 512
        self.NT = L // 128
        self.NLEV = int(round(math.log2(L)))
        assert 2 ** self.NLEV == L


def lambda_init(i):
    return 0.8 - 0.6 * math.exp(-0.3 * i)


def build_program(cfg):
    c = cfg
    L, D, E, G, H, KT, ET, NB, NT, NLEV = c.L, c.D, c.E, c.G, c.H, c.KT, c.ET, c.NB, c.NT, c.NLEV
    NS5, NAT = c.NS5, c.NAT
    nc = bass.Bass("TRN2", target_bir_lowering=False)

    _uid = [0]

    def SBT(name, shape, dt):
        _uid[0] += 1
        return nc.sbuf_tensor(f"{name}_u{_uid[0]}", shape, dt)

    def PST(name, shape, dt):
        _uid[0] += 1
        return nc.psum_tensor(f"{name}_u{_uid[0]}", shape, dt)

    def din(name, shape, dt=F32):
        return nc.dram_tensor(name, list(shape), dt, kind="ExternalInput").ap()

    def dscr(name, shape, dt):
        kind = "ExternalOutput" if (getattr(cfg, "debug", False) and name in ("uT", "zsT", "yT", "hA", "hB", "qT", "kT", "vA", "zs", "oS")) else "Internal"
        return nc.dram_tensor(name, list(shape), dt, kind=kind).ap()

    x_in = din("x", [L, D])
    tokmask = din("tokmask", [128, NT])
    keybias = din("keybias", [128, NT])
    ropec = din("ropec", [128, L])
    ropes = din("ropes", [128, L])
    cident = din("cident", [128, 128])
    cjj = din("cjj", [128, 128])
    crot = din("crot", [128, 128])
    w = {}
    w["s5_norm"] = din("s5_norm", [NS5, D])
    w["s5_w_in"] = din("s5_w_in", [NS5, D, 2 * E])
    w["s5_a_re"] = din("s5_a_re", [NS5, 2, G, 64])
    w["s5_a_im"] = din("s5_a_im", [NS5, 2, G, 64])
    w["s5_log_dt"] = din("s5_log_dt", [NS5, 2, G])
    w["s5_b_re"] = din("s5_b_re", [NS5, 2, G, 64, 16])
    w["s5_b_im"] = din("s5_b_im", [NS5, 2, G, 64, 16])
    w["s5_c_re"] = din("s5_c_re", [NS5, 2, G, 16, 64])
    w["s5_c_im"] = din("s5_c_im", [NS5, 2, G, 16, 64])
    w["s5_d"] = din("s5_d", [NS5, E])
    w["s5_w_glu"] = din("s5_w_glu", [NS5, E, E])
    w["s5_b_glu"] = din("s5_b_glu", [NS5, E])
    w["s5_w_out"] = din("s5_w_out", [NS5, E, D])
    w["attn_norm"] = din("attn_norm", [NAT, D])
    w["attn_w_in"] = din("attn_w_in", [NAT, D, 4 * D])
    for nm in ("attn_lambda_q1", "attn_lambda_k1", "attn_lambda_q2", "attn_lambda_k2"):
        w[nm] = din(nm, [NAT, 128])
    w["attn_subln"] = din("attn_subln", [NAT, 256])
    w["attn_w_out"] = din("attn_w_out", [NAT, D, D])
    w["final_norm"] = din("final_norm", [D])
    y_out = nc.dram_tensor("y", [L, D], F32, kind="ExternalOutput").ap()

    hA = dscr("hA", [L, D], F32)
    hB = dscr("hB", [L, D], F32)
    wb = {
        "s5_w_in": dscr("wb_s5_w_in", [NS5, D, 2 * E], BF16),
        "s5_w_glu": dscr("wb_s5_w_glu", [NS5, E, E], BF16),
        "s5_w_out": dscr("wb_s5_w_out", [NS5, E, D], BF16),
        "attn_w_in": dscr("wb_attn_w_in", [NAT, D, 4 * D], BF16),
        "attn_w_out": dscr("wb_attn_w_out", [NAT, D, D], BF16),
    }
    uT = dscr("uT", [E, L], BF16)
    zsT = dscr("zsT", [E, L], BF16)
    yT = dscr("yT", [E, L], BF16)
    qT = dscr("qT", [2 * H, 128, L], BF16)
    kT = dscr("kT", [2 * H, 128, L], BF16)
    vA = dscr("vA", [H, L, 256], BF16)
    zs = dscr("zs", [L, D], BF16)
    oS = dscr("oS", [L, D], BF16)

    stack = ExitStack()
    with stack:
        P = Prog(nc, stack)

        def sb(name, shape, dt):
            return stack.enter_context(SBT(name, list(shape), dt))

        ident_f = sb("ident_f", [128, 128], F32)
        ident = sb("ident", [128, 128], BF16)
        jj_f = sb("jj_f", [128, 128], F32)
        rot_f = sb("rot_f", [128, 128], F32)
        rotb = sb("rotb", [128, 128], BF16)
        tmask = sb("tmask", [128, NT], F32)
        kbias = sb("kbias", [128, NT], F32)
        b_const = P.buf("const")
        P.dma("sp", ident_f[:], cident[:, :], b_const, writes=[b_const])
        P.dma("sp", jj_f[:], cjj[:, :], b_const, writes=[b_const])
        P.dma("sp", rot_f[:], crot[:, :], b_const, writes=[b_const])
        P.dma("sp", tmask[:], tokmask[:, :], b_const, writes=[b_const])
        P.dma("sp", kbias[:], keybias[:, :], b_const, writes=[b_const])
        P.op("dve", lambda e: e.tensor_copy(out=ident[:], in_=ident_f[:]), reads=[b_const], writes=[b_const])
        P.op("dve", lambda e: e.tensor_copy(out=rotb[:], in_=rot_f[:]), reads=[b_const], writes=[b_const])

        def phase_cast():
            with ExitStack() as st:
                CW = 2048
                stg_f = [st.enter_context(SBT(f"cw_f{i}", [128, CW], F32)) for i in range(2)]
                stg_b = [st.enter_context(SBT(f"cw_b{i}", [128, CW], BF16)) for i in range(2)]
                bf = [P.buf(f"cw_f{i}") for i in range(2)]
                bb = [P.buf(f"cw_b{i}") for i in range(2)]
                it = 0
                for nm, dst in wb.items():
                    src = w[nm]
                    nl, R, C = src.shape
                    for l in range(nl):
                        for r0 in range(0, R, 128):
                            for c0 in range(0, C, CW):
                                cw = min(CW, C - c0)
                                s = it % 2
                                P.dma("sp", stg_f[s][:, :cw], src[l, r0:r0 + 128, c0:c0 + cw], bf[s], writes=[bf[s]])
                                eng = "dve" if it % 2 == 0 else "pool"
                                P.op(eng, lambda e, s=s, cw=cw: e.tensor_copy(out=stg_b[s][:, :cw], in_=stg_f[s][:, :cw]),
                                     reads=[bf[s]], writes=[bb[s]])
                                P.dma("act", dst[l, r0:r0 + 128, c0:c0 + cw], stg_b[s][:, :cw], bb[s], reads=[bb[s]])
                                it += 1
                P.barrier()

        def norm_block(st_bufs, h_src, b, gain_sb):
            (hb, hb_b, hn, hn_b, ss, ss_b, junk, junk_b, hnT, hnT_b, tp_ps, tp_b) = st_bufs
            P.dma("sp", hb[:], h_src[b * 512:(b + 1) * 512, :].rearrange("(i p) d -> p i d", p=128), hb_b, writes=[hb_b])
            for i in range(4):
                P.op("act", lambda e, i=i: e.activation(out=junk[:], in_=hb[:, i, :], func=AF.Square,
                                                         accum_out=ss[:, i:i + 1]),
                     reads=[hb_b], writes=[junk_b, ss_b])
            P.op("dve", lambda e: e.tensor_scalar(out=ss[:, 0:4], in0=ss[:, 0:4], scalar1=1.0 / D, scalar2=NORM_EPS,
                                                  op0=ALU.mult, op1=ALU.add), reads=[ss_b], writes=[ss_b])
            P.op("act", lambda e: e.activation(out=ss[:, 0:4], in_=ss[:, 0:4], func=AF.Sqrt), reads=[ss_b], writes=[ss_b])
            P.op("dve", lambda e: e.reciprocal(out=ss[:, 0:4], in_=ss[:, 0:4]), reads=[ss_b], writes=[ss_b])
            P.op("dve", lambda e: e.tensor_tensor(out=ss[:, 0:4], in0=ss[:, 0:4], in1=tmask[:, b * 4:(b + 1) * 4],
                                                  op=ALU.mult), reads=[ss_b, b_const], writes=[ss_b])
            for i in range(4):
                P.op("act", lambda e, i=i: e.activation(out=hn[:, i, :], in_=hb[:, i, :], func=AF.Copy,
                                                         scale=ss[:, i:i + 1]),
                     reads=[hb_b, ss_b], writes=[hn_b])
            for kt in range(KT):
                s = kt % 2
                for i in range(4):
                    P.op("pe", lambda e, i=i, kt=kt, s=s: e.transpose(out=tp_ps[s][:, i * 128:(i + 1) * 128],
                                                                   in_=hn[:, i, kt * 128:(kt + 1) * 128],
                                                                   identity=ident[:]),
                         reads=[hn_b, b_const], writes=[tp_b[s]])
                eng = "dve" if kt % 2 == 0 else "pool"
                if eng == "pool":
                    eng = "dve"
                P.op(eng, lambda e, kt=kt, s=s: e.tensor_scalar(out=hnT[:, kt, :], in0=tp_ps[s][:, 0:512], scalar1=gain_sb[:, kt:kt + 1],
                                                           scalar2=None, op0=ALU.mult),
                     reads=[tp_b[s], b_const], writes=[hnT_b])

        def alloc_norm_bufs(st, tag):
            hb = st.enter_context(SBT(f"hb_{tag}", [128, 4, D], F32))
            hn = st.enter_context(SBT(f"hn_{tag}", [128, 4, D], BF16))
            ss = st.enter_context(SBT(f"ss_{tag}", [128, 4], F32))
            junk = st.enter_context(SBT(f"junk_{tag}", [128, D], BF16))
            hnT = st.enter_context(SBT(f"hnT_{tag}", [128, KT, 512], BF16))
            tp = [st.enter_context(PST(f"tp_{tag}{i}", [128, 1024], BF16)) for i in range(2)]
            return (hb, P.buf("hb"), hn, P.buf("hn"), ss, P.buf("ss"), junk, P.buf("junk"), hnT, P.buf("hnT"),
                    tp, [P.buf("tp0"), P.buf("tp1")])

        def load_gain(st, src_row, tag):
            g = st.enter_context(SBT(f"gain_{tag}", [128, KT], F32))
            P.dma("sp", g[:], src_row.rearrange("(kt p) -> p kt", p=128), b_const, writes=[b_const],
                  allow_slow_non_contiguous=True)
            return g

        def phase_s5a(j, h_src):
            with ExitStack() as st:
                nb = alloc_norm_bufs(st, "s5a")
                hnT, hnT_b = nb[8], nb[9]
                gain = load_gain(st, w["s5_norm"][j], "s5a")
                MW = 512
                wp = [st.enter_context(SBT(f"wp_s5a{i}", [128, KT, MW], BF16)) for i in range(2)]
                wp_b = [P.buf("wp0"), P.buf("wp1")]
                ps = [st.enter_context(PST(f"ps_s5a{i}", [128, 512], F32)) for i in range(2)]
                ps_b = [P.buf("ps0"), P.buf("ps1")]
                og = [st.enter_context(SBT(f"og_s5a{i}", [128, 512], BF16)) for i in range(4)]
                og_b = [P.buf(f"og{i}") for i in range(4)]
                wsrc = wb["s5_w_in"][j]
                it = 0
                oi = 0
                for b in range(NB):
                    norm_block(nb, h_src, b, gain)
                    for pn in range(2 * E // MW):
                        s = it % 2
                        it += 1
                        P.dma("pool", wp[s][:], wsrc[:, pn * MW:(pn + 1) * MW].rearrange("(kt p) m -> p kt m", p=128),
                              wp_b[s], writes=[wp_b[s]])
                        for mi in range(MW // 128):
                            m = pn * (MW // 128) + mi
                            q = m % 2
                            for kt in range(KT):
                                P.op("pe", lambda e, q=q, s=s, mi=mi, kt=kt: e.matmul(
                                    ps[q][:], lhsT=wp[s][:, kt, mi * 128:(mi + 1) * 128], rhs=hnT[:, kt, :],
                                    start=(kt == 0), stop=(kt == KT - 1)),
                                     reads=[wp_b[s], hnT_b], writes=[ps_b[q]])
                            o = oi % 4
                            oi += 1
                            if m < ET:
                                P.op("dve", lambda e, o=o, q=q: e.tensor_copy(out=og[o][:], in_=ps[q][:]),
                                     reads=[ps_b[q]], writes=[og_b[o]])
                                P.dma("sp", uT[m * 128:(m + 1) * 128, b * 512:(b + 1) * 512], og[o][:], og_b[o], reads=[og_b[o]])
                            else:
                                P.op("act", lambda e, o=o, q=q: e.activation(out=og[o][:], in_=ps[q][:], func=AF.Silu),
                                     reads=[ps_b[q]], writes=[og_b[o]])
                                P.dma("sp", zsT[(m - ET) * 128:(m - ET + 1) * 128, b * 512:(b + 1) * 512], og[o][:], og_b[o],
                                      reads=[og_b[o]])
                P.barrier()

        def phase_s5b(j):
            with ExitStack() as st:
                G2 = 2 * G
                are = st.enter_context(SBT("t_are", [128, G2], F32))
                aim = st.enter_context(SBT("t_aim", [128, G2], F32))
                dt = st.enter_context(SBT("t_dt", [128, G2], F32))
                t0 = st.enter_context(SBT("t_t0", [128, G2], F32))
                t1 = st.enter_context(SBT("t_t1", [128, G2], F32))
                t2 = st.enter_context(SBT("t_t2", [128, G2], F32))
                t3 = st.enter_context(SBT("t_t3", [128, G2], F32))
                fr = st.enter_context(SBT("t_fr", [128, G2], F32))
                fi = st.enter_context(SBT("t_fi", [128, G2], F32))
                PA = st.enter_context(SBT("t_PA", [128, NLEV, G2], F32))
                PC = st.enter_context(SBT("t_PC", [128, NLEV, G2], F32))
                sgn = st.enter_context(SBT("t_sgn", [128, 1], F32))
                dsk = st.enter_context(SBT("t_dsk", [16, G], F32))
                tb = P.buf("tables")
                for half in range(2):
                    P.dma("sp", are[half * 64:(half + 1) * 64, :], w["s5_a_re"][j].rearrange("d g p -> p (d g)"), tb,
                          writes=[tb], allow_slow_non_contiguous=True)
                    P.dma("sp", aim[half * 64:(half + 1) * 64, :], w["s5_a_im"][j].rearrange("d g p -> p (d g)"), tb,
                          writes=[tb], allow_slow_non_contiguous=True)
                P.dma("sp", dt[:], w["s5_log_dt"][j].rearrange("d g -> (d g)").partition_broadcast(128), tb, writes=[tb])
                P.dma("sp", dsk[:], w["s5_d"][j].rearrange("(g i) -> i g", i=16), tb, writes=[tb],
                      allow_slow_non_contiguous=True)
                V = lambda fn: P.op("dve", fn, reads=[tb], writes=[tb])
                A_ = lambda fn: P.op("act", fn, reads=[tb], writes=[tb])
                V(lambda e: e.memset(sgn[0:64, :], 1.0))
                V(lambda e: e.memset(sgn[64:128, :], -1.0))
                V(lambda e: e.tensor_scalar(out=are[:], in0=are[:], scalar1=S5_LAMBDA_RE_MAX, scalar2=None, op0=ALU.min))
                A_(lambda e: e.activation(out=dt[:], in_=dt[:], func=AF.Exp))
                V(lambda e: e.tensor_tensor(out=t0[:], in0=are[:], in1=dt[:], op=ALU.mult))
                A_(lambda e: e.activation(out=t0[:], in_=t0[:], func=AF.Exp))
                V(lambda e: e.tensor_tensor(out=t1[:], in0=aim[:], in1=dt[:], op=ALU.mult))
                TWO_PI = 2.0 * math.pi
                def sin_of(dst, shift):
                    V(lambda e: e.tensor_scalar(out=dst[:], in0=t1[:], scalar1=shift, scalar2=None, op0=ALU.add))
                    for _ in range(5):
                        V(lambda e: e.tensor_scalar(out=fr[:], in0=dst[:], scalar1=TWO_PI, scalar2=TWO_PI, op0=ALU.is_ge, op1=ALU.mult))
                        V(lambda e: e.tensor_tensor(out=dst[:], in0=dst[:], in1=fr[:], op=ALU.subtract))
                    V(lambda e: e.tensor_scalar(out=dst[:], in0=dst[:], scalar1=-math.pi, scalar2=None, op0=ALU.add))
                    V(lambda e: e.tensor_scalar(out=dst[:], in0=dst[:], scalar1=math.pi, scalar2=-math.pi, op0=ALU.min, op1=ALU.max))
                    A_(lambda e: e.activation(out=dst[:], in_=dst[:], func=AF.Sin))
                sin_of(t2, math.pi)
                sin_of(t3, 1.5 * math.pi)
                V(lambda e: e.tensor_tensor(out=PA[:, 0, :], in0=t0[:], in1=t3[:], op=ALU.mult))
                V(lambda e: e.tensor_tensor(out=t2[:], in0=t0[:], in1=t2[:], op=ALU.mult))
                V(lambda e: e.tensor_tensor(out=t0[:], in0=are[:], in1=are[:], op=ALU.mult))
                V(lambda e: e.tensor_tensor(out=t1[:], in0=aim[:], in1=aim[:], op=ALU.mult))
                V(lambda e: e.tensor_tensor(out=t0[:], in0=t0[:], in1=t1[:], op=ALU.add))
                V(lambda e: e.reciprocal(out=t0[:], in_=t0[:]))
                V(lambda e: e.tensor_scalar(out=t3[:], in0=PA[:, 0, :], scalar1=-1.0, scalar2=None, op0=ALU.add))
                V(lambda e: e.tensor_tensor(out=fr[:], in0=t3[:], in1=are[:], op=ALU.mult))
                V(lambda e: e.tensor_tensor(out=t1[:], in0=t2[:], in1=aim[:], op=ALU.mult))
                V(lambda e: e.tensor_tensor(out=fr[:], in0=fr[:], in1=t1[:], op=ALU.add))
                V(lambda e: e.tensor_tensor(out=fr[:], in0=fr[:], in1=t0[:], op=ALU.mult))
                V(lambda e: e.tensor_tensor(out=fi[:], in0=t2[:], in1=are[:], op=ALU.mult))
                V(lambda e: e.tensor_tensor(out=t1[:], in0=t3[:], in1=aim[:], op=ALU.mult))
                V(lambda e: e.tensor_tensor(out=fi[:], in0=fi[:], in1=t1[:], op=ALU.subtract))
                V(lambda e: e.tensor_tensor(out=fi[:], in0=fi[:], in1=t0[:], op=ALU.mult))
                V(lambda e: e.tensor_scalar(out=fi[:], in0=fi[:], scalar1=sgn[:, 0:1], scalar2=-1.0, op0=ALU.mult, op1=ALU.mult))
                V(lambda e: e.tensor_scalar(out=PC[:, 0, :], in0=t2[:], scalar1=sgn[:, 0:1], scalar2=None, op0=ALU.mult))
                for k in range(1, NLEV):
                    V(lambda e, k=k: e.tensor_tensor(out=t0[:], in0=PA[:, k - 1, :], in1=PA[:, k - 1, :], op=ALU.mult))
                    V(lambda e, k=k: e.tensor_tensor(out=t1[:], in0=PC[:, k - 1, :], in1=PC[:, k - 1, :], op=ALU.mult))
                    V(lambda e, k=k: e.tensor_tensor(out=PC[:, k, :], in0=PA[:, k - 1, :], in1=PC[:, k - 1, :], op=ALU.mult))
                    V(lambda e, k=k: e.tensor_scalar(out=PC[:, k, :], in0=PC[:, k, :], scalar1=2.0, scalar2=None, op0=ALU.mult))
                    V(lambda e, k=k: e.tensor_tensor(out=PA[:, k, :], in0=t0[:], in1=t1[:], op=ALU.subtract))

                NCH = 2
                ug = [st.enter_context(SBT(f"ug{i}", [16, L], BF16)) for i in range(2)]
                ug_b = [P.buf(f"ug{i}") for i in range(2)]
                X = [st.enter_context(SBT(f"X{i}", [128, L], BF16)) for i in range(2)]
                X_b = [P.buf(f"X{i}") for i in range(2)]
                bt1 = [st.enter_context(SBT(f"bt1_{i}", [128, 16], F32)) for i in range(2)]
                bt2 = [st.enter_context(SBT(f"bt2_{i}", [128, 16], F32)) for i in range(2)]
                bt_b = [P.buf(f"bt{i}") for i in range(2)]
                bbar = [st.enter_context(SBT(f"bbar{i}", [128, 16], BF16)) for i in range(2)]
                bbar_b = [P.buf(f"bbar{i}") for i in range(2)]
                Bl = [st.enter_context(SBT(f"Bl{i}", [16, 128], BF16)) for i in range(2)]
                Bl_b = [P.buf(f"Bl{i}") for i in range(2)]
                ct = [st.enter_context(SBT(f"ct{i}", [16, 128], F32)) for i in range(2)]
                ct_b = [P.buf(f"ct{i}") for i in range(2)]
                ctb = [st.enter_context(SBT(f"ctb{i}", [16, 128], BF16)) for i in range(2)]
                ctb_b = [P.buf(f"ctb{i}") for i in range(2)]
                Cl = [st.enter_context(SBT(f"Cl{i}", [128, 16], BF16)) for i in range(2)]
                Cl_b = [P.buf(f"Cl{i}") for i in range(2)]
                AM = [st.enter_context(SBT(f"AM{i}", [128, NLEV, 128], BF16)) for i in range(2)]
                AM_b = [P.buf(f"AM{i}") for i in range(2)]
                amt = [st.enter_context(SBT(f"amt{i}", [128, 128], F32)) for i in range(2)]
                amt_b = [P.buf(f"amt{i}") for i in range(2)]
                tps = st.enter_context(PST("tps_s5b", [128, 1024], BF16))
                tps_b = P.buf("tps")
                NPS = 3
                ps = [st.enter_context(PST(f"ps_s5b{i}", [128, 2, 512], F32)) for i in range(NPS)]
                ps_b = [P.buf(f"psb{i}") for i in range(NPS)]
                psf = [p_[:].rearrange("p a b -> p (a b)") for p_ in ps]

                def cols(idx_list):
                    if len(idx_list) == 1:
                        return slice(idx_list[0], idx_list[0] + 1)
                    stp = idx_list[1] - idx_list[0]
                    return slice(idx_list[0], idx_list[-1] + 1, stp)

                dg = [st.enter_context(SBT(f"dg{i}", [16, 16], BF16)) for i in range(2)]
                dg_b = [P.buf(f"dg{i}") for i in range(2)]
                amt3 = [st.enter_context(SBT(f"amt3_{i}", [128, NLEV, 128], F32)) for i in range(2)]
                am3f = [st.enter_context(SBT(f"am3f_{i}", [128, NLEV, 128], F32)) for i in range(2)]
                am3_b = [P.buf(f"am3_{i}") for i in range(2)]
                ugk = [[P.buf(f"ugk{i}_{blk}") for blk in range(NB)] for i in range(2)]
                jjb = jj_f[:].unsqueeze(1).to_broadcast([128, NLEV, 128])
                idb = ident_f[:].unsqueeze(1).to_broadcast([128, NLEV, 128])
                steps = []
                for k in range(NLEV):
                    s_ = 2 ** (k + 1)
                    steps.append((k, list(range(s_ - 1, L, s_))))
                for k in range(NLEV - 2, -1, -1):
                    s_ = 2 ** (k + 1)
                    steps.append((k, list(range(s_ - 1 + 2 ** k, L, s_))))
                qi = 0
                for g in range(G):
                    gs = g % 2
                    P.dma("sp", ug[gs][:], uT[g * 16:(g + 1) * 16, :], ug_b[gs], writes=[ug_b[gs]] + ugk[gs])
                    P.op("pool", lambda e, gs=gs, g=g: e.tensor_scalar(out=dg[gs][:], in0=ident_f[0:16, 0:16], scalar1=dsk[:, g:g + 1],
                                                                     scalar2=None, op0=ALU.mult),
                         reads=[tb, b_const], writes=[dg_b[gs]])
                    Xtok = [None, None]
                    for d in range(2):
                        col = d * G + g
                        P.dma("pool", bt1[d][0:64, :], w["s5_b_re"][j, d, g], bt_b[d], writes=[bt_b[d]])
                        P.dma("pool", bt1[d][64:128, :], w["s5_b_im"][j, d, g], bt_b[d], writes=[bt_b[d]])
                        P.dma("pool", bt2[d][0:64, :], w["s5_b_im"][j, d, g], bt_b[d], writes=[bt_b[d]])
                        P.dma("pool", bt2[d][64:128, :], w["s5_b_re"][j, d, g], bt_b[d], writes=[bt_b[d]])
                        P.op("pool", lambda e, d=d, col=col: e.tensor_scalar(out=bt1[d][:], in0=bt1[d][:], scalar1=fr[:, col:col + 1],
                                                                           scalar2=None, op0=ALU.mult),
                             reads=[bt_b[d], tb], writes=[bt_b[d]])
                        P.op("dve", lambda e, d=d, col=col: e.scalar_tensor_tensor(out=bbar[d][:], in0=bt2[d][:],
                                                                                  scalar=fi[:, col:col + 1], in1=bt1[d][:],
                                                                                  op0=ALU.mult, op1=ALU.add),
                             reads=[bt_b[d], tb], writes=[bbar_b[d]])
                        P.op("pe", lambda e, d=d: e.transpose(out=tps[0:16, 0:128], in_=bbar[d][:], identity=ident[:]),
                             reads=[bbar_b[d], b_const], writes=[tps_b])
                        P.op("act", lambda e, d=d: e.activation(out=Bl[d][:], in_=tps[0:16, 0:128], func=AF.Copy), reads=[tps_b], writes=[Bl_b[d]])
                        P.dma("pool", ct[d][:, 0:64], w["s5_c_re"][j, d, g], ct_b[d], writes=[ct_b[d]])
                        P.dma("pool", ct[d][:, 64:128], w["s5_c_im"][j, d, g], ct_b[d], writes=[ct_b[d]])
                        P.op("pool", lambda e, d=d: e.tensor_copy(out=ctb[d][:, 0:64], in_=ct[d][:, 0:64]),
                             reads=[ct_b[d]], writes=[ctb_b[d]])
                        P.op("pool", lambda e, d=d: e.tensor_scalar(out=ctb[d][:, 64:128], in0=ct[d][:, 64:128], scalar1=-1.0,
                                                                  scalar2=None, op0=ALU.mult),
                             reads=[ct_b[d]], writes=[ctb_b[d]])
                        P.op("pe", lambda e, d=d: e.transpose(out=tps[:, 0:16], in_=ctb[d][:], identity=ident[0:16, 0:16]),
                             reads=[ctb_b[d], b_const], writes=[tps_b])
                        P.op("act", lambda e, d=d: e.activation(out=Cl[d][:], in_=tps[:, 0:16], func=AF.Copy), reads=[tps_b], writes=[Cl_b[d]])
                        pab = PA[:, :, col:col + 1].to_broadcast([128, NLEV, 128])
                        pcb = PC[:, :, col:col + 1].to_broadcast([128, NLEV, 128])
                        P.op("pool", lambda e, d=d, pcb=pcb: e.tensor_tensor(out=amt3[d][:], in0=jjb, in1=pcb, op=ALU.mult),
                             reads=[tb, b_const], writes=[am3_b[d]])
                        P.op("pool", lambda e, d=d, pab=pab: e.tensor_tensor(out=am3f[d][:], in0=idb, in1=pab, op=ALU.mult),
                             reads=[tb, b_const], writes=[am3_b[d]])
                        P.op("pool", lambda e, d=d: e.tensor_tensor(out=AM[d][:], in0=am3f[d][:], in1=amt3[d][:], op=ALU.add),
                             reads=[am3_b[d]], writes=[AM_b[d]])
                        x0 = P.buf(f"Xl{d}_in")
                        P.sync_on("act", X_b[d])
                        for b2 in range(0, NB, 2):
                            q = qi % NPS
                            qi += 1
                            for hi in range(2):
                                blk = b2 + hi
                                P.op("pe", lambda e, d=d, q=q, blk=blk, hi=hi: e.matmul(ps[q][:, hi, :], lhsT=Bl[d][:],
                                                                                    rhs=ug[gs][:, blk * 512:(blk + 1) * 512],
                                                                                    start=True, stop=True),
                                     reads=[Bl_b[d], ugk[gs][blk]], writes=[ps_b[q]])
                            P.op("act", lambda e, d=d, q=q, b2=b2: e.activation(out=X[d][:, b2 * 512:(b2 + 2) * 512], in_=psf[q],
                                                                             func=AF.Copy),
                                 reads=[ps_b[q]], marks=[x0])
                        Xtok[d] = x0
                    for (k, tgt) in steps:
                        hop = 2 ** k
                        ntok = [P.buf(f"Xl0_{k}"), P.buf(f"Xl1_{k}")]
                        for c0 in range(0, len(tgt), 1024):
                            tg = tgt[c0:c0 + 1024]
                            n = len(tg)
                            for d in range(2):
                                q = qi % NPS
                                qi += 1
                                tl = tg if d == 0 else sorted(L - 1 - t for t in tg)
                                for hi in range(2):
                                    part = tl[hi * 512:(hi + 1) * 512]
                                    if not part:
                                        continue
                                    if d == 0:
                                        scols = cols([t - hop for t in part])
                                    else:
                                        scols = cols([t + hop for t in part])
                                    npart = len(part)
                                    P.op("pe", lambda e, d=d, q=q, hi=hi, npart=npart, k=k, scols=scols: e.matmul(
                                        ps[q][:, hi, 0:npart], lhsT=AM[d][:, k, :], rhs=X[d][:, scols], start=True, stop=True),
                                         reads=[Xtok[d], AM_b[d]], writes=[ps_b[q]])
                                tcols = cols(tl)
                                P.op("dve", lambda e, d=d, q=q, n=n, tcols=tcols: e.tensor_tensor(out=X[d][:, tcols], in0=psf[q][:, 0:n],
                                                                                              in1=X[d][:, tcols], op=ALU.add),
                                     reads=[ps_b[q], Xtok[d]], marks=[ntok[d]])
                        Xtok = ntok
                    for b2 in range(0, NB, 2):
                        q = qi % NPS
                        qi += 1
                        for hi in range(2):
                            blk = b2 + hi
                            sl = slice(blk * 512, (blk + 1) * 512)
                            P.op("pe", lambda e, q=q, sl=sl, hi=hi: e.matmul(ps[q][0:16, hi, :], lhsT=Cl[0][:], rhs=X[0][:, sl], start=True, stop=False),
                                 reads=[Cl_b[0], Xtok[0]], writes=[ps_b[q]])
                            P.op("pe", lambda e, q=q, sl=sl, hi=hi: e.matmul(ps[q][0:16, hi, :], lhsT=Cl[1][:], rhs=X[1][:, sl], start=False, stop=False),
                                 reads=[Cl_b[1], Xtok[1]], writes=[ps_b[q]])
                            P.op("pe", lambda e, q=q, sl=sl, hi=hi, gs=gs: e.matmul(ps[q][0:16, hi, :], lhsT=dg[gs][:], rhs=ug[gs][:, sl], start=False, stop=True),
                                 reads=[dg_b[gs], ugk[gs][blk]], writes=[ps_b[q]])
                        P.op("act", lambda e, q=q, b2=b2, gs=gs: e.activation(out=ug[gs][:, b2 * 512:(b2 + 2) * 512], in_=psf[q][0:16, :], func=AF.Copy),
                             reads=[ps_b[q]], writes=[ugk[gs][b2], ugk[gs][b2 + 1]])
                    for d in range(2):
                        X_b[d].w = None
                        X_b[d].r = dict(Xtok[d].r)
                        if Xtok[d].w is not None:
                            X_b[d].r[("w", id(Xtok[d].w[1]))] = Xtok[d].w
                    P.dma("act", yT[g * 16:(g + 1) * 16, :], ug[gs][:], ug_b[gs], reads=[ug_b[gs]] + ugk[gs])
                P.barrier()

        def phase_s5c(j, h_src, h_dst):
            with ExitStack() as st:
                NW0 = min(256, D)
                yb = st.enter_context(SBT("c_yb", [128, ET, 512], BF16))
                gb = st.enter_context(SBT("c_gb", [128, ET, 512], BF16))
                yb_b, gb_b = P.buf("yb"), P.buf("gb")
                tq = st.enter_context(SBT("c_tq", [128, 2048], F32))
                tq_b = P.buf("tq")
                zt = [st.enter_context(SBT(f"c_zt{i}", [128, 512], BF16)) for i in range(2)]
                zt_b = [P.buf("zt0"), P.buf("zt1")]
                sg = [st.enter_context(SBT(f"c_sg{i}", [128, 512], BF16)) for i in range(2)]
                sg_b = [P.buf("sg0"), P.buf("sg1")]
                hb = [st.enter_context(SBT(f"c_hb{i}", [128, 4, NW0], F32)) for i in range(2)]
                hb_b = [P.buf("chb0"), P.buf("chb1")]
                MW = 256
                wg = [st.enter_context(SBT(f"c_wg{i}", [128, ET, MW], BF16)) for i in range(2)]
                wg_b = [P.buf("wg0"), P.buf("wg1")]
                NW = min(256, D)
                wo = [st.enter_context(SBT(f"c_wo{i}", [128, ET, NW], BF16)) for i in range(2)]
                wo_b = [P.buf("wo0"), P.buf("wo1")]
                bglu = st.enter_context(SBT("c_bglu", [128, ET], F32))
                ps = [st.enter_context(PST(f"c_ps{i}", [128, 512], F32)) for i in range(2)]
                ps_b = [P.buf("cps0"), P.buf("cps1")]
                po = [st.enter_context(PST(f"c_po{i}", [128, 4, NW], F32)) for i in range(2)]
                po_b = [P.buf("cpo0"), P.buf("cpo1")]
                P.dma("sp", bglu[:], w["s5_b_glu"][j].rearrange("(m p) -> p m", p=128), b_const, writes=[b_const],
                      allow_slow_non_contiguous=True)
                wgsrc = wb["s5_w_glu"][j]
                wosrc = wb["s5_w_out"][j]
                itg = 0
                ito = 0
                ih = 0
                for b in range(NB):
                    bs = slice(b * 512, (b + 1) * 512)
                    P.dma("sp", yb[:], yT[:, bs].rearrange("(m p) t -> p m t", p=128), yb_b, writes=[yb_b])
                    CH = 4
                    for c0 in range(0, ET, CH):
                        cn = min(CH, ET - c0)
                        ysl = yb[:, c0:c0 + cn, :]
                        tsl = tq[:, 0:cn * 512].rearrange("p (m t) -> p m t", t=512)
                        gsl = gb[:, c0:c0 + cn, :]
                        P.op("act", lambda e, ysl=ysl, tsl=tsl: e.activation(out=tsl, in_=ysl, func=AF.Square),
                             reads=[yb_b], writes=[tq_b])
                        P.op("dve", lambda e, tsl=tsl: e.tensor_scalar(out=tsl, in0=tsl, scalar1=0.044715, scalar2=1.0,
                                                                      op0=ALU.mult, op1=ALU.add), reads=[tq_b], writes=[tq_b])
                        P.op("dve", lambda e, tsl=tsl, ysl=ysl: e.tensor_tensor(out=tsl, in0=tsl, in1=ysl, op=ALU.mult),
                             reads=[tq_b, yb_b], writes=[tq_b])
                        P.op("act", lambda e, tsl=tsl: e.activation(out=tsl, in_=tsl, func=AF.Sigmoid, scale=1.5957691216057308),
                             reads=[tq_b], writes=[tq_b])
                        P.op("dve", lambda e, tsl=tsl, ysl=ysl, gsl=gsl: e.tensor_tensor(out=gsl, in0=tsl, in1=ysl, op=ALU.mult),
                             reads=[tq_b, yb_b], writes=[gb_b])
                    for pn in range(E // MW):
                        s = itg % 2
                        itg += 1
                        P.dma("pool", wg[s][:], wgsrc[:, pn * MW:(pn + 1) * MW].rearrange("(kt p) m -> p kt m", p=128),
                              wg_b[s], writes=[wg_b[s]])
                        for mi in range(MW // 128):
                            m = pn * (MW // 128) + mi
                            q = m % 2
                            P.dma("sp", zt[q][:], zsT[m * 128:(m + 1) * 128, bs], zt_b[q], writes=[zt_b[q]])
                            for kt in range(ET):
                                P.op("pe", lambda e, q=q, s=s, mi=mi, kt=kt: e.matmul(
                                    ps[q][:], lhsT=wg[s][:, kt, mi * 128:(mi + 1) * 128], rhs=gb[:, kt, :],
                                    start=(kt == 0), stop=(kt == ET - 1)),
                                     reads=[wg_b[s], gb_b], writes=[ps_b[q]])
                            P.op("act", lambda e, q=q, m=m: e.activation(out=sg[q][:], in_=ps[q][:], func=AF.Sigmoid,
                                                                         bias=bglu[:, m:m + 1]),
                                 reads=[ps_b[q], b_const], writes=[sg_b[q]])
                            P.op("dve", lambda e, q=q: e.tensor_tensor(out=sg[q][:], in0=sg[q][:], in1=zt[q][:], op=ALU.mult),
                                 reads=[sg_b[q], zt_b[q]], writes=[sg_b[q]])
                            P.op("dve", lambda e, q=q, m=m: e.tensor_tensor(out=yb[:, m, :], in0=yb[:, m, :], in1=sg[q][:], op=ALU.mult),
                                 reads=[sg_b[q], yb_b], writes=[yb_b])
                    for pn in range(D // NW):
                        s = ito % 2
                        ito += 1
                        P.dma("pool", wo[s][:], wosrc[:, pn * NW:(pn + 1) * NW].rearrange("(kt p) n -> p kt n", p=128),
                              wo_b[s], writes=[wo_b[s]])
                        q = pn % 2
                        for i in range(4):
                            for kt in range(ET):
                                P.op("pe", lambda e, q=q, s=s, i=i, kt=kt: e.matmul(
                                    po[q][:, i, :], lhsT=yb[:, kt, i * 128:(i + 1) * 128], rhs=wo[s][:, kt, :],
                                    start=(kt == 0), stop=(kt == ET - 1)),
                                     reads=[wo_b[s], yb_b], writes=[po_b[q]])
                        hs = ih % 2
                        ih += 1
                        hv = hb[hs][:]
                        P.dma("sp", hv, h_src[bs, pn * NW:(pn + 1) * NW].rearrange("(i p) n -> p i n", p=128), hb_b[hs],
                              writes=[hb_b[hs]])
                        P.op("dve", lambda e, hv=hv, q=q: e.tensor_tensor(out=hv, in0=hv, in1=po[q][:], op=ALU.add),
                             reads=[hb_b[hs], po_b[q]], writes=[hb_b[hs]])
                        P.dma("act", h_dst[bs, pn * NW:(pn + 1) * NW].rearrange("(i p) n -> p i n", p=128), hv, hb_b[hs],
                              reads=[hb_b[hs]])
                P.barrier()

        def phase_at_a(j, h_src):
            with ExitStack() as st:
                nb = alloc_norm_bufs(st, "ata")
                hnT, hnT_b = nb[8], nb[9]
                gain = load_gain(st, w["attn_norm"][j], "ata")
                MW = min(512, D)
                wp = [st.enter_context(SBT(f"a_wp{i}", [128, KT, MW], BF16)) for i in range(2)]
                wp_b = [P.buf("awp0"), P.buf("awp1")]
                ps = [st.enter_context(PST(f"a_ps{i}", [128, 512], F32)) for i in range(2)]
                ps_b = [P.buf("aps0"), P.buf("aps1")]
                pr = [st.enter_context(PST(f"a_pr{i}", [128, 512], F32)) for i in range(2)]
                pr_b = [P.buf("apr0"), P.buf("apr1")]
                qb = [st.enter_context(SBT(f"a_qb{i}", [128, 512], BF16)) for i in range(2)]
                qb_b = [P.buf("aqb0"), P.buf("aqb1")]
                qf = [st.enter_context(SBT(f"a_qf{i}", [128, 512], F32)) for i in range(2)]
                qf_b = [P.buf("aqf0"), P.buf("aqf1")]
                og = [st.enter_context(SBT(f"a_og{i}", [128, 512], BF16)) for i in range(4)]
                og_b = [P.buf(f"aog{i}") for i in range(4)]
                rc = st.enter_context(SBT("a_rc", [128, 512], F32))
                rs_ = st.enter_context(SBT("a_rs", [128, 512], F32))
                rc_b = P.buf("rc")
                wsrc = wb["attn_w_in"][j]
                it = 0
                oi = 0
                for b in range(NB):
                    bs = slice(b * 512, (b + 1) * 512)
                    norm_block(nb, h_src, b, gain)
                    P.dma("sp", rc[:], ropec[:, bs], rc_b, writes=[rc_b])
                    P.dma("sp", rs_[:], ropes[:, bs], rc_b, writes=[rc_b])
                    for pn in range(4 * D // MW):
                        s = it % 2
                        it += 1
                        P.dma("pool", wp[s][:], wsrc[:, pn * MW:(pn + 1) * MW].rearrange("(kt p) m -> p kt m", p=128),
                              wp_b[s], writes=[wp_b[s]])
                        if pn < 2 * D // MW:
                            for mi in range(MW // 128):
                                m = pn * (MW // 128) + mi
                                q = m % 2
                                for kt in range(KT):
                                    P.op("pe", lambda e, q=q, s=s, mi=mi, kt=kt: e.matmul(
                                        ps[q][:], lhsT=wp[s][:, kt, mi * 128:(mi + 1) * 128], rhs=hnT[:, kt, :],
                                        start=(kt == 0), stop=(kt == KT - 1)),
                                         reads=[wp_b[s], hnT_b], writes=[ps_b[q]])
                                P.op("act", lambda e, q=q: e.activation(out=qb[q][:], in_=ps[q][:], func=AF.Copy),
                                     reads=[ps_b[q]], writes=[qb_b[q]])
                                P.op("pe", lambda e, q=q: e.matmul(pr[q][:], lhsT=rotb[:], rhs=qb[q][:], start=True, stop=True),
                                     reads=[qb_b[q], b_const], writes=[pr_b[q]])
                                P.op("dve", lambda e, q=q: e.tensor_tensor(out=qf[q][:], in0=ps[q][:], in1=rc[:], op=ALU.mult),
                                     reads=[ps_b[q], rc_b], writes=[qf_b[q]])
                                o = oi % 4
                                oi += 1
                                P.op("dve", lambda e, q=q, o=o: e.tensor_tensor(out=og[o][:], in0=pr[q][:], in1=rs_[:], op=ALU.mult),
                                     reads=[pr_b[q], rc_b], writes=[og_b[o]])
                                P.op("pool", lambda e, q=q, o=o: e.tensor_tensor(out=og[o][:], in0=og[o][:], in1=qf[q][:], op=ALU.add),
                                     reads=[qf_b[q], og_b[o]], writes=[og_b[o]])
                                dst = qT[m] if m < 2 * H else kT[m - 2 * H]
                                P.dma("sp", dst[:, bs], og[o][:], og_b[o], reads=[og_b[o]])
                        else:
                            cc0 = pn * MW - 2 * D
                            for i in range(4):
                                q = i % 2
                                for kt in range(KT):
                                    P.op("pe", lambda e, q=q, s=s, i=i, kt=kt: e.matmul(
                                        ps[q][:, 0:MW], lhsT=hnT[:, kt, i * 128:(i + 1) * 128], rhs=wp[s][:, kt, :],
                                        start=(kt == 0), stop=(kt == KT - 1)),
                                         reads=[wp_b[s], hnT_b], writes=[ps_b[q]])
                                o = oi % 4
                                oi += 1
                                rs = slice(b * 512 + i * 128, b * 512 + (i + 1) * 128)
                                if cc0 < D:
                                    P.op("dve", lambda e, o=o, q=q: e.tensor_copy(out=og[o][:, 0:MW], in_=ps[q][:, 0:MW]),
                                         reads=[ps_b[q]], writes=[og_b[o]])
                                    for hh in range(MW // 256):
                                        hd = (cc0 + hh * 256) // 256
                                        P.dma("sp", vA[hd, rs, :], og[o][:, hh * 256:(hh + 1) * 256], og_b[o], reads=[og_b[o]])
                                else:
                                    P.op("act", lambda e, o=o, q=q: e.activation(out=og[o][:, 0:MW], in_=ps[q][:, 0:MW], func=AF.Silu),
                                         reads=[ps_b[q]], writes=[og_b[o]])
                                    P.dma("sp", zs[rs, cc0 - D:cc0 - D + MW], og[o][:, 0:MW], og_b[o], reads=[og_b[o]])
                P.barrier()

        def phase_at_b(j, layer_idx):
            li = lambda_init(layer_idx)
            with ExitStack() as st:
                QB = 256
                NQ = L // QB
                Kt = st.enter_context(SBT("b_Kt", [128, 2, L], BF16))
                Vt = st.enter_context(SBT("b_Vt", [128, NT, 258], BF16))
                Kt_b, Vt_b = P.buf("Kt"), P.buf("Vt")
                Qt = [st.enter_context(SBT(f"b_Qt{i}", [128, 2, QB], BF16)) for i in range(2)]
                Qt_b = [P.buf("Qt0"), P.buf("Qt1")]
                Pt = [st.enter_context(SBT(f"b_Pt{i}", [128, 2, 512], BF16)) for i in range(3)]
                Pt_b = [P.buf(f"Pt{i}") for i in range(3)]
                sps = [st.enter_context(PST(f"b_sps{i}", [128, 2, 512], F32)) for i in range(2)]
                sps_b = [P.buf("sps0"), P.buf("sps1")]
                acc = [st.enter_context(PST(f"b_acc{i}", [128, 512], F32)) for i in range(4)]
                acc_b = [P.buf(f"acc{i}") for i in range(4)]
                lam = st.enter_context(SBT("b_lam", [128, 8], F32))
                lqk = st.enter_context(SBT("b_lqk", [128, 4, 128], F32))
                subg = st.enter_context(SBT("b_subg", [128, 256], F32))
                lam_b = P.buf("lam")
                o1 = st.enter_context(SBT("b_o1", [128, 256], F32))
                o2 = st.enter_context(SBT("b_o2", [128, 256], F32))
                o_b = P.buf("o12")
                rcp = st.enter_context(SBT("b_rcp", [128, 4], F32))
                zt = [st.enter_context(SBT(f"b_zt{i}", [128, 256], BF16)) for i in range(2)]
                zt_b = [P.buf("bzt0"), P.buf("bzt1")]
                oo = [st.enter_context(SBT(f"b_oo{i}", [128, 256], BF16)) for i in range(2)]
                oo_b = [P.buf("boo0"), P.buf("boo1")]
                junk = st.enter_context(SBT("b_junk", [128, 256], F32))
                for ii, nm in enumerate(("attn_lambda_q1", "attn_lambda_k1", "attn_lambda_q2", "attn_lambda_k2")):
                    P.dma("sp", lqk[:, ii, :], w[nm][j].partition_broadcast(128), lam_b, writes=[lam_b])
                P.dma("sp", subg[:], w["attn_subln"][j].partition_broadcast(128), lam_b, writes=[lam_b])
                for ii in range(2):
                    P.op("dve", lambda e, ii=ii: e.tensor_tensor(out=lqk[:, 2 * ii, :], in0=lqk[:, 2 * ii, :], in1=lqk[:, 2 * ii + 1, :],
                                                              op=ALU.mult), reads=[lam_b], writes=[lam_b])
                    P.op("dve", lambda e, ii=ii: e.reduce_sum(out=lam[:, ii:ii + 1], in_=lqk[:, 2 * ii, :], axis=AX.X),
                         reads=[lam_b], writes=[lam_b])
                P.op("act", lambda e: e.activation(out=lam[:, 0:2], in_=lam[:, 0:2], func=AF.Exp), reads=[lam_b], writes=[lam_b])
                P.op("dve", lambda e: e.tensor_tensor(out=lam[:, 2:3], in0=lam[:, 0:1], in1=lam[:, 1:2], op=ALU.subtract),
                     reads=[lam_b], writes=[lam_b])
                P.op("dve", lambda e: e.tensor_scalar(out=lam[:, 3:4], in0=lam[:, 2:3], scalar1=li, scalar2=-1.0,
                                                      op0=ALU.add, op1=ALU.mult), reads=[lam_b], writes=[lam_b])
                P.op("dve", lambda e: e.tensor_scalar(out=subg[:], in0=subg[:], scalar1=1.0 - li, scalar2=None, op0=ALU.mult),
                     reads=[lam_b], writes=[lam_b])
                scale = 1.0 / math.sqrt(128.0)
                QB2 = 512
                NQ2 = L // QB2
                NQT = QB2 // 128
                Qs = [st.enter_context(SBT(f"b_Qs{i}", [128, QB2], BF16)) for i in range(2)]
                Qs_b = [P.buf("Qs0"), P.buf("Qs1")]
                o1s = st.enter_context(SBT("b_o1s", [128, NQT, 256], F32))
                o1s_b = P.buf("o1s")
                pi = 0
                qn = 0
                zn = 0
                for hd in range(H):
                    for s2 in range(2):
                        P.dma("sp", Kt[:, s2, :], kT[2 * hd + s2], Kt_b, writes=[Kt_b])
                    P.dma("pool", Vt[:, :, 0:256], vA[hd].rearrange("(t p) c -> p t c", p=128), Vt_b, writes=[Vt_b])
                    P.op("dve", lambda e: e.tensor_copy(out=Vt[:, :, 256:257], in_=tmask[:, :].unsqueeze(2)), reads=[b_const], writes=[Vt_b])
                    for qb in range(NQ2):
                        qs = slice(qb * QB2, (qb + 1) * QB2)
                        for s2 in range(2):
                            qq = qn % 2
                            qn += 1
                            P.dma("sp", Qs[qq][:], qT[2 * hd + s2][:, qs], Qs_b[qq], writes=[Qs_b[qq]])
                            for kp in range(NT // 2):
                                sq = kp % 2
                                for kk in range(2):
                                    kt = 2 * kp + kk
                                    P.op("pe", lambda e, sq=sq, s2=s2, kt=kt, kk=kk, qq=qq: e.matmul(
                                        sps[sq][:, kk, :], lhsT=Kt[:, s2, kt * 128:(kt + 1) * 128], rhs=Qs[qq][:],
                                        start=True, stop=True),
                                         reads=[Kt_b, Qs_b[qq]], writes=[sps_b[sq]])
                                pp = pi % 3
                                pi += 1
                                P.op("act", lambda e, pp=pp, sq=sq: e.activation(out=Pt[pp][:], in_=sps[sq][:], func=AF.Exp, scale=scale),
                                     reads=[sps_b[sq]], writes=[Pt_b[pp]])
                                for kk in range(2):
                                    kt = 2 * kp + kk
                                    for qi in range(NQT):
                                        P.op("pe", lambda e, pp=pp, qi=qi, kt=kt, kk=kk: e.matmul(
                                            acc[qi][:, 0:257], lhsT=Pt[pp][:, kk, qi * 128:(qi + 1) * 128],
                                            rhs=Vt[:, kt, 0:257], start=(kt == 0), stop=(kt == NT - 1)),
                                             reads=[Pt_b[pp], Vt_b], writes=[acc_b[qi]])
                            for qi in range(NQT):
                                rs = slice(qb * QB2 + qi * 128, qb * QB2 + (qi + 1) * 128)
                                if s2 == 0:
                                    P.op("dve", lambda e, qi=qi: e.reciprocal(out=rcp[:, 0:1], in_=acc[qi][:, 256:257]),
                                         reads=[acc_b[qi]], writes=[o_b])
                                    P.op("act", lambda e, qi=qi: e.activation(out=o1s[:, qi, :], in_=acc[qi][:, 0:256], func=AF.Copy,
                                                                           scale=rcp[:, 0:1]),
                                         reads=[acc_b[qi], o_b], writes=[o1s_b])
                                else:
                                    zq = zn % 2
                                    zn += 1
                                    P.dma("sp", zt[zq][:], zs[rs, hd * 256:(hd + 1) * 256], zt_b[zq], writes=[zt_b[zq]])
                                    P.op("dve", lambda e, qi=qi: e.reciprocal(out=rcp[:, 1:2], in_=acc[qi][:, 256:257]),
                                         reads=[acc_b[qi]], writes=[o_b])
                                    P.op("dve", lambda e: e.tensor_tensor(out=rcp[:, 1:2], in0=rcp[:, 1:2], in1=lam[:, 3:4], op=ALU.mult),
                                         reads=[o_b, lam_b], writes=[o_b])
                                    P.op("dve", lambda e, qi=qi: e.scalar_tensor_tensor(out=o2[:], in0=acc[qi][:, 0:256], scalar=rcp[:, 1:2],
                                                                                      in1=o1s[:, qi, :], op0=ALU.mult, op1=ALU.add),
                                         reads=[acc_b[qi], o_b, o1s_b], writes=[o_b])
                                    P.op("act", lambda e: e.activation(out=junk[:], in_=o2[:], func=AF.Square, accum_out=rcp[:, 2:3]),
                                         reads=[o_b], writes=[o_b])
                                    P.op("dve", lambda e: e.tensor_scalar(out=rcp[:, 2:3], in0=rcp[:, 2:3], scalar1=1.0 / 256.0, scalar2=SUBLN_EPS,
                                                                          op0=ALU.mult, op1=ALU.add), reads=[o_b], writes=[o_b])
                                    P.op("act", lambda e: e.activation(out=rcp[:, 2:3], in_=rcp[:, 2:3], func=AF.Sqrt), reads=[o_b], writes=[o_b])
                                    P.op("dve", lambda e: e.reciprocal(out=rcp[:, 2:3], in_=rcp[:, 2:3]), reads=[o_b], writes=[o_b])
                                    P.op("dve", lambda e: e.scalar_tensor_tensor(out=o1[:], in0=o2[:], scalar=rcp[:, 2:3], in1=subg[:],
                                                                                 op0=ALU.mult, op1=ALU.mult),
                                         reads=[o_b, lam_b], writes=[o_b])
                                    P.op("dve", lambda e, zq=zq: e.tensor_tensor(out=oo[zq][:], in0=o1[:], in1=zt[zq][:], op=ALU.mult),
                                         reads=[o_b, zt_b[zq]], writes=[oo_b[zq]])
                                    P.dma("act", oS[rs, hd * 256:(hd + 1) * 256], oo[zq][:], oo_b[zq], reads=[oo_b[zq]])
                P.barrier()

        def phase_at_c(j, h_src, h_dst):
            with ExitStack() as st:
                ob = st.enter_context(SBT("ac_ob", [128, 4, D], BF16))
                ob_b = P.buf("ob")
                oT = st.enter_context(SBT("ac_oT", [128, KT, 512], BF16))
                oT_b = P.buf("oT")
                tp = [st.enter_context(PST(f"ac_tp{i}", [128, 1024], BF16)) for i in range(2)]
                tp_b = [P.buf("actp0"), P.buf("actp1")]
                NW = min(256, D)
                wo = [st.enter_context(SBT(f"ac_wo{i}", [128, KT, NW], BF16)) for i in range(2)]
                wo_b = [P.buf("acwo0"), P.buf("acwo1")]
                po = [st.enter_context(PST(f"ac_po{i}", [128, 4, NW], F32)) for i in range(2)]
                po_b = [P.buf("acpo0"), P.buf("acpo1")]
                hb = [st.enter_context(SBT(f"ac_hb{i}", [128, 4, NW], F32)) for i in range(2)]
                hb_b = [P.buf("achb0"), P.buf("achb1")]
                wosrc = wb["attn_w_out"][j]
                ito = 0
                for b in range(NB):
                    bs = slice(b * 512, (b + 1) * 512)
                    P.dma("sp", ob[:], oS[bs, :].rearrange("(i p) d -> p i d", p=128), ob_b, writes=[ob_b])
                    for kt in range(KT):
                        s = kt % 2
                        for i in range(4):
                            P.op("pe", lambda e, i=i, kt=kt, s=s: e.transpose(out=tp[s][:, i * 128:(i + 1) * 128],
                                                                           in_=ob[:, i, kt * 128:(kt + 1) * 128], identity=ident[:]),
                                 reads=[ob_b, b_const], writes=[tp_b[s]])
                        P.op("dve", lambda e, kt=kt, s=s: e.tensor_copy(out=oT[:, kt, :], in_=tp[s][:, 0:512]),
                             reads=[tp_b[s]], writes=[oT_b])
                    for pn in range(D // NW):
                        s = ito % 2
                        ito += 1
                        P.dma("pool", wo[s][:], wosrc[:, pn * NW:(pn + 1) * NW].rearrange("(kt p) n -> p kt n", p=128),
                              wo_b[s], writes=[wo_b[s]])
                        for i in range(4):
                            for kt in range(KT):
                                P.op("pe", lambda e, s=s, i=i, kt=kt: e.matmul(
                                    po[s][:, i, :], lhsT=oT[:, kt, i * 128:(i + 1) * 128], rhs=wo[s][:, kt, :],
                                    start=(kt == 0), stop=(kt == KT - 1)),
                                     reads=[wo_b[s], oT_b], writes=[po_b[s]])
                        P.dma("sp", hb[s][:], h_src[bs, pn * NW:(pn + 1) * NW].rearrange("(i p) n -> p i n", p=128), hb_b[s],
                              writes=[hb_b[s]])
                        P.op("dve", lambda e, s=s: e.tensor_tensor(out=hb[s][:], in0=hb[s][:], in1=po[s][:], op=ALU.add),
                             reads=[hb_b[s], po_b[s]], writes=[hb_b[s]])
                        P.dma("act", h_dst[bs, pn * NW:(pn + 1) * NW].rearrange("(i p) n -> p i n", p=128), hb[s][:], hb_b[s],
                              reads=[hb_b[s]])
                P.barrier()

        def phase_final(h_src):
            with ExitStack() as st:
                hb = [st.enter_context(SBT(f"f_hb{i}", [128, D], F32)) for i in range(2)]
                hb_b = [P.buf("fhb0"), P.buf("fhb1")]
                ho = [st.enter_context(SBT(f"f_ho{i}", [128, D], F32)) for i in range(2)]
                ho_b = [P.buf("fho0"), P.buf("fho1")]
                junk = st.enter_context(SBT("f_junk", [128, D], BF16))
                junk_b = P.buf("fjunk")
                ss = st.enter_context(SBT("f_ss", [128, 2], F32))
                ss_b = P.buf("fss")
                gfull = st.enter_context(SBT("f_g", [128, D], F32))
                P.dma("sp", gfull[:], w["final_norm"].partition_broadcast(128), b_const, writes=[b_const])
                for t in range(NT):
                    s = t % 2
                    rs = slice(t * 128, (t + 1) * 128)
                    P.dma("sp", hb[s][:], h_src[rs, :], hb_b[s], writes=[hb_b[s]])
                    P.op("act", lambda e, s=s: e.activation(out=junk[:], in_=hb[s][:], func=AF.Square, accum_out=ss[:, s:s + 1]),
                         reads=[hb_b[s]], writes=[junk_b, ss_b])
                    P.op("dve", lambda e, s=s: e.tensor_scalar(out=ss[:, s:s + 1], in0=ss[:, s:s + 1], scalar1=1.0 / D, scalar2=NORM_EPS,
                                                            op0=ALU.mult, op1=ALU.add), reads=[ss_b], writes=[ss_b])
                    P.op("act", lambda e, s=s: e.activation(out=ss[:, s:s + 1], in_=ss[:, s:s + 1], func=AF.Sqrt), reads=[ss_b], writes=[ss_b])
                    P.op("dve", lambda e, s=s: e.reciprocal(out=ss[:, s:s + 1], in_=ss[:, s:s + 1]), reads=[ss_b], writes=[ss_b])
                    P.op("dve", lambda e, s=s: e.scalar_tensor_tensor(out=ho[s][:], in0=hb[s][:], scalar=ss[:, s:s + 1], in1=gfull[:],
                                                                   op0=ALU.mult, op1=ALU.mult),
                         reads=[hb_b[s], ss_b, b_const], writes=[ho_b[s]])
                    P.dma("act", y_out[rs, :], ho[s][:], ho_b[s], reads=[ho_b[s]])
                P.barrier()

        phase_cast()
        hbufs = [hA, hB]
        cur = x_in
        for i in range(c.depth):
            j = i // 2
            nxt = hbufs[i % 2]
            if i % 2 == 0:
                phase_s5a(j, cur)
                phase_s5b(j)
                phase_s5c(j, cur, nxt)
            else:
                stop = getattr(cfg, "stop", "")
                phase_at_a(j, cur)
                if stop != "at_a":
                    phase_at_b(j, i)
                    if stop != "at_b":
                        phase_at_c(j, cur, nxt)
            cur = nxt
        phase_final(cur)
        build_program.n_inst = P.n_inst
        for k, v in P.maxwait.items():
            assert v <= P.total.get(k, 0), ("unreachable wait", v, P.total.get(k, 0))
    return nc


def host_constants(L):
    ident = np.eye(128, dtype=np.float32)
    jj = np.zeros((128, 128), np.float32)
    for k in range(64):
        jj[k, k + 64] = 1.0
        jj[k + 64, k] = 1.0
    rot = jj.copy()
    pos = np.arange(L, dtype=np.float32)
    inv_freq = (1.0 / (np.float32(ROPE_THETA) ** (np.arange(0, 128, 2, dtype=np.float32) / np.float32(128)))).astype(np.float32)
    ang = pos[None, :] * inv_freq[:, None]
    cos = np.cos(ang).astype(np.float32)
    sin = np.sin(ang).astype(np.float32)
    ropec = np.concatenate([cos, cos], axis=0)
    ropes = np.concatenate([-sin, sin], axis=0)
    return dict(cident=ident, cjj=jj, crot=rot, ropec=np.ascontiguousarray(ropec), ropes=np.ascontiguousarray(ropes))


def run_cfg(cfg, seqs, weights, n_cores):
    L, D = cfg.L, cfg.D
    nc = build_program(cfg)
    consts = host_constants(L)
    in_maps = []
    for ci in range(n_cores):
        xs = seqs[ci % len(seqs)]
        Li = xs.shape[0]
        x = np.zeros((L, D), np.float32)
        x[:Li] = xs
        m = np.zeros((L,), np.float32)
        m[:Li] = 1.0
        kb = np.where(m > 0, 0.0, -30000.0).astype(np.float32)
        d = {"x": x,
             "tokmask": np.ascontiguousarray(m.reshape(L // 128, 128).T),
             "keybias": np.ascontiguousarray(kb.reshape(L // 128, 128).T)}
        d.update(consts)
        for k, v in weights.items():
            d[k] = np.ascontiguousarray(v, dtype=np.float32)
        in_maps.append(d)
    res = run_bass_kernel_spmd(nc, in_maps, core_ids=list(range(n_cores)))
    run_cfg.last = res
    outs = []
    for ci in range(len(seqs)):
        outs.append(np.asarray(res.results[ci]["y"])[:seqs[ci].shape[0]])
    return outs


def kernel(x_prompt, x_sample, **weights):
    x_prompt = np.asarray(x_prompt, dtype=np.float32)
    x_sample = np.asarray(x_sample, dtype=np.float32)
    D = x_prompt.shape[-1]
    L = max(x_prompt.shape[1], x_sample.shape[1])
    cfg = Cfg(L=L, D=D, H=D // 256)
    seqs = [x_sample[i] for i in range(x_sample.shape[0])] + [x_prompt[i] for i in range(x_prompt.shape[0])]
    assert len(seqs) <= 8
    outs = run_cfg(cfg, seqs, weights, 8)
    ns = x_sample.shape[0]
    y_sample = np.stack(outs[:ns], axis=0)
    y_prompt = np.stack(outs[ns:], axis=0)
    return (y_prompt.astype(np.float32), y_sample.astype(np.float32))
```
